# Optimizing a Trainium2 kernel written in Bass

```python
import math, functools
import jax, jax.numpy as jnp
from jax import lax
import numpy as np

D_MODEL = 1024
BATCH = 8
SEQ = 2048
DEPTH = 2

GRID_W = 64
CTX_LEN = 256
EPS = 1e-6
F32 = jnp.float32
N_BRANCH = 3
BRANCH_WIDTH = D_MODEL // 2
S5_WIDTH = BRANCH_WIDTH
S5_GROUP = 16
S5_GROUPS = S5_WIDTH // S5_GROUP
S5_STATE = 64
HEAD_DIM = 64
N_HEADS = BRANCH_WIDTH // HEAD_DIM
N_KV_HEADS = 2
Q_PER_KV = N_HEADS // N_KV_HEADS
WINDOW = 128
ATTN_BLOCK = 128
ROPE_BASE = 10000.0
ROPE_AXIS_DIM = HEAD_DIM // 2
ROPE_AXIS_PAIRS = ROPE_AXIS_DIM // 2
HYENA_WIDTH = BRANCH_WIDTH
HYENA_ORDER = 2
HYENA_BANDS = 16
HYENA_POS_DIM = 1 + 2 * HYENA_BANDS
HYENA_FILTER_HIDDEN = 64
SHORT_CONV = 3
FFN_DENSE = 2816
N_EXPERTS = 8
TOP_K = 2
FFN_EXPERT = 3584
N_DENSE_LAYERS = (DEPTH + 1) // 2
N_MOE_LAYERS = DEPTH // 2
Q_WIDTH = N_HEADS * HEAD_DIM
KV_WIDTH = N_KV_HEADS * HEAD_DIM
HY_IN_WIDTH = (HYENA_ORDER + 1) * HYENA_WIDTH
GATE_WIDTH = N_BRANCH * D_MODEL
IN_SPLITS = (S5_WIDTH, S5_WIDTH + Q_WIDTH, S5_WIDTH + Q_WIDTH + KV_WIDTH, S5_WIDTH + Q_WIDTH + 2 * KV_WIDTH, S5_WIDTH + Q_WIDTH + 2 * KV_WIDTH + HY_IN_WIDTH)
IN_WIDTH = IN_SPLITS[-1] + GATE_WIDTH

kernel_name = 'hybrid_s5_swa_hyena_moe_dit'


def _rms_norm(x, g):
    x32 = x.astype(F32)
    return (x32 * lax.rsqrt(jnp.mean(x32 * x32, axis=-1, keepdims=True) + EPS)).astype(x.dtype) * g


def _modulation(cvec, w, b):
    m = (jax.nn.silu(cvec) @ w + b)[:, None, :]
    return jnp.split(m, 6, axis=-1)


def _adaln(x, g, shift, scale):
    return _rms_norm(x, g) * (1.0 + scale) + shift


def _heads(t, n):
    return t.reshape(t.shape[:-1] + (n, HEAD_DIM))


def _axial_rope(t):
    L = t.shape[1]
    rows = L // GRID_W
    row = jnp.repeat(jnp.arange(rows, dtype=F32), GRID_W)
    col = jnp.tile(jnp.arange(GRID_W, dtype=F32), rows)
    inv = ROPE_BASE ** (-2.0 * jnp.arange(ROPE_AXIS_PAIRS, dtype=F32) / ROPE_AXIS_DIM)
    ang = jnp.concatenate([row[:, None] * inv, col[:, None] * inv], axis=-1)
    cos = jnp.cos(ang)[None, :, None, :]
    sin = jnp.sin(ang)[None, :, None, :]
    tp = t.astype(F32).reshape(t.shape[:-1] + (HEAD_DIM // 2, 2))
    t0, t1 = tp[..., 0], tp[..., 1]
    out = jnp.stack([t0 * cos - t1 * sin, t0 * sin + t1 * cos], axis=-1)
    return out.reshape(t.shape).astype(t.dtype)


def _s5_discretize(lam_re, lam_im, log_dt, b_re, b_im):
    lam_re, lam_im, log_dt, b_re, b_im = (a.astype(F32) for a in (lam_re, lam_im, log_dt, b_re, b_im))
    dt = jnp.exp(log_dt)[:, None]
    mag = jnp.exp(lam_re * dt)
    ab_re = mag * jnp.cos(lam_im * dt)
    ab_im = mag * jnp.sin(lam_im * dt)
    den = lam_re * lam_re + lam_im * lam_im
    f_re = ((ab_re - 1.0) * lam_re + ab_im * lam_im) / den
    f_im = (ab_im * lam_re - (ab_re - 1.0) * lam_im) / den
    bb_re = f_re[..., None] * b_re - f_im[..., None] * b_im
    bb_im = f_re[..., None] * b_im + f_im[..., None] * b_re
    return ab_re, ab_im, bb_re, bb_im


def _complex_affine_combine(e1, e2):
    a1r, a1i, b1r, b1i = e1
    a2r, a2i, b2r, b2i = e2
    return (a2r * a1r - a2i * a1i, a2r * a1i + a2i * a1r,
            a2r * b1r - a2i * b1i + b2r, a2r * b1i + a2i * b1r + b2i)


def _s5_scan(u, ab_re, ab_im, bb_re, bb_im, h0, reverse):
    bu_re = jnp.einsum('blgc,gpc->blgp', u, bb_re)
    bu_im = jnp.einsum('blgc,gpc->blgp', u, bb_im)
    if h0 is not None:
        h0_re, h0_im = h0
        edge = -1 if reverse else 0
        bu_re = bu_re.at[:, edge].add(ab_re * h0_re - ab_im * h0_im)
        bu_im = bu_im.at[:, edge].add(ab_re * h0_im + ab_im * h0_re)
    L = u.shape[1]
    a_re = jnp.broadcast_to(ab_re, (1, L) + ab_re.shape)
    a_im = jnp.broadcast_to(ab_im, (1, L) + ab_im.shape)
    _, _, h_re, h_im = lax.associative_scan(_complex_affine_combine, (a_re, a_im, bu_re, bu_im), reverse=reverse, axis=1)
    return h_re, h_im


def _s5_readout(h_re, h_im, c_re, c_im):
    return (jnp.einsum('gcp,blgp->blgc', c_re.astype(F32), h_re)
            - jnp.einsum('gcp,blgp->blgc', c_im.astype(F32), h_im))


def _s5_output(y, u, d, glu_w, glu_b, dtype):
    B, L = u.shape[:2]
    y = y.reshape(B, L, S5_WIDTH) + d.astype(F32) * u.reshape(B, L, S5_WIDTH)
    y = jax.nn.gelu(y)
    y = y * jax.nn.sigmoid(y @ glu_w.astype(F32) + glu_b.astype(F32))
    return y.astype(dtype)


def _s5_branch(u_l, u_c, lam_re, lam_im, log_dt, b_re, b_im, c_re, c_im, d, glu_w, glu_b, need_ctx):
    B, L = u_l.shape[:2]
    Lc = u_c.shape[1]
    ul = u_l.astype(F32).reshape(B, L, S5_GROUPS, S5_GROUP)
    uc = u_c.astype(F32).reshape(B, Lc, S5_GROUPS, S5_GROUP)
    y_l = 0.0
    y_c = 0.0
    for direction, reverse in enumerate((False, True)):
        disc = _s5_discretize(lam_re[direction], lam_im[direction], log_dt[direction], b_re[direction], b_im[direction])
        hc_re, hc_im = _s5_scan(uc, *disc, None, reverse)
        edge = 0 if reverse else -1
        hl_re, hl_im = _s5_scan(ul, *disc, (hc_re[:, edge], hc_im[:, edge]), reverse)
        y_l = y_l + _s5_readout(hl_re, hl_im, c_re[direction], c_im[direction])
        if need_ctx:
            y_c = y_c + _s5_readout(hc_re, hc_im, c_re[direction], c_im[direction])
    out_l = _s5_output(y_l, ul, d, glu_w, glu_b, u_l.dtype)
    out_c = _s5_output(y_c, uc, d, glu_w, glu_b, u_c.dtype) if need_ctx else None
    return out_l, out_c


def _band_blocks(t, nb):
    B = t.shape[0]
    tp = jnp.pad(t, ((0, 0), (ATTN_BLOCK, ATTN_BLOCK), (0, 0), (0, 0)))
    tp = tp.reshape((B, nb + 2, ATTN_BLOCK) + t.shape[2:])
    return jnp.concatenate([tp[:, :-2], tp[:, 1:-1], tp[:, 2:]], axis=2)


def _latent_window_attention(q, k, v, kc, vc, sink):
    B, L = q.shape[:2]
    Lc = kc.shape[1]
    nb = L // ATTN_BLOCK
    nk = 3 * ATTN_BLOCK
    scale = HEAD_DIM ** -0.5
    qb = q.reshape(B, nb, ATTN_BLOCK, N_KV_HEADS, Q_PER_KV, HEAD_DIM)
    kb = _band_blocks(k, nb)
    vb = _band_blocks(v, nb)
    s_band = jnp.einsum('bnqhgd,bnkhd->bnhgqk', qb, kb).astype(F32) * scale
    qpos = jnp.arange(L).reshape(nb, ATTN_BLOCK)
    kpos = (jnp.arange(nb)[:, None] - 1) * ATTN_BLOCK + jnp.arange(nk)[None, :]
    valid = ((jnp.abs(qpos[:, :, None] - kpos[:, None, :]) <= WINDOW)
             & (kpos[:, None, :] >= 0) & (kpos[:, None, :] < L))
    s_band = jnp.where(valid[None, :, None, None], s_band, -jnp.inf)
    s_ctx = jnp.einsum('bnqhgd,bchd->bnhgqc', qb, kc).astype(F32) * scale
    s_sink = jnp.broadcast_to(sink.astype(F32).reshape(1, 1, N_KV_HEADS, Q_PER_KV, 1, 1), s_band.shape[:-1] + (1,))
    p = jax.nn.softmax(jnp.concatenate([s_band, s_ctx, s_sink], axis=-1), axis=-1).astype(v.dtype)
    o = (jnp.einsum('bnhgqk,bnkhd->bnqhgd', p[..., :nk], vb)
         + jnp.einsum('bnhgqc,bchd->bnqhgd', p[..., nk:nk + Lc], vc))
    return o.reshape(B, L, Q_WIDTH)


def _context_attention(qc, kc, vc, sink):
    B, Lc = qc.shape[:2]
    qg = qc.reshape(B, Lc, N_KV_HEADS, Q_PER_KV, HEAD_DIM)
    s = jnp.einsum('bqhgd,bchd->bhgqc', qg, kc).astype(F32) * HEAD_DIM ** -0.5
    s_sink = jnp.broadcast_to(sink.astype(F32).reshape(1, N_KV_HEADS, Q_PER_KV, 1, 1), s.shape[:-1] + (1,))
    p = jax.nn.softmax(jnp.concatenate([s, s_sink], axis=-1), axis=-1)[..., :Lc].astype(vc.dtype)
    return jnp.einsum('bhgqc,bchd->bqhgd', p, vc).reshape(B, Lc, Q_WIDTH)


def _short_conv(u, w, b):
    L = u.shape[1]
    pad = SHORT_CONV // 2
    up = jnp.pad(u, ((0, 0), (pad, pad), (0, 0)))
    out = b
    for j in range(SHORT_CONV):
        out = out + up[:, j:j + L] * w[j]
    return out


def _hyena_filters(L, w1, b1, w2, b2, w3, b3, freq, decay):
    t = jnp.arange(L, dtype=F32) / L
    bands = jnp.arange(1, HYENA_BANDS + 1, dtype=F32)
    arg = 2.0 * math.pi * t[:, None] * bands[None, :]
    feats = jnp.concatenate([t[:, None], jnp.sin(arg), jnp.cos(arg)], axis=-1)
    h = jnp.sin(freq * (feats @ w1 + b1))
    h = jnp.sin(freq * (h @ w2 + b2))
    h = (h @ w3 + b3) * jnp.exp(-t[:, None] * decay)
    h = h.astype(F32).reshape(L, HYENA_ORDER, 2, HYENA_WIDTH)
    h = h * lax.rsqrt(jnp.sum(h * h, axis=(0, 2), keepdims=True) + EPS)
    fwd, bwd = h[:, :, 0], h[:, :, 1]
    k = jnp.concatenate([fwd, jnp.zeros((1, HYENA_ORDER, HYENA_WIDTH), F32), bwd[:0:-1]], axis=0)
    return jnp.fft.rfft(k, axis=0)


def _hyena(z, conv_w, conv_b, filt_fft, bias):
    z = _short_conv(z, conv_w, conv_b)
    v, *gates = jnp.split(z, HYENA_ORDER + 1, axis=-1)
    L = z.shape[1]
    y = v.astype(F32)
    for o, gate in enumerate(gates):
        conv = jnp.fft.irfft(jnp.fft.rfft(y, n=2 * L, axis=1) * filt_fft[None, :, o], n=2 * L, axis=1)[:, :L]
        y = gate.astype(F32) * (conv + bias[o].astype(F32) * y)
    return y.astype(z.dtype)


def _merge_branches(ya, yb, yc, gates, w_branch, w_out):
    ys = jnp.stack([ya, yb, yc], axis=2)
    p = jnp.einsum('blnc,ncd->blnd', ys, w_branch)
    g = jax.nn.sigmoid(gates.reshape(gates.shape[:-1] + (N_BRANCH, D_MODEL)))
    return jnp.sum(g * p, axis=2) @ w_out


def _swiglu(h, w1, w3, w2):
    return (jax.nn.silu(h @ w1) * (h @ w3)) @ w2


def _moe_swiglu(h, router_w, w1, w3, w2):
    shape = h.shape
    t = h.reshape(-1, D_MODEL)
    probs = jax.nn.softmax((t @ router_w).astype(F32), axis=-1)
    top_p, top_i = lax.top_k(probs, TOP_K)
    top_p = top_p / jnp.sum(top_p, axis=-1, keepdims=True)
    combine = jnp.sum(jax.nn.one_hot(top_i, N_EXPERTS, dtype=F32) * top_p[..., None], axis=1)
    out = jnp.zeros_like(t)
    for e in range(N_EXPERTS):
        out = out + combine[:, e:e + 1].astype(t.dtype) * _swiglu(t, w1[e], w3[e], w2[e])
    return out.reshape(shape)


def setup_inputs(seed: int = 0) -> dict:
    key = jax.random.key(seed)
    keys = jax.random.split(key, 64)
    counter = iter(range(64))

    def nrm(shape, scale):
        return scale * jax.random.normal(keys[next(counter)], shape, F32)

    def uni(shape, lo, hi):
        return jax.random.uniform(keys[next(counter)], shape, F32, lo, hi)

    G, P, CG, HF = S5_GROUPS, S5_STATE, S5_GROUP, HYENA_FILTER_HIDDEN
    state_idx = jnp.arange(P, dtype=F32)
    filt_out = HYENA_ORDER * 2 * HYENA_WIDTH
    return {
        'x': nrm((BATCH, SEQ, D_MODEL), 1.0),
        'c': nrm((BATCH, D_MODEL), 1.0),
        'ctx': nrm((BATCH, CTX_LEN, D_MODEL), 1.0),
        'c_ctx': nrm((D_MODEL,), 1.0),
        'mod_w': nrm((DEPTH, D_MODEL, 6 * D_MODEL), 0.5 * D_MODEL ** -0.5),
        'mod_b': nrm((DEPTH, 6 * D_MODEL), 0.02),
        'norm1_g': 1.0 + nrm((DEPTH, D_MODEL), 0.02),
        'norm2_g': 1.0 + nrm((DEPTH, D_MODEL), 0.02),
        'w_in': nrm((DEPTH, D_MODEL, IN_WIDTH), D_MODEL ** -0.5),
        's5_lam_re': -0.5 + nrm((DEPTH, 2, G, P), 0.01),
        's5_lam_im': math.pi * state_idx + nrm((DEPTH, 2, G, P), 0.01),
        's5_log_dt': uni((DEPTH, 2, G), math.log(1e-3), math.log(1e-1)),
        's5_b_re': nrm((DEPTH, 2, G, P, CG), (2 * CG) ** -0.5),
        's5_b_im': nrm((DEPTH, 2, G, P, CG), (2 * CG) ** -0.5),
        's5_c_re': nrm((DEPTH, 2, G, CG, P), P ** -0.5),
        's5_c_im': nrm((DEPTH, 2, G, CG, P), P ** -0.5),
        's5_d': nrm((DEPTH, S5_WIDTH), 1.0),
        's5_glu_w': nrm((DEPTH, S5_WIDTH, S5_WIDTH), S5_WIDTH ** -0.5),
        's5_glu_b': nrm((DEPTH, S5_WIDTH), 0.02),
        'q_norm_g': 1.0 + nrm((DEPTH, HEAD_DIM), 0.02),
        'k_norm_g': 1.0 + nrm((DEPTH, HEAD_DIM), 0.02),
        'attn_sink': nrm((DEPTH, N_HEADS), 0.5),
        'hy_conv_w': nrm((DEPTH, SHORT_CONV, HY_IN_WIDTH), SHORT_CONV ** -0.5),
        'hy_conv_b': nrm((DEPTH, HY_IN_WIDTH), 0.02),
        'hy_filt_w1': nrm((DEPTH, HYENA_POS_DIM, HF), HYENA_POS_DIM ** -0.5),
        'hy_filt_b1': nrm((DEPTH, HF), 0.1),
        'hy_filt_w2': nrm((DEPTH, HF, HF), HF ** -0.5),
        'hy_filt_b2': nrm((DEPTH, HF), 0.1),
        'hy_filt_w3': nrm((DEPTH, HF, filt_out), HF ** -0.5),
        'hy_filt_b3': nrm((DEPTH, filt_out), 0.02),
        'hy_filt_freq': 1.0 + nrm((DEPTH, HF), 0.1),
        'hy_filt_decay': uni((DEPTH, filt_out), 3.0, 15.0),
        'hy_bias': nrm((DEPTH, HYENA_ORDER, HYENA_WIDTH), 0.5),
        'w_branch': nrm((DEPTH, N_BRANCH, BRANCH_WIDTH, D_MODEL), BRANCH_WIDTH ** -0.5),
        'w_out': nrm((DEPTH, D_MODEL, D_MODEL), D_MODEL ** -0.5),
        'ffn_w1': nrm((N_DENSE_LAYERS, D_MODEL, FFN_DENSE), D_MODEL ** -0.5),
        'ffn_w3': nrm((N_DENSE_LAYERS, D_MODEL, FFN_DENSE), D_MODEL ** -0.5),
        'ffn_w2': nrm((N_DENSE_LAYERS, FFN_DENSE, D_MODEL), FFN_DENSE ** -0.5),
        'router_w': nrm((N_MOE_LAYERS, D_MODEL, N_EXPERTS), D_MODEL ** -0.5),
        'moe_w1': nrm((N_MOE_LAYERS, N_EXPERTS, D_MODEL, FFN_EXPERT), D_MODEL ** -0.5),
        'moe_w3': nrm((N_MOE_LAYERS, N_EXPERTS, D_MODEL, FFN_EXPERT), D_MODEL ** -0.5),
        'moe_w2': nrm((N_MOE_LAYERS, N_EXPERTS, FFN_EXPERT, D_MODEL), FFN_EXPERT ** -0.5),
    }


def reference(x, c, ctx, c_ctx, mod_w, mod_b, norm1_g, norm2_g, w_in,
              s5_lam_re, s5_lam_im, s5_log_dt, s5_b_re, s5_b_im, s5_c_re, s5_c_im,
              s5_d, s5_glu_w, s5_glu_b, q_norm_g, k_norm_g, attn_sink,
              hy_conv_w, hy_conv_b, hy_filt_w1, hy_filt_b1, hy_filt_w2, hy_filt_b2,
              hy_filt_w3, hy_filt_b3, hy_filt_freq, hy_filt_decay, hy_bias,
              w_branch, w_out, ffn_w1, ffn_w3, ffn_w2, router_w, moe_w1, moe_w3, moe_w2):
    xl, xc = x, ctx
    L, Lc = x.shape[1], ctx.shape[1]
    for i in range(DEPTH):
        need_ctx = i < DEPTH - 1
        sh1_l, sc1_l, g1_l, sh2_l, sc2_l, g2_l = _modulation(c, mod_w[i], mod_b[i])
        sh1_c, sc1_c, g1_c, sh2_c, sc2_c, g2_c = _modulation(c_ctx[None], mod_w[i], mod_b[i])

        zl = _adaln(xl, norm1_g[i], sh1_l, sc1_l) @ w_in[i]
        zc = _adaln(xc, norm1_g[i], sh1_c, sc1_c) @ w_in[i]
        ua_l, q_l, k_l, v_l, uh_l, gt_l = jnp.split(zl, IN_SPLITS, axis=-1)
        ua_c, q_c, k_c, v_c, uh_c, gt_c = jnp.split(zc, IN_SPLITS, axis=-1)

        ya_l, ya_c = _s5_branch(ua_l, ua_c, s5_lam_re[i], s5_lam_im[i], s5_log_dt[i], s5_b_re[i], s5_b_im[i],
                                s5_c_re[i], s5_c_im[i], s5_d[i], s5_glu_w[i], s5_glu_b[i], need_ctx)

        k_ctx = _rms_norm(_heads(k_c, N_KV_HEADS), k_norm_g[i])
        v_ctx = _heads(v_c, N_KV_HEADS)
        q_lat = _axial_rope(_rms_norm(_heads(q_l, N_HEADS), q_norm_g[i]))
        k_lat = _axial_rope(_rms_norm(_heads(k_l, N_KV_HEADS), k_norm_g[i]))
        yb_l = _latent_window_attention(q_lat, k_lat, _heads(v_l, N_KV_HEADS), k_ctx, v_ctx, attn_sink[i])

        hy = (hy_filt_w1[i], hy_filt_b1[i], hy_filt_w2[i], hy_filt_b2[i], hy_filt_w3[i], hy_filt_b3[i],
              hy_filt_freq[i], hy_filt_decay[i])
        yc_l = _hyena(uh_l, hy_conv_w[i], hy_conv_b[i], _hyena_filters(L, *hy), hy_bias[i])

        xl = xl + g1_l * _merge_branches(ya_l, yb_l, yc_l, gt_l, w_branch[i], w_out[i])
        if need_ctx:
            yb_c = _context_attention(_rms_norm(_heads(q_c, N_HEADS), q_norm_g[i]), k_ctx, v_ctx, attn_sink[i])
            yc_c = _hyena(uh_c, hy_conv_w[i], hy_conv_b[i], _hyena_filters(Lc, *hy), hy_bias[i])
            xc = xc + g1_c * _merge_branches(ya_c, yb_c, yc_c, gt_c, w_branch[i], w_out[i])

        j = i // 2
        if i % 2 == 0:
            ffn = functools.partial(_swiglu, w1=ffn_w1[j], w3=ffn_w3[j], w2=ffn_w2[j])
        else:
            ffn = functools.partial(_moe_swiglu, router_w=router_w[j], w1=moe_w1[j], w3=moe_w3[j], w2=moe_w2[j])
        xl = xl + g2_l * ffn(_adaln(xl, norm2_g[i], sh2_l, sc2_l))
        if need_ctx:
            xc = xc + g2_c * ffn(_adaln(xc, norm2_g[i], sh2_c, sc2_c))
    return xl
```

```python
import math
import contextlib
import numpy as np
import ml_dtypes
import concourse.bass as bass
import concourse.mybir as mybir
from concourse.bass_utils import run_bass_kernel_spmd

F32 = mybir.dt.float32
BF16 = mybir.dt.bfloat16
I32 = mybir.dt.int32
AF = mybir.ActivationFunctionType
ALU = mybir.AluOpType
AX = mybir.AxisListType

D = 1024
NL = 2048
NCX = 256
NT = NL + NCX
TT = [(0, 512, 0), (512, 512, 0), (1024, 512, 0), (1536, 512, 0), (2048, 256, 1)]
IN_W = 5888
TWO_PI = 2.0 * math.pi
CH = 256


class Prog:
    ENG = ('pe', 'act', 'dve', 'pool', 'sp')
    DMAENG = ('pool', 'sp')
    NSLOT = 20
    ENGOBJ = {'pe': 'tensor', 'act': 'scalar', 'dve': 'vector', 'pool': 'gpsimd', 'sp': 'sync'}

    def __init__(self, nc, stack):
        self.nc = nc
        self.csem = {e: stack.enter_context(nc.semaphore('c_' + e)) for e in self.ENG}
        self.dsem = {e: [stack.enter_context(nc.semaphore(f'd_{e}{i}')) for i in range(self.NSLOT)] for e in self.DMAENG}
        self.ctot = {e: 0 for e in self.ENG}
        self.dtot = {e: [0] * self.NSLOT for e in self.DMAENG}
        self.dcount = {e: 0 for e in self.DMAENG}
        self.nflush = 0
        self.count = 0
        self._reset()

    def _reset(self):
        self.ins = {e: [] for e in self.ENG}
        self.lastw = {}
        self.readers = {}

    def op(self, eng, fn, reads=(), writes=(), dma=False):
        deps = set()
        for k in reads:
            w = self.lastw.get(k)
            if w is not None:
                deps.add(w)
        for k in writes:
            w = self.lastw.get(k)
            if w is not None:
                deps.add(w)
            for r in self.readers.get(k, ()):
                deps.add(r)
        idx = len(self.ins[eng])
        me = (eng, idx)
        if eng == 'pe':
            deps = {d for d in deps if d[0] != 'pe'}
        deps.discard(me)
        it = dict(fn=fn, deps=deps, dma=dma, sig=dma)
        if dma:
            assert eng in self.DMAENG
            slot = self.dcount[eng] % self.NSLOT
            self.dcount[eng] += 1
            it['slot'] = slot
            it['prev'] = self.dtot[eng][slot]
            self.dtot[eng][slot] += 16
            it['val'] = self.dtot[eng][slot]
        self.ins[eng].append(it)
        for k in reads:
            self.readers.setdefault(k, []).append(me)
        for k in writes:
            self.lastw[k] = me
            self.readers[k] = []
        self.count += 1
        return me

    def flush(self):
        nc = self.nc
        ins = self.ins
        for e in self.ENG:
            for it in ins[e]:
                for (e2, i2) in it['deps']:
                    ins[e2][i2]['sig'] = True
            for it in reversed(ins[e]):
                if not it['dma']:
                    it['sig'] = True
                    break
        start_c = dict(self.ctot)
        start_d = getattr(self, '_dprev', {e: [0] * self.NSLOT for e in self.DMAENG})
        for e in self.ENG:
            cc = self.ctot[e]
            for it in ins[e]:
                if it['sig'] and not it['dma']:
                    cc += 1
                    it['val'] = cc
            self.ctot[e] = cc
        self._dprev = {e: list(self.dtot[e]) for e in self.DMAENG}
        csem, dsem = self.csem, self.dsem
        first = self.nflush == 0
        self.nflush += 1
        with nc.Block() as block:
            def emit(e):
                def body(eng):
                    waited = {}
                    if not first:
                        for e2 in self.ENG:
                            if start_c[e2] > 0:
                                eng.wait_ge(csem[e2], start_c[e2])
                        for e2 in self.DMAENG:
                            for s in range(self.NSLOT):
                                if start_d[e2][s] > 0:
                                    eng.wait_ge(dsem[e2][s], start_d[e2][s])
                                    waited[(e2, s)] = start_d[e2][s]
                    for it in ins[e]:
                        need = {}
                        for (e2, i2) in it['deps']:
                            src = ins[e2][i2]
                            key = (e2, src['slot']) if src['dma'] else (e2, None)
                            need[key] = max(need.get(key, 0), src['val'])
                        if it['dma'] and it['prev'] > 0:
                            key = (e, it['slot'])
                            need[key] = max(need.get(key, 0), it['prev'])
                        for key, v in need.items():
                            if waited.get(key, 0) < v:
                                sem = csem[key[0]] if key[1] is None else dsem[key[0]][key[1]]
                                eng.wait_ge(sem, v)
                                waited[key] = v
                        r = it['fn'](eng)
                        if it['dma']:
                            r.then_inc(dsem[e][it['slot']], 16)
                        elif it['sig']:
                            r.then_inc(csem[e], 1)
                return body
            for e in self.ENG:
                getattr(block, self.ENGOBJ[e])(emit(e))
        self._reset()

    def finish(self):
        nc = self.nc
        with nc.Block() as block:
            def body(eng):
                for e2 in self.ENG:
                    if self.ctot[e2] > 0:
                        eng.wait_ge(self.csem[e2], self.ctot[e2])
                for e2 in self.DMAENG:
                    for s in range(self.NSLOT):
                        if self.dtot[e2][s] > 0:
                            eng.wait_ge(self.dsem[e2][s], self.dtot[e2][s])
            block.sync(body)


def _bf(a):
    return np.ascontiguousarray(a.astype(ml_dtypes.bfloat16))


_CONST_CACHE = {}


def host_constants():
    if _CONST_CACHE:
        return _CONST_CACHE
    c = {}
    c['k_ident32'] = np.eye(128, dtype=np.float32)
    c['k_identbf'] = _bf(np.eye(128, dtype=np.float32))
    pos = np.arange(NL)
    row = (pos // 64).astype(np.float64)
    col = (pos % 64).astype(np.float64)
    inv = 10000.0 ** (-2.0 * np.arange(16, dtype=np.float64) / 32.0)
    ang = np.concatenate([row[None, :] * inv[:, None], col[None, :] * inv[:, None]], axis=0)
    ang = ang.astype(np.float32).astype(np.float64)
    cos2 = np.concatenate([np.cos(ang), np.cos(ang)], axis=0)
    sin2 = np.concatenate([-np.sin(ang), np.sin(ang)], axis=0)
    c['k_cos2'] = cos2.astype(np.float32)
    c['k_sin2'] = sin2.astype(np.float32)
    sperm = np.zeros((64, 64), np.float32)
    for m in range(64):
        sperm[(m + 32) % 64, m] = 1.0
    c['k_sperm'] = _bf(sperm)
    j = np.arange(128)[:, None]
    i = np.arange(128)[None, :]
    mprev = (j >= i).astype(np.float32)
    mnext = (j <= i).astype(np.float32)
    c['k_mprev'] = _bf(np.tile(mprev, (1, 4)))
    c['k_mnext'] = _bf(np.tile(mnext, (1, 4)))
    t = np.arange(2048, dtype=np.float64)
    ph = 2.0 * np.pi * np.outer(t, t) / 4096.0
    A = np.cos(ph)
    B = np.sin(ph)
    c['k_Ar'] = _bf(A)
    c['k_Br'] = _bf(B)
    c['k_At'] = _bf(A.reshape(16, 128, 16, 128).transpose(2, 1, 0, 3))
    c['k_Bt'] = _bf(B.reshape(16, 128, 16, 128).transpose(2, 1, 0, 3))
    tc_ = np.arange(256, dtype=np.float64)
    phc = 2.0 * np.pi * np.outer(tc_, tc_) / 512.0
    c['k_Ac'] = _bf(np.cos(phc))
    c['k_Bc'] = _bf(np.sin(phc))
    alt = (1.0 - 2.0 * (np.arange(2048) % 2)).astype(np.float32)
    c['k_altrow'] = _bf(alt[None, :])
    c['k_altcol'] = _bf(alt[:128, None])
    fs = np.full((128, 16), 2.0 / 4096.0, np.float32)
    fs[0, 0] = 1.0 / 4096.0
    c['k_fscale'] = fs
    fsc = np.full((128, 2), 2.0 / 512.0, np.float32)
    fsc[0, 0] = 1.0 / 512.0
    c['k_fscale_c'] = fsc

    def feats(L):
        tt = np.arange(L, dtype=np.float32) / np.float32(L)
        bands = np.arange(1, 17, dtype=np.float32)
        arg = (2.0 * math.pi) * tt[:, None] * bands[None, :]
        f = np.concatenate([tt[:, None], np.sin(arg), np.cos(arg)], axis=-1).astype(np.float32)
        return np.ascontiguousarray(f.T), tt
    fl, tl = feats(2048)
    fc, tcx = feats(256)
    c['k_featsT'] = fl
    c['k_featsT_c'] = fc
    c['k_negt'] = np.ascontiguousarray((-tl).reshape(16, 128).T)
    c['k_negt_c'] = np.ascontiguousarray((-tcx).reshape(2, 128).T)
    p = np.arange(128)
    c['k_m01'] = np.stack([((p // 16) % 2 == 0), ((p // 16) % 2 == 1)], axis=1).astype(np.float32)
    c['k_tau'] = np.tile(np.arange(CH, dtype=np.float32)[None, :], (128, 1))
    _CONST_CACHE.update(c)
    return c


CONST_SPECS = None


class KB:
    def __init__(self, nc, debug=(), nlayers=2, stop_after=None):
        self.nc = nc
        self.debug = set(debug)
        self.nlayers = nlayers
        self.stop_after = stop_after
        self.gst = contextlib.ExitStack()
        self.P = Prog(nc, self.gst)
        self.uid = 0

    def halt_at(self, tag):
        if self.stop_after == tag:
            self.halted = True
        return getattr(self, 'halted', False)

    def dump(self, name, ap, shape, dt, keys):
        if name not in self.debug:
            return
        t = self.nc.dram_tensor(name, list(shape), dt, kind="ExternalOutput").ap()
        self.dma('sp', t, ap, keys, [name])

    def din(self, name, shape, dt=F32):
        return self.nc.dram_tensor(name, list(shape), dt, kind="ExternalInput").ap()

    def dscr(self, name, shape, dt):
        kind = "ExternalOutput" if name in self.debug else "Internal"
        return self.nc.dram_tensor(name, list(shape), dt, kind=kind).ap()

    def sb(self, st, name, shape, dt):
        self.uid += 1
        return st.enter_context(self.nc.sbuf_tensor(f"{name}_{self.uid}", list(shape), dt))

    def mm(self, out, lhsT, rhs, start, stop, r, w, tp=None):
        kw = {}
        if tp is not None:
            kw['tile_position'] = tp
            kw['skip_group_check'] = True
        self.P.op('pe', lambda e: e.matmul(out, lhsT=lhsT, rhs=rhs, start=start, stop=stop, **kw), r, w)

    def tr(self, out, in_, ident, r, w):
        self.P.op('pe', lambda e: e.transpose(out, in_, ident), r, w)

    def act(self, out, in_, func, r, w, scale=None, bias=None):
        kw = {}
        if scale is not None:
            kw['scale'] = scale
        if bias is not None:
            kw['bias'] = bias
        self.P.op('act', lambda e: e.activation(out=out, in_=in_, func=func, **kw), r, w)

    def tt(self, eng, out, in0, in1, op, r, w):
        self.P.op(eng, lambda e: e.tensor_tensor(out=out, in0=in0, in1=in1, op=op), r, w)

    def ts(self, eng, out, in0, s1, op0, r, w, s2=None, op1=None):
        if op1 is None:
            self.P.op(eng, lambda e: e.tensor_scalar(out=out, in0=in0, scalar1=s1, scalar2=None, op0=op0), r, w)
        else:
            self.P.op(eng, lambda e: e.tensor_scalar(out=out, in0=in0, scalar1=s1, scalar2=s2, op0=op0, op1=op1), r, w)

    def stt(self, out, in0, scalar, in1, op0, op1, r, w):
        self.P.op('dve', lambda e: e.scalar_tensor_tensor(out=out, in0=in0, scalar=scalar, in1=in1, op0=op0, op1=op1), r, w)

    def cp(self, eng, out, in_, r, w):
        if eng == 'act':
            self.P.op('act', lambda e: e.activation(out=out, in_=in_, func=AF.Copy), r, w)
        else:
            self.P.op(eng, lambda e: e.tensor_copy(out=out, in_=in_), r, w)

    def recip(self, out, in_, r, w):
        self.P.op('dve', lambda e: e.reciprocal(out=out, in_=in_), r, w)

    def memset(self, eng, ap, val, w):
        self.P.op(eng, lambda e: e.memset(ap, val), (), w)

    def dma(self, eng, out, in_, r, w, slow=False):
        if slow:
            self.P.op(eng, lambda e: e.dma_start(out=out, in_=in_, allow_slow_non_contiguous=True), r, w, dma=True)
        else:
            self.P.op(eng, lambda e: e.dma_start(out=out, in_=in_), r, w, dma=True)

    def load_xnT(self, xnT, ntok):
        for it_, (t0_, sz_, _c) in enumerate(TT):
            if t0_ >= ntok:
                break
            self.dma('sp', xnT[:, :, t0_:t0_ + sz_], self.xnd[:, :, t0_:t0_ + sz_], [], [f'xnT_{it_}'])

    def scan(self, out, d0, d1, r, w):
        self.P.op('dve', lambda e: e.tensor_tensor_scan(out=out, data0=d0, data1=d1, initial=0.0, op0=ALU.mult, op1=ALU.add), r, w)

    def sin_rr(self, out, x, add, ti, tf, key, r, w, eng='dve'):
        P_ = self.P
        self.ts(eng, ti, x, float(add), ALU.add, r, [key + 'i'], s2=1.0 / TWO_PI, op1=ALU.mult)
        self.cp(eng, tf, ti, [key + 'i'], [key + 'f'])
        if eng == 'dve':
            self.stt(tf, tf, -TWO_PI, x, ALU.mult, ALU.add, list(r) + [key + 'f'], [key + 'f'])
        else:
            self.ts(eng, tf, tf, -TWO_PI, ALU.mult, [key + 'f'], [key + 'f'])
            self.tt(eng, tf, tf, x, ALU.add, list(r) + [key + 'f'], [key + 'f'])
        if add != 0.0:
            self.ts(eng, tf, tf, float(add), ALU.add, [key + 'f'], [key + 'f'])
        self.ts(eng, tf, tf, -3.1415925, ALU.max, [key + 'f'], [key + 'f'], s2=3.1415925, op1=ALU.min)
        self.act(out, tf, AF.Sin, [key + 'f'], w)

    def declare(self):
        nc = self.nc
        self.x = self.din("x", [NL, D])
        self.c = self.din("c", [D])
        self.ctx = self.din("ctx", [NCX, D])
        self.c_ctx = self.din("c_ctx", [D])
        shapes = dict(
            mod_w=(2, 1024, 6144), mod_b=(2, 6144), norm1_g=(2, 1024), norm2_g=(2, 1024), w_in=(2, 1024, 5888),
            s5_lam_re=(2, 2, 32, 64), s5_lam_im=(2, 2, 32, 64), s5_log_dt=(2, 2, 32), s5_b_re=(2, 2, 32, 64, 16),
            s5_b_im=(2, 2, 32, 64, 16), s5_c_re=(2, 2, 32, 16, 64), s5_c_im=(2, 2, 32, 16, 64), s5_d=(2, 512),
            s5_glu_w=(2, 512, 512), s5_glu_b=(2, 512), q_norm_g=(2, 64), k_norm_g=(2, 64), attn_sink=(2, 8),
            hy_conv_w=(2, 3, 1536), hy_conv_b=(2, 1536), hy_filt_w1=(2, 33, 64), hy_filt_b1=(2, 64),
            hy_filt_w2=(2, 64, 64), hy_filt_b2=(2, 64), hy_filt_w3=(2, 64, 2048), hy_filt_b3=(2, 2048),
            hy_filt_freq=(2, 64), hy_filt_decay=(2, 2048), hy_bias=(2, 2, 512), w_branch=(2, 3, 512, 1024),
            w_out=(2, 1024, 1024), ffn_w1=(1, 1024, 2816), ffn_w3=(1, 1024, 2816), ffn_w2=(1, 2816, 1024),
            router_w=(1, 1024, 8), moe_w1=(1, 8, 1024, 3584), moe_w3=(1, 8, 1024, 3584), moe_w2=(1, 8, 3584, 1024))
        self.W = {k: self.din(k, v) for k, v in shapes.items()}
        hc = host_constants()
        self.K = {}
        for k, v in hc.items():
            dt = BF16 if v.dtype == ml_dtypes.bfloat16 else F32
            self.K[k] = self.din(k, v.shape, dt)
        self.out = nc.dram_tensor("out", [NL, D], F32, kind="ExternalOutput").ap()
        self.xres = self.dscr("xres", [128, 8, NT], F32)
        self.ybr_a = self.dscr("ybr_a", [128, 4, NT], BF16)
        self.ybr_b = self.dscr("ybr_b", [64, 8, NT], BF16)
        self.ybr_c = self.dscr("ybr_c", [128, 4, NT], BF16)
        self.hcd = self.dscr("hcd", [128, 12, NT], BF16)
        self.combd = self.dscr("combd", [8, NL], F32)
        self.xnd = self.dscr("xnd", [128, 8, NT], BF16)

    def consts(self):
        g = self.gst
        K = self.K
        self.ident32 = self.sb(g, "ident32", [128, 128], F32)
        self.identbf = self.sb(g, "identbf", [128, 128], BF16)
        self.ones_bf = self.sb(g, "ones_bf", [128, 128], BF16)
        self.ones32 = self.sb(g, "ones32", [128, 128], F32)
        self.eps = self.sb(g, "eps", [128, 1], F32)
        self.m01 = self.sb(g, "m01", [128, 2], F32)
        self.altcol = self.sb(g, "altcol", [128, 1], BF16)
        self.altrow = self.sb(g, "altrow", [1, 2048], BF16)
        self.dma('sp', self.ident32[:], K['k_ident32'][:, :], [], ['ident32'])
        self.dma('sp', self.identbf[:], K['k_identbf'][:, :], [], ['identbf'])
        self.dma('sp', self.m01[:], K['k_m01'][:, :], [], ['m01'])
        self.dma('sp', self.altcol[:], K['k_altcol'][:, :], [], ['altcol'])
        self.dma('sp', self.altrow[:], K['k_altrow'][:, :], [], ['altrow'])
        self.memset('dve', self.ones_bf[:], 1.0, ['ones_bf'])
        self.memset('dve', self.ones32[:], 1.0, ['ones32'])
        self.memset('dve', self.eps[:], 1e-6, ['eps'])
        self.ps = [g.enter_context(self.nc.psum_tensor(f"ps{i}", [128, 512], F32)) for i in range(7)]
        self.psb = g.enter_context(self.nc.psum_tensor("psb", [128, 1024], BF16))
        self.P.flush()

    def phase_input(self):
        with contextlib.ExitStack() as st:
            xin = [self.sb(st, "xin", [128, 1024], F32) for _ in range(3)]
            xo = [self.sb(st, "xo", [128, 8, 512], F32) for _ in range(2)]
            for ti in range(18):
                b = ti % 3
                g4, j4 = ti // 4, ti % 4
                ob = g4 % 2
                src = self.x[ti * 128:(ti + 1) * 128, :] if ti < 16 else self.ctx[(ti - 16) * 128:(ti - 15) * 128, :]
                self.dma('sp', xin[b][:], src, [], [f'xin{b}'])
                for k in range(8):
                    bank = 2 * (ti % 2) + k // 4
                    self.tr(self.ps[bank][:, (k % 4) * 128:(k % 4 + 1) * 128], xin[b][:, k * 128:(k + 1) * 128], self.ident32[:],
                            [f'xin{b}', 'ident32'], [f'ps{bank}'])
                for h_ in range(2):
                    bank = 2 * (ti % 2) + h_
                    self.cp('act' if h_ == 0 else 'dve', xo[ob][:, h_ * 4:(h_ + 1) * 4, j4 * 128:(j4 + 1) * 128],
                            self.ps[bank][:, :].rearrange("p (a b) -> p a b", a=4), [f'ps{bank}'], [f'xo{ob}{h_}'])
                if j4 == 3 or ti == 17:
                    t0 = g4 * 512
                    w_ = (j4 + 1) * 128
                    self.dma('sp', self.xres[:, :, t0:t0 + w_], xo[ob][:, :, 0:w_], [f'xo{ob}0', f'xo{ob}1'], [f'xres{g4}'])
            self.P.flush()

    def phase_mod(self, l, lst):
        W = self.W
        self.modT = self.sb(lst, "modT", [128, 48, 2], F32)
        self.gm1 = self.sb(lst, "gm1", [128, 8, 2], F32)
        self.gm2 = self.sb(lst, "gm2", [128, 8, 2], F32)
        with contextlib.ExitStack() as st:
            cT = self.sb(st, "cT", [128, 2, 8], F32)
            scb = self.sb(st, "scb", [128, 8, 2], BF16)
            mb = self.sb(st, "mb", [128, 48], F32)
            ng = self.sb(st, "ng", [128, 2, 8], F32)
            tmp = self.sb(st, "modtmp", [128, 8, 2], F32)
            wblk = [self.sb(st, "mw", [128, 8, 512], BF16) for _ in range(2)]
            self.dma('sp', cT[:, 0, :], self.c.rearrange("(k p) -> p k", p=128), [], ['cT'], slow=True)
            self.dma('sp', cT[:, 1, :], self.c_ctx.rearrange("(k p) -> p k", p=128), [], ['cT'], slow=True)
            self.dma('sp', mb[:], W['mod_b'][l].rearrange("(j p) -> p j", p=128), [], ['mb'], slow=True)
            self.dma('sp', ng[:, 0, :], W['norm1_g'][l].rearrange("(k p) -> p k", p=128), [], ['ng'], slow=True)
            self.dma('sp', ng[:, 1, :], W['norm2_g'][l].rearrange("(k p) -> p k", p=128), [], ['ng'], slow=True)
            self.act(scb[:].rearrange("p k w -> p w k"), cT[:], AF.Silu, ['cT'], ['scb'])
            mw = W['mod_w'][l].rearrange("(k p) n -> p k n", p=128)
            for blk in range(12):
                b = blk % 2
                self.dma('pool', wblk[b][:], mw[:, :, blk * 512:(blk + 1) * 512], [], [f'mw{b}'])
                for jj in range(4):
                    j = blk * 4 + jj
                    for k in range(8):
                        self.mm(self.ps[0][:, j * 2:(j + 1) * 2], wblk[b][:, k, jj * 128:(jj + 1) * 128], scb[:, k, :],
                                k == 0, k == 7, [f'mw{b}', 'scb'], ['ps0'])
            self.tt('dve', self.modT[:], self.ps[0][:, 0:96].rearrange("p (j w) -> p j w", w=2),
                    mb[:].unsqueeze(2).to_broadcast([128, 48, 2]), ALU.add, ['ps0', 'mb'], ['modT'])
            for (gm, sidx, gi) in ((self.gm1, 1, 0), (self.gm2, 4, 1)):
                self.ts('dve', tmp[:], self.modT[:, sidx * 8:(sidx + 1) * 8, :], 1.0, ALU.add, ['modT'], ['modtmp'])
                self.tt('dve', gm[:], tmp[:], ng[:, gi, :].unsqueeze(2).to_broadcast([128, 8, 2]), ALU.mult,
                        ['modtmp', 'ng'], ['gm'])
            self.P.flush()

    def phase_norm(self, l, which, tiles, router=False):
        W = self.W
        sh_idx = 0 if which == 1 else 3
        gm = self.gm1 if which == 1 else self.gm2
        with contextlib.ExitStack() as st:
            xnT = self.sb(st, "xnT", [128, 8, NT], BF16)
            NB = 4
            pbank = [0, 1, 4, 5]
            xt = [self.sb(st, "xt", [128, 8, 512], F32) for _ in range(NB)]
            sq = [self.sb(st, "sq", [128, 8, 512], BF16) for _ in range(NB)]
            rs = [self.sb(st, "rs", [128, 512], F32) for _ in range(NB)]
            if router:
                rw = self.sb(st, "rw", [128, 8, 8], F32)
                lg = self.sb(st, "lg", [128, 4, 8], F32)
                mx = self.sb(st, "mx", [128, 4, 8], F32)
                ee = self.sb(st, "ee", [128, 4, 8], F32)
                mk = self.sb(st, "mk", [128, 4, 8], F32)
                ssum = self.sb(st, "ssum", [128, 4], F32)
                combT = self.sb(st, "combT", [8, NL], F32)
                self.dma('sp', rw[:], W['router_w'][0].rearrange("(k p) e -> p k e", p=128), [], ['rw'])
            def load(it_):
                t0_, sz_, _c = tiles[it_]
                b_ = it_ % NB
                self.dma('sp', xt[b_][:, :, 0:sz_], self.xres[:, :, t0_:t0_ + sz_], ['xres'], [f'xt{b_}'])
            for it_ in range(min(NB, len(tiles))):
                load(it_)
            pending = None
            for it, (t0, sz, col) in enumerate(tiles):
                b = it % NB
                pb = pbank[b]
                self.act(sq[b][:, :, 0:sz], xt[b][:, :, 0:sz], AF.Square, [f'xt{b}'], [f'sq{b}'])
                for k in range(8):
                    self.mm(self.ps[pb][:, 0:sz], self.ones_bf[:], sq[b][:, k, 0:sz], k == 0, k == 7, [f'sq{b}', 'ones_bf'], [f'ps{pb}'])
                self.act(rs[b][:, 0:sz], self.ps[pb][:, 0:sz], AF.Sqrt, [f'ps{pb}', 'eps'], [f'rs{b}'], scale=1.0 / D, bias=self.eps[:, 0:1])
                self.recip(rs[b][:, 0:sz], rs[b][:, 0:sz], [f'rs{b}'], [f'rs{b}'])
                self.tt('dve', xt[b][:, :, 0:sz], xt[b][:, :, 0:sz], rs[b][:, 0:sz].unsqueeze(1).to_broadcast([128, 8, sz]), ALU.mult,
                        [f'xt{b}', f'rs{b}'], [f'xt{b}'])
                for k in range(8):
                    if router:
                        self.act(xt[b][:, k, 0:sz], xt[b][:, k, 0:sz], AF.Identity, [f'xt{b}', 'gm', 'modT'], [f'xt{b}'],
                                 scale=gm[:, k, col:col + 1], bias=self.modT[:, sh_idx * 8 + k, col:col + 1])
                        self.cp('pool', xnT[:, k, t0:t0 + sz], xt[b][:, k, 0:sz], [f'xt{b}'], [f'xnT{it}'])
                    else:
                        self.act(xnT[:, k, t0:t0 + sz], xt[b][:, k, 0:sz], AF.Identity, [f'xt{b}', 'gm', 'modT'], [f'xnT{it}'],
                                 scale=gm[:, k, col:col + 1], bias=self.modT[:, sh_idx * 8 + k, col:col + 1])
                if it + NB < len(tiles):
                    load(it + NB)
                self.dma('sp', self.xnd[:, :, t0:t0 + sz], xnT[:, :, t0:t0 + sz], [f'xnT{it}'], [f'xnd{it}'])
                if router:
                    def router_part(b=b, t0=t0, sz=sz):
                        for s in range(sz // 128):
                            for k in range(8):
                                self.mm(self.ps[2][:, s * 8:(s + 1) * 8], xt[b][:, k, s * 128:(s + 1) * 128], rw[:, k, :], k == 0, k == 7,
                                        [f'xt{b}', 'rw'], ['ps2'])
                        ns = sz // 128
                        self.cp('dve', lg[:, 0:ns, :], self.ps[2][:, 0:ns * 8].rearrange("p (s e) -> p s e", e=8), ['ps2'], ['lg'])
                        for s in range(ns):
                            self.P.op('dve', (lambda o, i_: (lambda e: e.max(out=o, in_=i_)))(mx[:, s, :], lg[:, s, :]), ['lg'], ['mx'])
                        self.tt('dve', ee[:, 0:ns, :], lg[:, 0:ns, :], mx[:, 0:ns, 0:1].to_broadcast([128, ns, 8]), ALU.subtract, ['lg', 'mx'], ['ee'])
                        self.act(ee[:, 0:ns, :], ee[:, 0:ns, :], AF.Exp, ['ee'], ['ee'])
                        self.tt('dve', mk[:, 0:ns, :], lg[:, 0:ns, :], mx[:, 0:ns, 1:2].to_broadcast([128, ns, 8]), ALU.is_ge, ['lg', 'mx'], ['mk'])
                        self.tt('dve', ee[:, 0:ns, :], ee[:, 0:ns, :], mk[:, 0:ns, :], ALU.mult, ['ee', 'mk'], ['ee'])
                        self.P.op('dve', (lambda o, i_: (lambda e: e.tensor_reduce(out=o, in_=i_, axis=AX.X, op=ALU.add)))(ssum[:, 0:ns], ee[:, 0:ns, :]),
                                  ['ee'], ['ssum'])
                        self.recip(ssum[:, 0:ns], ssum[:, 0:ns], ['ssum'], ['ssum'])
                        self.tt('dve', ee[:, 0:ns, :], ee[:, 0:ns, :], ssum[:, 0:ns].unsqueeze(2).to_broadcast([128, ns, 8]), ALU.mult,
                                ['ee', 'ssum'], ['ee'])
                        for s in range(ns):
                            self.tr(self.ps[3][0:8, s * 128:(s + 1) * 128], ee[:, s, :], self.ident32[:], ['ee', 'ident32'], ['ps3'])
                        self.cp('dve', combT[:, t0:t0 + sz], self.ps[3][0:8, 0:sz], ['ps3'], ['combT'])

                    if pending is not None:
                        pending()
                    pending = router_part
            if router:
                pending()
                self.dma('sp', self.combd[:, :], combT[:], ['combT'], ['combd'])
            self.P.flush()


def phase_s5(self, l):
    W = self.W
    K = self.K
    PI = math.pi
    with contextlib.ExitStack() as st:
        sb = lambda n, s, d: self.sb(st, n, s, d)
        uT = sb("uT", [128, 4, NT], BF16)
        N2 = sb("N2", [128, 5, 32], F32)
        cosT = sb("cosT", [128, 32, CH], BF16)
        sinT = sb("sinT", [128, 32, CH], BF16)
        BT = sb("BT", [128, 2, 2, 4, 128], BF16)
        CT = sb("CT", [128, 3, 2, 16, 32], BF16)
        dvec = sb("dvec", [128, 4], F32)
        gb = sb("gb", [128, 4], F32)
        with contextlib.ExitStack() as s2:
            sb2 = lambda n, s, d: self.sb(s2, n, s, d)
            xnT = sb2("xnT", [128, 8, NT], BF16)
            wblk = sb2("ws5", [128, 8, 512], BF16)
            self.load_xnT(xnT, NT)
            self.dma('pool', wblk[:], W['w_in'][l].rearrange("(k p) n -> p k n", p=128)[:, :, 0:512], [], ['ws5'])
            for it, (t0, sz, col) in enumerate(TT):
                for c4 in range(4):
                    bank = c4
                    for k in range(8):
                        self.mm(self.ps[bank][:, 0:sz], wblk[:, k, c4 * 128:(c4 + 1) * 128], xnT[:, k, t0:t0 + sz], k == 0, k == 7,
                                ['ws5', f'xnT_{it}'], [f'ps{bank}'])
                    self.cp('act' if c4 % 2 == 0 else 'dve', uT[:, c4, t0:t0 + sz], self.ps[bank][:, 0:sz], [f'ps{bank}'], [f'uT{c4}'])
            self.P.flush()
        with contextlib.ExitStack() as s2:
            sb2 = lambda n, s, d: self.sb(s2, n, s, d)
            self.dma('sp', dvec[:], W['s5_d'][l].rearrange("(c p) -> p c", p=128), [], ['dvec'], slow=True)
            self.dma('sp', gb[:], W['s5_glu_b'][l].rearrange("(c p) -> p c", p=128), [], ['gb'], slow=True)
            lre = sb2("lre", [64, 64], F32)
            lim = sb2("lim", [64, 64], F32)
            ldt = sb2("ldt", [64, 1], F32)
            self.dma('sp', lre[:], W['s5_lam_re'][l].rearrange("d g p -> (d g) p"), [], ['lre'])
            self.dma('sp', lim[:], W['s5_lam_im'][l].rearrange("d g p -> (d g) p"), [], ['lim'])
            self.dma('sp', ldt[:], W['s5_log_dt'][l].rearrange("d (g o) -> (d g) o", o=1), [], ['ldt'])
            dt = sb2("dt", [64, 1], F32)
            xr = sb2("xr", [64, 64], F32)
            xi = sb2("xi", [64, 64], F32)
            mag = sb2("mag", [64, 64], F32)
            sn = sb2("sn", [64, 64], F32)
            cs = sb2("cs", [64, 64], F32)
            ti = sb2("ti", [64, 64], I32)
            tf = sb2("tf", [64, 64], F32)
            t1 = sb2("t1", [64, 64], F32)
            t2 = sb2("t2", [64, 64], F32)
            abre = sb2("abre", [64, 64], F32)
            abim = sb2("abim", [64, 64], F32)
            rden = sb2("rden", [64, 64], F32)
            Q = sb2("Q", [64, 7, 128], F32)
            self.act(dt[:], ldt[:], AF.Exp, ['ldt'], ['dt'])
            self.ts('dve', xr[:], lre[:], dt[:, 0:1], ALU.mult, ['lre', 'dt'], ['xr'])
            self.ts('dve', xi[:], lim[:], dt[:, 0:1], ALU.mult, ['lim', 'dt'], ['xi'])
            self.act(mag[:], xr[:], AF.Exp, ['xr'], ['mag'])
            self.sin_rr(sn[:], xi[:], 0.0, ti[:], tf[:], 's5a', ['xi'], ['sn'])
            self.sin_rr(cs[:], xi[:], PI / 2, ti[:], tf[:], 's5a', ['xi'], ['cs'])
            self.tt('dve', abre[:], mag[:], cs[:], ALU.mult, ['mag', 'cs'], ['abre'])
            self.tt('dve', abim[:], mag[:], sn[:], ALU.mult, ['mag', 'sn'], ['abim'])
            self.tt('dve', t1[:], lre[:], lre[:], ALU.mult, ['lre'], ['t1'])
            self.tt('dve', t2[:], lim[:], lim[:], ALU.mult, ['lim'], ['t2'])
            self.tt('dve', t1[:], t1[:], t2[:], ALU.add, ['t1', 't2'], ['t1'])
            self.recip(rden[:], t1[:], ['t1'], ['rden'])
            self.ts('dve', t1[:], abre[:], -1.0, ALU.add, ['abre', 'rden'], ['t1'])
            self.tt('dve', t2[:], t1[:], lre[:], ALU.mult, ['t1', 'lre'], ['t2'])
            self.tt('dve', tf[:], abim[:], lim[:], ALU.mult, ['abim', 'lim'], ['s5af'])
            self.tt('dve', t2[:], t2[:], tf[:], ALU.add, ['t2', 's5af'], ['t2'])
            self.tt('dve', Q[:, 5, 0:64], t2[:], rden[:], ALU.mult, ['t2', 'rden'], ['Q5'])
            self.tt('dve', t2[:], abim[:], lre[:], ALU.mult, ['abim', 'lre', 'Q5'], ['t2'])
            self.tt('dve', tf[:], t1[:], lim[:], ALU.mult, ['t1', 'lim'], ['s5af'])
            self.tt('dve', t2[:], t2[:], tf[:], ALU.subtract, ['t2', 's5af'], ['t2'])
            self.tt('dve', Q[:, 6, 0:64], t2[:], rden[:], ALU.mult, ['t2', 'rden'], ['Q6'])
            self.ts('dve', t1[:], xi[:], float(CH), ALU.mult, ['xi', 'Q6'], ['t1'])
            self.sin_rr(sn[:], t1[:], 0.0, ti[:], tf[:], 's5a', ['t1', 'abim'], ['sn'])
            self.sin_rr(cs[:], t1[:], PI / 2, ti[:], tf[:], 's5a', ['t1', 'abre'], ['cs'])
            self.cp('dve', Q[:, 0, 0:64], xi[:], ['xi'], ['Q0'])
            self.cp('dve', Q[:, 1, 0:64], mag[:], ['mag'], ['Q1'])
            self.tt('dve', Q[:, 2, 0:64], mag[:], cs[:], ALU.mult, ['mag', 'cs'], ['Q2'])
            self.tt('dve', Q[:, 3, 0:64], mag[:], sn[:], ALU.mult, ['mag', 'sn'], ['Q3'])
            self.ts('dve', Q[:, 4, 0:64], Q[:, 3, 0:64], -1.0, ALU.mult, ['Q3'], ['Q4'])
            self.cp('dve', Q[:, :, 64:128], Q[:, :, 0:64], [f'Q{i}' for i in range(7)], ['Qd'])
            for qi in range(5):
                self.tr(self.ps[4][:, qi * 64:(qi + 1) * 64], Q[:, qi, :], self.ident32[0:64, 0:64], ['Qd', f'Q{qi}', 'ident32'], ['ps4'])
            self.cp('dve', N2[0:64, :, :], self.ps[4][0:64, 0:320].rearrange("p (q c two) -> p q c two", q=5, two=2)[:, :, :, 0], ['ps4'], ['N2a'])
            self.cp('dve', N2[64:128, :, :], self.ps[4][64:128, 0:320].rearrange("p (q c two) -> p q c two", q=5, two=2)[:, :, :, 1], ['ps4'], ['N2b'])
            fN1 = sb2("fN1", [64, 2, 64], F32)
            for qi in range(2):
                self.tr(self.ps[5][0:64, qi * 64:(qi + 1) * 64], Q[:, 5 + qi, 0:64], self.ident32[0:64, 0:64], [f'Q{5 + qi}', 'ident32'], ['ps5'])
            self.cp('dve', fN1[:], self.ps[5][0:64, 0:128].rearrange("p (q c) -> p q c", q=2), ['ps5'], ['fN1'])
            bre = sb2("bre", [64, 64, 16], F32)
            bim = sb2("bim", [64, 64, 16], F32)
            Bb = sb2("Bb", [64, 2, 64, 16], F32)
            tb1 = sb2("tb1", [64, 64, 16], F32)
            tb2 = sb2("tb2", [64, 64, 16], F32)
            self.dma('sp', bre[:], W['s5_b_re'][l].rearrange("d g p c -> p (d g) c"), [], ['bre'])
            self.dma('sp', bim[:], W['s5_b_im'][l].rearrange("d g p c -> p (d g) c"), [], ['bim'])
            frb = fN1[:, 0, :].unsqueeze(2).to_broadcast([64, 64, 16])
            fib = fN1[:, 1, :].unsqueeze(2).to_broadcast([64, 64, 16])
            self.tt('dve', tb1[:], bre[:], frb, ALU.mult, ['bre', 'fN1'], ['tb1'])
            self.tt('pool', tb2[:], bim[:], fib, ALU.mult, ['bim', 'fN1'], ['tb2'])
            self.tt('dve', Bb[:, 0, :, :], tb1[:], tb2[:], ALU.subtract, ['tb1', 'tb2'], ['Bb0'])
            self.tt('dve', tb1[:], bim[:], frb, ALU.mult, ['bim', 'fN1', 'Bb0'], ['tb1'])
            self.tt('pool', tb2[:], bre[:], fib, ALU.mult, ['bre', 'fN1', 'Bb0'], ['tb2'])
            self.tt('dve', Bb[:, 1, :, :], tb1[:], tb2[:], ALU.add, ['tb1', 'tb2'], ['Bb1'])
            n = 0
            for ri in range(2):
                for d in range(2):
                    for c4 in range(4):
                        bank = 5 + (n % 2)
                        n += 1
                        g0 = d * 32 + c4 * 8
                        self.tr(self.ps[bank][:, 0:64], Bb[:, ri, g0:g0 + 8, :].rearrange("p g c -> p (g c)"), self.ident32[0:64, 0:64],
                                [f'Bb{ri}', 'ident32'], [f'ps{bank}'])
                        self.ts('dve', BT[:, ri, d, c4, 0:64], self.ps[bank][:, 0:64], self.m01[:, 0:1], ALU.mult, [f'ps{bank}', 'm01'], ['BT'])
                        self.ts('dve', BT[:, ri, d, c4, 64:128], self.ps[bank][:, 0:64], self.m01[:, 1:2], ALU.mult, [f'ps{bank}', 'm01'], ['BT'])
            Cl = sb2("Cl", [128, 2, 8, 128], F32)
            for ri, nm in enumerate(('s5_c_re', 's5_c_im')):
                src = W[nm][l].rearrange("d (c4 g8) c p -> (g8 c) (d c4) p", g8=8)
                for dup in range(2):
                    self.dma('sp', Cl[:, ri, :, dup * 64:(dup + 1) * 64], src, [], ['Cl'])
            self.memset('pool', CT[:], 0.0, ['CT'])
            n = 0
            for ri in range(2):
                for d in range(2):
                    for c4 in range(4):
                        bank = 5 + (n % 2)
                        n += 1
                        self.tr(self.ps[bank][:, 0:128], Cl[:, ri, d * 4 + c4, :], self.ident32[:], ['Cl', 'ident32'], [f'ps{bank}'])
                        sc = 1.0 if ri == 0 else -1.0
                        v0 = self.ps[bank][0:64, 0:128].rearrange("p (q g c) -> p q g c", q=4, g=2)[:, :, 0, :]
                        v1 = self.ps[bank][64:128, 0:128].rearrange("p (q g c) -> p q g c", q=4, g=2)[:, :, 1, :]
                        self.act(CT[0:64, ri, d, c4 * 4:(c4 + 1) * 4, 0:16], v0, AF.Identity, [f'ps{bank}'], ['CT'], scale=sc)
                        self.act(CT[64:128, ri, d, c4 * 4:(c4 + 1) * 4, 16:32], v1, AF.Identity, [f'ps{bank}'], ['CT'], scale=sc)
                        if ri == 0:
                            self.act(CT[0:64, 2, d, c4 * 4:(c4 + 1) * 4, 0:16], v0, AF.Identity, [f'ps{bank}'], ['CT'], scale=-1.0)
                            self.act(CT[64:128, 2, d, c4 * 4:(c4 + 1) * 4, 16:32], v1, AF.Identity, [f'ps{bank}'], ['CT'], scale=-1.0)
            tau = sb2("tau", [128, CH], F32)
            ph = sb2("ph", [128, 32, CH], F32)
            tib = sb2("tib", [128, 32 * CH], I32)
            tfb = sb2("tfb", [128, 32 * CH], F32)
            self.dma('sp', tau[:], K['k_tau'][:, :], [], ['tau'])
            self.tt('dve', ph[:], N2[:, 0, :].unsqueeze(2).to_broadcast([128, 32, CH]), tau[:].unsqueeze(1).to_broadcast([128, 32, CH]), ALU.mult,
                    ['N2a', 'N2b', 'tau'], ['ph'])
            phf = ph[:].rearrange("p a b -> p (a b)")
            self.sin_rr(sinT[:].rearrange("p a b -> p (a b)"), phf, 0.0, tib[:], tfb[:], 's5b', ['ph'], ['sinT'])
            self.sin_rr(cosT[:].rearrange("p a b -> p (a b)"), phf, PI / 2, tib[:], tfb[:], 's5b', ['ph'], ['cosT'])
            self.P.flush()
        with contextlib.ExitStack() as s3:
            sb3 = lambda n, s, d: self.sb(s3, n, s, d)
            NCH = NT // CH
            Rt = [sb3("Rt", [128, 2, CH], F32) for _ in range(2)]
            onesc = sb3("onesc", [128, 2, CH], F32)
            self.memset('dve', onesc[:], 1.0, ['onesc'])
            self.memset('dve', onesc[:, 1, 0:1], 0.0, ['onesc'])
            coefX = sb3("coefX", [128, 32, 2], F32)
            ctmp = sb3("ctmp", [128, 2, 1], F32)
            self.cp('dve', coefX[:, :, 0], N2[:, 4, :], ['N2a', 'N2b'], ['coefX'])
            self.cp('dve', coefX[:, :, 1], N2[:, 3, :], ['N2a', 'N2b'], ['coefX'])
            gin = sb3("gin", [128, NCH, 2, CH], BF16)
            gout = [sb3("gout", [128, NCH, 2, CH], F32) for _ in range(2)]
            gob = [sb3("gob", [128, 2, NT], BF16)] * 2
            bfull = [sb3("bfull", [128, 2, NT], BF16) for _ in range(2)]
            dmt = [sb3("dmt", [128, NT], BF16) for _ in range(4)]
            hq = dmt
            gflat = gout[0][:].rearrange("p c r t -> p (c r t)")
            ypre = gflat[:, 0:NT]
            gx = gflat[:, NT:2 * NT]
            yg = sb3("yg", [128, 4, NT], BF16)
            lat_ch = NL // CH
            chunks_f = list(range(lat_ch, NCH)) + list(range(lat_ch))
            chunks_b = list(range(NCH - 1, lat_ch - 1, -1)) + list(range(lat_ch - 1, -1, -1))
            vch = lambda ap: ap.rearrange("p (a b) -> p a b", b=CH)
            units = [(c4, q, d) for c4 in range(4) for q in range(4) for d in range(2)]

            def uinfo(ui):
                c4, q, d = units[ui]
                gp = c4 * 4 + q
                u = d * 16 + gp
                ub = ui % 2
                cosu = cosT[:, u, :] if d == 0 else cosT[:, u, :][:, ::-1]
                sinu = sinT[:, u, :] if d == 0 else sinT[:, u, :][:, ::-1]
                cbN = cosu.unsqueeze(1).to_broadcast([128, NCH, CH])
                sbN = sinu.unsqueeze(1).to_broadcast([128, NCH, CH])
                return c4, q, d, gp, u, ub, cbN, sbN

            def evac(ui):
                c4, q, d, gp, u, ub, cbN, sbN = uinfo(ui)
                bf_ = bfull[ub]
                for it, (t0, sz, col) in enumerate(TT):
                    pb5 = 5 + (it % 2)
                    self.mm(self.ps[pb5][:, 0:sz], BT[32 * q:32 * q + 32, 0, d, c4, :], uT[32 * q:32 * q + 32, c4, t0:t0 + sz], True, True,
                            [f'uT{c4}'], [f'ps{pb5}'], tp=(32 * q, 0))
                    self.cp('act', bf_[:, 0, t0:t0 + sz], self.ps[pb5][:, 0:sz], [f'ps{pb5}'], [f'bfr{ub}'])
                for it, (t0, sz, col) in enumerate(TT):
                    pb5 = 5 + ((it + 1) % 2)
                    self.mm(self.ps[pb5][:, 0:sz], BT[32 * q:32 * q + 32, 1, d, c4, :], uT[32 * q:32 * q + 32, c4, t0:t0 + sz], True, True,
                            [f'uT{c4}'], [f'ps{pb5}'], tp=(32 * q, 0))
                    self.cp('act', bf_[:, 1, t0:t0 + sz], self.ps[pb5][:, 0:sz], [f'ps{pb5}'], [f'bfi{ub}'])

            def mod(ui):
                c4, q, d, gp, u, ub, cbN, sbN = uinfo(ui)
                bf_ = bfull[ub]
                self.ts('dve', Rt[ub][:], onesc[:], N2[:, 1, u:u + 1], ALU.mult, ['onesc'], [f'R{ub}'])
                self.tt('dve', vch(dmt[0][:]), vch(bf_[:, 0, :]), cbN, ALU.mult, [f'bfr{ub}'], ['dm0'])
                self.tt('dve', vch(dmt[1][:]), vch(bf_[:, 1, :]), sbN, ALU.mult, [f'bfi{ub}'], ['dm1'])
                self.tt('dve', gin[:, :, 0, :], vch(dmt[0][:]), vch(dmt[1][:]), ALU.add, ['dm0', 'dm1'], ['ginr'])
                self.tt('dve', vch(dmt[2][:]), vch(bf_[:, 1, :]), cbN, ALU.mult, [f'bfi{ub}'], ['dm2'])
                self.tt('dve', vch(dmt[3][:]), vch(bf_[:, 0, :]), sbN, ALU.mult, [f'bfr{ub}'], ['dm3'])
                self.tt('dve', gin[:, :, 1, :], vch(dmt[2][:]), vch(dmt[3][:]), ALU.subtract, ['dm2', 'dm3'], ['gini'])

            def scans(ui):
                c4, q, d, gp, u, ub, cbN, sbN = uinfo(ui)
                R = Rt[ub]
                go = gout[ub]
                order = chunks_f if d == 0 else chunks_b
                prev = None
                tf_ = 0 if d == 0 else CH - 1
                tl_ = CH - 1 if d == 0 else 0
                gkeys = ['ginr', 'gini']
                gokeys = [f'gor{ub}', f'goi{ub}']
                for ci in order:
                    if prev is not None:
                        gv = gin[:, ci, :, tf_:tf_ + 1]
                        self.stt(gv, go[:, prev, :, tl_:tl_ + 1], N2[:, 2, u:u + 1], gv, ALU.mult, ALU.add, gkeys + gokeys, gkeys)
                        self.tt('dve', ctmp[:], go[:, prev, ::-1, tl_:tl_ + 1], coefX[:, u, :].unsqueeze(2), ALU.mult, gokeys + ['coefX'], ['ctmp'])
                        self.tt('dve', gv, gv, ctmp[:], ALU.add, gkeys + ['ctmp'], gkeys)
                    src = gin[:, ci, :, :].rearrange("p r t -> p (r t)")
                    dst = go[:, ci, :, :].rearrange("p r t -> p (r t)")
                    if d == 1:
                        src = src[:, ::-1]
                        dst = dst[:, ::-1]
                    self.scan(dst, R[:].rearrange("p r t -> p (r t)"), src, gkeys + [f'R{ub}'], gokeys)
                    prev = ci
                gb_ = gob[ub]
                self.cp('act', vch(gb_[:, 0, :]), go[:, :, 0, :], [f'gor{ub}'], ['gbr'])
                self.cp('act', vch(gb_[:, 1, :]), go[:, :, 1, :], [f'goi{ub}'], ['gbi'])

            def demod(ui):
                c4, q, d, gp, u, ub, cbN, sbN = uinfo(ui)
                gb_ = gob[ub]
                self.tt('dve', vch(hq[0][:]), vch(gb_[:, 0, :]), cbN, ALU.mult, ['gbr'], ['dm0'])
                self.tt('dve', vch(hq[1][:]), vch(gb_[:, 1, :]), sbN, ALU.mult, ['gbi'], ['dm1'])
                self.tt('dve', vch(hq[2][:]), vch(gb_[:, 0, :]), sbN, ALU.mult, ['gbr'], ['dm2'])
                self.tt('dve', vch(hq[3][:]), vch(gb_[:, 1, :]), cbN, ALU.mult, ['gbi'], ['dm3'])
                csel = (0, 2, 1, 1)
                for it, (t0, sz, col) in enumerate(TT):
                    for m in range(4):
                        self.mm(self.ps[it][32 * q:32 * q + 32, 0:sz], CT[:, csel[m], d, gp, :], hq[m][:, t0:t0 + sz],
                                d == 0 and m == 0, d == 1 and m == 3, [f'dm{m}'], [f'ps{it}'], tp=(0, 32 * q))

            def gelu(c4):
                for it, (t0, sz, col) in enumerate(TT):
                    self.stt(ypre[:, t0:t0 + sz], uT[:, c4, t0:t0 + sz], dvec[:, c4:c4 + 1], self.ps[it][:, 0:sz], ALU.mult, ALU.add,
                             [f'ps{it}', 'dvec'], ['gor0'])
                self.tt('pool', gx, ypre, ypre, ALU.mult, ['gor0'], ['goi0'])
                self.ts('pool', gx, gx, 0.044715, ALU.mult, ['goi0'], ['goi0'], s2=1.0, op1=ALU.add)
                self.tt('pool', gx, gx, ypre, ALU.mult, ['goi0', 'gor0'], ['goi0'])
                self.act(gx, gx, AF.Sigmoid, ['goi0'], ['goi0'], scale=1.5957691216057308)
                self.tt('pool', yg[:, c4, :], ypre, gx, ALU.mult, ['goi0', 'gor0'], [f'yg{c4}'])

            nu = len(units)
            evac(0)
            mod(0)
            for ui in range(nu):
                if ui + 1 < nu:
                    evac(ui + 1)
                scans(ui)
                if ui + 1 < nu:
                    mod(ui + 1)
                demod(ui)
                if ui % 8 == 7:
                    gelu(units[ui][0])
            gw = sb3("gw", [128, 4, 512], BF16)
            self.dma('pool', gw[:], W['s5_glu_w'][l].rearrange("(c p) n -> p c n", p=128), [], ['gw'])
            yat = [sb3("yat", [128, 4, 512], BF16) for _ in range(2)]
            sgt = [sb3("sgt", [128, 512], F32) for _ in range(2)]
            n = 0
            for it, (t0, sz, col) in enumerate(TT):
                b = it % 2
                for co in range(4):
                    bank = 5 + (n % 2)
                    sg = sgt[n % 2]
                    n += 1
                    for ci in range(4):
                        self.mm(self.ps[bank][:, 0:sz], gw[:, ci, co * 128:(co + 1) * 128], yg[:, ci, t0:t0 + sz], ci == 0, ci == 3,
                                ['gw'] + [f'yg{i}' for i in range(4)], [f'ps{bank}'])
                    self.act(sg[:, 0:sz], self.ps[bank][:, 0:sz], AF.Sigmoid, [f'ps{bank}', 'gb'], [f'sg{n % 2}'], bias=gb[:, co:co + 1])
                    self.tt('dve', yat[b][:, co, 0:sz], yg[:, co, t0:t0 + sz], sg[:, 0:sz], ALU.mult, [f'sg{n % 2}', f'yg{co}'], [f'yat{b}'])
                self.dma('sp', self.ybr_a[:, :, t0:t0 + sz], yat[b][:, :, 0:sz], [f'yat{b}'], ['ybr_a'])
            self.P.flush()


KB.phase_s5 = phase_s5

def phase_att(self, l):
    W = self.W
    K = self.K
    need_ctx = (l == 0)
    with contextlib.ExitStack() as st:
        sb = lambda n, s, d: self.sb(st, n, s, d)
        qT = sb("qT", [128, 4, NT], BF16)
        kT = sb("kT", [128, NT], BF16)
        vtm = sb("vtm", [128, 18, 128], BF16)
        yb = sb("yb", [64, 8, NT], BF16)
        with contextlib.ExitStack() as s2:
            sb2 = lambda n, s, d: self.sb(s2, n, s, d)
            xnT = sb2("xnT", [128, 8, NT], BF16)
            wraw = sb2("wraw", [128, 8, 768], BF16)
            wqk = sb2("wqk", [128, 8, 640], BF16)
            gq = sb2("gq", [128, 1], F32)
            gk = sb2("gk", [128, 1], F32)
            cos2 = sb2("cos2", [128, NL], F32)
            sin2 = sb2("sin2", [128, NL], F32)
            sperm = sb2("sperm", [128, 128], BF16)
            ones2 = sb2("ones2", [128, 128], BF16)
            self.load_xnT(xnT, NT)
            self.dma('pool', wraw[:], W['w_in'][l].rearrange("(k p) n -> p k n", p=128)[:, :, 512:1280], [], ['wraw'])
            self.memset('dve', sperm[:], 0.0, ['sperm'])
            self.memset('dve', ones2[:], 0.0, ['ones2'])
            for hf in range(2):
                ps_ = slice(hf * 64, (hf + 1) * 64)
                self.dma('sp', cos2[ps_, :], K['k_cos2'][:, :], [], ['cos2'])
                self.dma('sp', sin2[ps_, :], K['k_sin2'][:, :], [], ['sin2'])
                self.dma('sp', sperm[ps_, hf * 64:(hf + 1) * 64], K['k_sperm'][:, :], ['sperm'], ['sperm'])
                self.memset('dve', ones2[ps_, hf * 64:(hf + 1) * 64], 1.0, ['ones2'])
                for g_t, nm in ((gq, 'q_norm_g'), (gk, 'k_norm_g')):
                    src = W[nm][l].rearrange("(i t) -> i t", t=2)
                    self.dma('sp', g_t[hf * 64:hf * 64 + 32, :], src[:, 0:1], [], ['gqk'], slow=True)
                    self.dma('sp', g_t[hf * 64 + 32:hf * 64 + 64, :], src[:, 1:2], [], ['gqk'], slow=True)
            for k in range(8):
                for g0 in range(2):
                    self.cp('pool' if (k + g0) % 2 == 0 else 'dve',
                            wqk[:, k, 0:512].rearrange("p (q g t i) -> p q g t i", q=4, g=2, t=2, i=32)[:, :, g0, :, :],
                            wraw[:, k, g0 * 256:(g0 + 1) * 256].rearrange("p (q i t) -> p q t i", q=4, i=32, t=2), ['wraw'], [f'wqk{k}_{g0}'])
                self.cp('dve' if k % 2 == 0 else 'pool',
                        wqk[:, k, 512:640].rearrange("p (h t i) -> p h t i", h=2, t=2, i=32),
                        wraw[:, k, 512:640].rearrange("p (h i t) -> p h t i", h=2, i=32, t=2), ['wraw'], [f'wqk{k}_2'])
            wqk_keys = [f'wqk{k}_{g}' for k in range(8) for g in range(3)]
            sqt = [sb2("sqt", [128, 512], BF16) for _ in range(2)]
            rst = [sb2("rst", [128, 512], F32) for _ in range(2)]
            qnt = [sb2("qnt", [128, 512], BF16) for _ in range(2)]
            r1t = [sb2("r1t", [128, 512], F32) for _ in range(2)]
            r2t = [sb2("r2t", [128, 512], F32) for _ in range(2)]
            cnt = [0]

            ulist = []
            for it, (t0, sz, col) in enumerate(TT):
                for pr in range(5):
                    if pr < 4 and col == 1 and not need_ctx:
                        continue
                    ulist.append((it, t0, sz, col, pr))
            pbanks = [0, 1, 6]

            def uinfo(n):
                it, t0, sz, col, pr = ulist[n]
                bank = pbanks[n % 3]
                b = n % 2
                g_t = gq if pr < 4 else gk
                dst = qT[:, pr, t0:t0 + sz] if pr < 4 else kT[:, t0:t0 + sz]
                dstkey = 'qT' if pr < 4 else 'kT'
                rope_t0 = t0 if col == 0 else None
                return it, t0, sz, pr, bank, b, g_t, dst, dstkey, rope_t0

            def proj(n):
                it, t0, sz, pr, bank, b, g_t, dst, dstkey, rope_t0 = uinfo(n)
                for k in range(8):
                    self.mm(self.ps[bank][:, 0:sz], wqk[:, k, pr * 128:(pr + 1) * 128], xnT[:, k, t0:t0 + sz], k == 0, k == 7,
                            wqk_keys + [f'xnT_{it}'], [f'ps{bank}'])

            def st1(n):
                it, t0, sz, pr, bank, b, g_t, dst, dstkey, rope_t0 = uinfo(n)
                self.act(sqt[b][:, 0:sz], self.ps[bank][:, 0:sz], AF.Square, [f'ps{bank}'], [f'sqt{b}'])
                self.mm(self.ps[2 + b][:, 0:sz], ones2[:], sqt[b][:, 0:sz], True, True, [f'sqt{b}', 'ones2'], [f'ps{2 + b}'])

            def st2(n):
                it, t0, sz, pr, bank, b, g_t, dst, dstkey, rope_t0 = uinfo(n)
                raw = self.ps[bank][:, 0:sz]
                self.act(rst[b][:, 0:sz], self.ps[2 + b][:, 0:sz], AF.Sqrt, [f'ps{2 + b}', 'eps'], [f'rst{b}'], scale=1.0 / 64.0, bias=self.eps[:, 0:1])
                self.recip(rst[b][:, 0:sz], rst[b][:, 0:sz], [f'rst{b}'], [f'rst{b}'])
                if rope_t0 is None:
                    self.stt(dst, raw, g_t[:, 0:1], rst[b][:, 0:sz], ALU.mult, ALU.mult, [f'ps{bank}', f'rst{b}', 'gqk'], [dstkey])
                    return
                self.stt(qnt[b][:, 0:sz], raw, g_t[:, 0:1], rst[b][:, 0:sz], ALU.mult, ALU.mult, [f'ps{bank}', f'rst{b}', 'gqk'], [f'qnt{b}'])
                self.mm(self.ps[4 + b][:, 0:sz], sperm[:], qnt[b][:, 0:sz], True, True, [f'qnt{b}', 'sperm'], [f'ps{4 + b}'])

            def st3(n):
                it, t0, sz, pr, bank, b, g_t, dst, dstkey, rope_t0 = uinfo(n)
                if rope_t0 is None:
                    return
                self.tt('pool', r1t[b][:, 0:sz], qnt[b][:, 0:sz], cos2[:, rope_t0:rope_t0 + sz], ALU.mult, [f'qnt{b}', 'cos2'], [f'r1t{b}'])
                self.tt('dve', r2t[b][:, 0:sz], self.ps[4 + b][:, 0:sz], sin2[:, rope_t0:rope_t0 + sz], ALU.mult, [f'ps{4 + b}', 'sin2'], [f'r2t{b}'])
                self.tt('pool', dst, r1t[b][:, 0:sz], r2t[b][:, 0:sz], ALU.add, [f'r1t{b}', f'r2t{b}'], [dstkey])

            NU = len(ulist)
            proj(0)
            for n in range(NU + 2):
                if n + 1 < NU:
                    proj(n + 1)
                if n < NU:
                    st1(n)
                if 0 <= n - 1 < NU:
                    st2(n - 1)
                if 0 <= n - 2 < NU:
                    st3(n - 2)
            vbanks = [6, 0, 1]
            for i in range(18):
                vb = vbanks[i % 3]
                for k in range(8):
                    self.mm(self.ps[vb][:, 0:128], xnT[:, k, i * 128:(i + 1) * 128], wraw[:, k, 640:768], k == 0, k == 7,
                            [f'xnT_{min(i // 4, 4)}', 'wraw'], [f'ps{vb}'])
                self.cp('act', vtm[:, i, :], self.ps[vb][:, 0:128], [f'ps{vb}'], ['vtm'])
            self.P.flush()
        with contextlib.ExitStack() as s3:
            sb3 = lambda n, s, d: self.sb(s3, n, s, d)
            mprev = sb3("mprev", [128, 512], BF16)
            mnext = sb3("mnext", [128, 512], BF16)
            sk = sb3("sk", [64, 8], F32)
            eskb = sb3("eskb", [64, 8, 128], F32)
            pt = [sb3("pt", [128, 512], BF16) for _ in range(4)]
            sbanks = [0, 1, 6]
            dsum = [sb3("dsum", [64, 512], F32) for _ in range(2)]
            self.dma('sp', mprev[:], K['k_mprev'][:, :], [], ['mprev'])
            self.dma('sp', mnext[:], K['k_mnext'][:, :], [], ['mnext'])
            self.dma('sp', sk[:], W['attn_sink'][l].partition_broadcast(64), [], ['sk'])
            self.act(sk[:], sk[:], AF.Exp, ['sk'], ['sk'])
            self.cp('dve', eskb[:], sk[:].unsqueeze(2).to_broadcast([64, 8, 128]), ['sk'], ['eskb'])
            qbs = list(range(16)) + ([16, 17] if need_ctx else [])
            n = 0
            m = 0
            for qb in qbs:
                for kvh in range(2):
                    if qb < 16:
                        kbs = ([(qb - 1, 'prev')] if qb > 0 else []) + [(qb, 'own')] + ([(qb + 1, 'next')] if qb < 15 else []) + [(16, 'ctx'), (17, 'ctx')]
                    else:
                        kbs = [(16, 'ctx'), (17, 'ctx')]
                    ob = 2 + 2 * (m % 2)
                    db = ob + 1
                    mb_ = m % 2
                    m += 1
                    def score(j):
                        kb, kind = kbs[j]
                        sbk = sbanks[(n + j) % 3]
                        pb = (n + j) % 4
                        self.mm(self.ps[sbk][:, :], kT[kvh * 64:(kvh + 1) * 64, kb * 128:(kb + 1) * 128], qT[kvh * 64:(kvh + 1) * 64, :, qb * 128:(qb + 1) * 128],
                                True, True, ['kT', 'qT'], [f'ps{sbk}'])
                        self.act(pt[pb][:], self.ps[sbk][:, :], AF.Exp, [f'ps{sbk}'], [f'pt{pb}'], scale=0.125)
                        if kind == 'prev':
                            self.tt('pool', pt[pb][:], pt[pb][:], mprev[:], ALU.mult, [f'pt{pb}', 'mprev'], [f'pt{pb}'])
                        elif kind == 'next':
                            self.tt('pool', pt[pb][:], pt[pb][:], mnext[:], ALU.mult, [f'pt{pb}', 'mnext'], [f'pt{pb}'])

                    def pv(j):
                        kb, kind = kbs[j]
                        pb = (n + j) % 4
                        self.mm(self.ps[ob][0:64, :], vtm[:, kb, kvh * 64:(kvh + 1) * 64], pt[pb][:], j == 0, j == len(kbs) - 1,
                                ['vtm', f'pt{pb}'], [f'ps{ob}'])
                        self.mm(self.ps[db][0:64, :], self.ones_bf[:, 0:64], pt[pb][:], j == 0, j == len(kbs) - 1,
                                ['ones_bf', f'pt{pb}'], [f'ps{db}'])

                    score(0)
                    if len(kbs) > 1:
                        score(1)
                    for j in range(len(kbs)):
                        if j + 2 < len(kbs):
                            score(j + 2)
                        pv(j)
                    n += len(kbs)
                    ds = dsum[mb_]
                    self.tt('dve', ds[:].rearrange("p (a b) -> p a b", a=4), self.ps[db][0:64, :].rearrange("p (a b) -> p a b", a=4),
                            eskb[:, kvh * 4:(kvh + 1) * 4, :], ALU.add, [f'ps{db}', 'eskb'], [f'dsum{mb_}'])
                    self.recip(ds[:], ds[:], [f'dsum{mb_}'], [f'dsum{mb_}'])
                    self.tt('dve', yb[:, kvh * 4:(kvh + 1) * 4, qb * 128:(qb + 1) * 128], self.ps[ob][0:64, :].rearrange("p (a b) -> p a b", a=4),
                            ds[:].rearrange("p (a b) -> p a b", a=4), ALU.mult, [f'ps{ob}', f'dsum{mb_}'], ['yb'])
            hi = NT if need_ctx else NL
            self.dma('sp', self.ybr_b[:, :, 0:hi], yb[:, :, 0:hi], ['yb'], ['ybr_b'])
            self.P.flush()


KB.phase_att = phase_att

def phase_ffn(self, l):
    W = self.W
    dense = (l == 0)
    tiles = TT if dense else TT[:4]
    ntok = NT if dense else NL
    H = 2816 if dense else 3584
    nh = H // 128
    nexp = 1 if dense else 8
    with contextlib.ExitStack() as st:
        sb = lambda n, s, d: self.sb(st, n, s, d)
        xnT = sb("xnT", [128, 8, ntok], BF16)
        hT = sb("hT", [128, nh, ntok], BF16)
        w1b = [sb("w1b", [128, 8, 256], BF16) for _ in range(2)]
        w3b = [sb("w3b", [128, 8, 256], BF16) for _ in range(2)]
        w2b = [sb("w2b", [128, nh, 128], BF16) for _ in range(2)]
        sl = [sb("silu", [128, 512], F32) for _ in range(2)]
        xo = [sb("xo", [128, 512], F32) for _ in range(3)]
        tmpc = [sb("tmpc", [128, 512], F32) for _ in range(2)]
        if not dense:
            comb = [sb("comb", [128, NL], F32)] * 2
        self.load_xnT(xnT, ntok)
        na = 0
        nb = 0
        nw2 = 0
        for e in range(nexp):
            if dense:
                w1, w3, w2 = W['ffn_w1'][0], W['ffn_w3'][0], W['ffn_w2'][0]
            else:
                w1, w3, w2 = W['moe_w1'][0][e], W['moe_w3'][0][e], W['moe_w2'][0][e]
                cbt = comb[e % 2]
                self.dma('sp', cbt[:], self.combd[e:e + 1, :].partition_broadcast(128), [], ['comb0'])
            w1v = w1.rearrange("(k p) n -> p k n", p=128)
            w3v = w3.rearrange("(k p) n -> p k n", p=128)
            w2v = w2.rearrange("(j p) n -> p j n", p=128)
            for hb in range(H // 256):
                wbuf = (e * (H // 256) + hb) % 2
                self.dma('pool', w1b[wbuf][:], w1v[:, :, hb * 256:(hb + 1) * 256], [], [f'w1b{wbuf}'])
                self.dma('pool', w3b[wbuf][:], w3v[:, :, hb * 256:(hb + 1) * 256], [], [f'w3b{wbuf}'])
                for js in range(2):
                    j = hb * 2 + js
                    for it, (t0, sz, col) in enumerate(tiles):
                        pa = 2 * (na % 2)
                        pb = pa + 1
                        sk = na % 2
                        na += 1
                        for k in range(8):
                            self.mm(self.ps[pa][:, 0:sz], w1b[wbuf][:, k, js * 128:(js + 1) * 128], xnT[:, k, t0:t0 + sz], k == 0, k == 7,
                                    [f'w1b{wbuf}', f'xnT_{it}'], [f'ps{pa}'])
                        for k in range(8):
                            self.mm(self.ps[pb][:, 0:sz], w3b[wbuf][:, k, js * 128:(js + 1) * 128], xnT[:, k, t0:t0 + sz], k == 0, k == 7,
                                    [f'w3b{wbuf}', f'xnT_{it}'], [f'ps{pb}'])
                        self.act(sl[sk][:, 0:sz], self.ps[pa][:, 0:sz], AF.Silu, [f'ps{pa}'], [f'silu{sk}'])
                        self.tt('dve', hT[:, j, t0:t0 + sz], self.ps[pb][:, 0:sz], sl[sk][:, 0:sz], ALU.mult, [f'ps{pb}', f'silu{sk}'], [f'hT{j}'])
            hkeys = [f'hT{j}' for j in range(nh)]
            for dt in range(8):
                wb = nw2 % 2
                nw2 += 1
                self.dma('pool', w2b[wb][:], w2v[:, :, dt * 128:(dt + 1) * 128], [], [f'w2b{wb}'])
                for it, (t0, sz, col) in enumerate(tiles):
                    bank = 4 + (nb % 2)
                    xb = nb % 3
                    tk = nb % 2
                    nb += 1
                    self.dma('sp', xo[xb][:, 0:sz], self.xres[:, dt, t0:t0 + sz], [f'xres{dt}_{it}'], [f'xo{xb}'])
                    for j in range(nh):
                        self.mm(self.ps[bank][:, 0:sz], w2b[wb][:, j, :], hT[:, j, t0:t0 + sz], j == 0, j == nh - 1, [f'w2b{wb}'] + hkeys, [f'ps{bank}'])
                    g2 = self.modT[:, 40 + dt, col:col + 1]
                    if dense:
                        self.stt(xo[xb][:, 0:sz], self.ps[bank][:, 0:sz], g2, xo[xb][:, 0:sz], ALU.mult, ALU.add, [f'ps{bank}', f'xo{xb}', 'modT'], [f'xo{xb}'])
                    else:
                        self.tt('dve', tmpc[tk][:, 0:sz], self.ps[bank][:, 0:sz], cbt[:, t0:t0 + sz], ALU.mult, [f'ps{bank}', f'comb{e % 2}'], [f'tmpc{tk}'])
                        self.stt(xo[xb][:, 0:sz], tmpc[tk][:, 0:sz], g2, xo[xb][:, 0:sz], ALU.mult, ALU.add, [f'tmpc{tk}', f'xo{xb}', 'modT'], [f'xo{xb}'])
                    self.dma('sp', self.xres[:, dt, t0:t0 + sz], xo[xb][:, 0:sz], [f'xo{xb}'], [f'xres{dt}_{it}'])
        self.P.flush()


def phase_output(self):
    with contextlib.ExitStack() as st:
        xt = [self.sb(st, "xto", [128, 8, 512], F32) for _ in range(2)]
        ot = [self.sb(st, "oto", [128, 1024], F32) for _ in range(2)]
        self.dma('sp', xt[0][:], self.xres[:, :, 0:512], [], ['xto0'])
        for g4 in range(4):
            xb = g4 % 2
            if g4 + 1 < 4:
                self.dma('sp', xt[1 - xb][:], self.xres[:, :, (g4 + 1) * 512:(g4 + 2) * 512], [], [f'xto{1 - xb}'])
            for j4 in range(4):
                ti = g4 * 4 + j4
                b = ti % 2
                for k in range(8):
                    bank = 2 * b + k // 4
                    self.tr(self.ps[bank][:, (k % 4) * 128:(k % 4 + 1) * 128], xt[xb][:, k, j4 * 128:(j4 + 1) * 128], self.ident32[:],
                            [f'xto{xb}', 'ident32'], [f'ps{bank}'])
                self.cp('act', ot[b][:, 0:512], self.ps[2 * b][:, :], [f'ps{2 * b}'], [f'oto{b}a'])
                self.cp('dve', ot[b][:, 512:1024], self.ps[2 * b + 1][:, :], [f'ps{2 * b + 1}'], [f'oto{b}b'])
                self.dma('sp', self.out[ti * 128:(ti + 1) * 128, :], ot[b][:], [f'oto{b}a', f'oto{b}b'], [f'out{ti}'])
        self.P.flush()


KB.phase_ffn = phase_ffn
KB.phase_output = phase_output

def phase_hy(self, l):
    W = self.W
    K = self.K
    need_ctx = (l == 0)
    hi_tok = NT if need_ctx else NL
    with contextlib.ExitStack() as st:
        sb = lambda n, s, d: self.sb(st, n, s, d)
        xnT = sb("xnT", [128, 8, NT], BF16)
        wblk = [sb("why", [128, 8, 512], BF16) for _ in range(2)]
        zc = [sb("zc", [128, NT], BF16) for _ in range(2)]
        ho = [sb("ho", [128, NT], BF16) for _ in range(2)]
        cw = sb("cw", [128, 3, 12], F32)
        cb = sb("cb", [128, 12], F32)
        self.load_xnT(xnT, NT)
        for tap in range(3):
            self.dma('sp', cw[:, tap, :], W['hy_conv_w'][l][tap].rearrange("(k p) -> p k", p=128), [], ['cw'], slow=True)
        self.dma('sp', cb[:], W['hy_conv_b'][l].rearrange("(k p) -> p k", p=128), [], ['cb'], slow=True)
        wsrc = W['w_in'][l].rearrange("(k p) n -> p k n", p=128)
        n = 0
        for j in range(12):
            blk, jj = j // 4, j % 4
            wb = blk % 2
            zb = j % 2
            if jj == 0:
                self.dma('pool', wblk[wb][:], wsrc[:, :, 1280 + blk * 512:1280 + (blk + 1) * 512], [], [f'why{wb}'])
            for it, (t0, sz, col) in enumerate(TT):
                if col == 1 and not need_ctx:
                    continue
                bank = n % 4
                n += 1
                for k in range(8):
                    self.mm(self.ps[bank][:, 0:sz], wblk[wb][:, k, jj * 128:(jj + 1) * 128], xnT[:, k, t0:t0 + sz], k == 0, k == 7,
                            [f'why{wb}', f'xnT_{it}'], [f'ps{bank}'])
                self.cp('act', zc[zb][:, t0:t0 + sz], self.ps[bank][:, 0:sz], [f'ps{bank}'], [f'zc{zb}'])
            for (a, b) in ([(0, NL), (NL, NT)] if need_ctx else [(0, NL)]):
                self.ts('dve', ho[zb][:, a:b], zc[zb][:, a:b], cw[:, 1, j:j + 1], ALU.mult, [f'zc{zb}', 'cw', 'cb'], [f'ho{zb}'], s2=cb[:, j:j + 1], op1=ALU.add)
                self.stt(ho[zb][:, a + 1:b], zc[zb][:, a:b - 1], cw[:, 0, j:j + 1], ho[zb][:, a + 1:b], ALU.mult, ALU.add, [f'zc{zb}', f'ho{zb}', 'cw'], [f'ho{zb}'])
                self.stt(ho[zb][:, a:b - 1], zc[zb][:, a + 1:b], cw[:, 2, j:j + 1], ho[zb][:, a:b - 1], ALU.mult, ALU.add, [f'zc{zb}', f'ho{zb}', 'cw'], [f'ho{zb}'])
            self.dma('sp', self.hcd[:, j, 0:hi_tok], ho[zb][:, 0:hi_tok], [f'ho{zb}'], ['hcd'])
        self.P.flush()
    if self.halt_at(f'hyA_{l}'):
        return
    seqs = [('lat', NL, 0)] + ([('ctx', NCX, NL)] if need_ctx else [])
    for (sname, L, toff) in seqs:
        nf = L // 128
        N = 2 * L
        with contextlib.ExitStack() as sq_:
            sbq = lambda n, s, d: self.sb(sq_, n, s, d)
            Kh = sbq("Kh", [128, nf, 2, 512], BF16)
            Zst = sbq("Zst", [128, nf, 2, 512], BF16)
            vbuf = [sbq("vbuf", [128, 4, L], BF16) for _ in range(2)]
            h2T = sbq("h2T", [64, L], F32)
            h2Tb = sbq("h2Tb", [64, L], BF16)
            knyq = sbq("knyq", [1, 512], F32)
            znyq = sbq("znyq", [1, 512], BF16)
            hbias = sbq("hbias", [128, 2, 4], F32)
            fscale = sbq("fscale", [128, nf], F32)
            negt = sbq("negt", [128, nf], F32)
            with contextlib.ExitStack() as s1:
                sb1 = lambda n, s, d: self.sb(s1, n, s, d)
                featsT = sb1("featsT", [33, L], F32)
                w1f = sb1("w1f", [33, 64], F32)
                w2f = sb1("w2f", [64, 64], F32)
                bfr = sb1("bfr", [64, 3], F32)
                h1T = sb1("h1T", [64, L], F32)
                zt = sb1("zt", [64, 512], F32)
                tiq = sb1("tiq", [64, 512], I32)
                tfq = sb1("tfq", [64, 512], F32)
                self.dma('sp', featsT[:], K['k_featsT' if sname == 'lat' else 'k_featsT_c'][:, :], [], ['featsT'])
                self.dma('sp', w1f[:], W['hy_filt_w1'][l], [], ['w1f'])
                self.dma('sp', w2f[:], W['hy_filt_w2'][l], [], ['w2f'])
                self.dma('sp', bfr[:, 0:1], W['hy_filt_b1'][l].rearrange("(p o) -> p o", o=1), [], ['bfr'])
                self.dma('sp', bfr[:, 1:2], W['hy_filt_b2'][l].rearrange("(p o) -> p o", o=1), [], ['bfr'])
                self.dma('sp', bfr[:, 2:3], W['hy_filt_freq'][l].rearrange("(p o) -> p o", o=1), [], ['bfr'])
                for oo in range(2):
                    self.dma('sp', hbias[:, oo, :], W['hy_bias'][l][oo].rearrange("(c p) -> p c", p=128), [], ['hbias'], slow=True)
                self.dma('sp', fscale[:], K['k_fscale' if sname == 'lat' else 'k_fscale_c'][:, :], [], ['fscale'])
                self.dma('sp', negt[:], K['k_negt' if sname == 'lat' else 'k_negt_c'][:, :], [], ['negt'])
                bw = min(512, L)
                for (wf, kin, src, dst, bcol, srck, dstk) in ((w1f, 33, featsT, h1T, 0, 'featsT', 'h1T'), (w2f, 64, h1T, h2T, 1, 'h1T', 'h2T')):
                    for tb in range(L // bw):
                        self.mm(self.ps[0][0:64, 0:bw], wf[0:kin, 0:64], src[0:kin, tb * bw:(tb + 1) * bw], True, True, [srck, 'w1f', 'w2f'], ['ps0'])
                        self.ts('dve', zt[:, 0:bw], self.ps[0][0:64, 0:bw], bfr[:, bcol:bcol + 1], ALU.add, ['ps0', 'bfr'], ['zt'], s2=bfr[:, 2:3], op1=ALU.mult)
                        self.sin_rr(dst[:, tb * bw:(tb + 1) * bw], zt[:, 0:bw], 0.0, tiq[:, 0:bw], tfq[:, 0:bw], 'hyq', ['zt'], [dstk])
                self.cp('dve', h2Tb[:], h2T[:], ['h2T'], ['h2Tb'])
                self.P.flush()
            if self.halt_at(f'hyM_{l}'):
                return
            for o in range(2):
                vT = vbuf[o % 2]
                yT = vbuf[(o + 1) % 2]
                with contextlib.ExitStack() as s2:
                    sb2 = lambda n, s, d: self.sb(s2, n, s, d)
                    w3f = sb2("w3f", [64, 1024], BF16)
                    b3r = sb2("b3r", [1, 1024], BF16)
                    dcy = sb2("dcy", [128, 1024], F32)
                    hst = sb2("hst", [128, nf, 1024], BF16)
                    dect = [sb2("dect", [128, 512], F32) for _ in range(2)]
                    sqt = [sb2("sqh", [128, 512], BF16) for _ in range(2)]
                    sc = sb2("sc", [128, 512], F32)
                    tsum = [sb2("tsum", [128, 512], BF16) for _ in range(2)]
                    tab = [[sb2("tab", [128, nf, 128], BF16) for _ in range(2)] for _ in range(2)]
                    self.dma('pool', w3f[:], W['hy_filt_w3'][l][:, o * 1024:(o + 1) * 1024], [], ['w3f'])
                    self.dma('pool', b3r[:], W['hy_filt_b3'][l].rearrange("(o n) -> o n", o=2)[o:o + 1, :], [], ['b3r'])
                    self.dma('sp', dcy[:], W['hy_filt_decay'][l].rearrange("(o n) -> o n", o=2)[o:o + 1, :].partition_broadcast(128), [], ['dcy'])
                    n = 0
                    for i in range(nf):
                        for dr in range(2):
                            b = n % 2
                            n += 1
                            cs_ = slice(dr * 512, (dr + 1) * 512)
                            self.mm(self.ps[b][:, :], h2Tb[0:64, i * 128:(i + 1) * 128], w3f[0:64, cs_], True, False, ['h2Tb', 'w3f'], [f'ps{b}'])
                            self.mm(self.ps[b][:, :], self.ones_bf[0:1, 0:128], b3r[0:1, cs_], False, True, ['ones_bf', 'b3r'], [f'ps{b}'])
                            self.act(dect[b][:], dcy[:, cs_], AF.Exp, ['dcy', 'negt'], [f'dect{b}'], scale=negt[:, i:i + 1])
                            self.tt('dve', hst[:, i, cs_], self.ps[b][:, :], dect[b][:], ALU.mult, [f'ps{b}', f'dect{b}'], [f'hst{i}'])
                            self.act(sqt[b][:], hst[:, i, cs_], AF.Square, [f'hst{i}'], [f'sqh{b}'])
                            self.mm(self.ps[4 + dr][:, :], self.ones_bf[:], sqt[b][:], i == 0, i == nf - 1, [f'sqh{b}', 'ones_bf'], [f'ps{4 + dr}'])
                        if i == 0:
                            self.memset('pool', hst[0:1, 0, 512:1024], 0.0, ['hst0'])
                        tb2 = i % 2
                        self.tt('pool', tsum[tb2][:], hst[:, i, 0:512], hst[:, i, 512:1024], ALU.add, [f'hst{i}'], [f'tsum{tb2}'])
                        self.tt('pool', hst[:, i, 512:1024], hst[:, i, 0:512], hst[:, i, 512:1024], ALU.subtract, [f'hst{i}'], [f'hst{i}'])
                        self.cp('pool', hst[:, i, 0:512], tsum[tb2][:], [f'tsum{tb2}', f'hst{i}'], [f'hst{i}'])
                    self.cp('dve', sc[:], self.ps[5][:, :], ['ps5'], ['sc'])
                    self.tt('dve', sc[:], sc[:], self.ps[4][:, :], ALU.add, ['sc', 'ps4'], ['sc'])
                    self.act(sc[:], sc[:], AF.Sqrt, ['sc', 'eps'], ['sc'], bias=self.eps[:, 0:1])
                    self.recip(sc[:], sc[:], ['sc'], ['sc'])
                    hkeys = [f'hst{i}' for i in range(nf)]
                    for j in range(nf):
                        tb_ = j % 2
                        self._load_tab(tab[tb_], sname, j, nf, f'tab{tb_}')
                        for ab in range(2):
                            bank = 2 * (j % 2) + ab
                            for i in range(nf):
                                self.mm(self.ps[bank][:, :], tab[tb_][ab][:, i, :], hst[:, i, ab * 512:(ab + 1) * 512], i == 0, i == nf - 1,
                                        [f'tab{tb_}{ab}'] + hkeys, [f'ps{bank}'])
                            self.stt(Kh[:, j, ab, :], self.ps[bank][:, :], fscale[:, j:j + 1], sc[:], ALU.mult, ALU.mult, [f'ps{bank}', 'fscale', 'sc'], ['Kh'])
                    for i in range(nf):
                        self.mm(self.ps[6][0:1, :], self.altcol[:, 0:1], hst[:, i, 0:512], i == 0, i == nf - 1, ['altcol'] + hkeys, ['ps6'])
                    self.stt(knyq[:], self.ps[6][0:1, :], 1.0 / N, sc[0:1, :], ALU.mult, ALU.mult, ['ps6', 'sc'], ['knyq'])
                    self.P.flush()
                if self.halt_at(f'hyB_{l}'):
                    return
                with contextlib.ExitStack() as s3:
                    sb3 = lambda n, s, d: self.sb(s3, n, s, d)
                    vtm = sb3("vtmh", [128, nf, 512], BF16)
                    tab = [[sb3("tab", [128, nf, 128], BF16) for _ in range(2)] for _ in range(2)]
                    zt4 = [[sb3("zt4", [128, 512], F32) for _ in range(4)] for _ in range(2)]
                    if o == 0:
                        self.dma('sp', vT[:], self.hcd[:, 0:4, toff:toff + L], [], ['vT'])
                    for i in range(nf):
                        hb_ = 4 + (i % 2)
                        for c in range(4):
                            self.mm(self.ps[hb_][:, c * 128:(c + 1) * 128], vT[:, c, i * 128:(i + 1) * 128], self.identbf[:], True, True,
                                    ['vT', 'identbf'], [f'ps{hb_}'])
                        self.cp('act' if i % 2 == 0 else 'dve', vtm[:, i, :], self.ps[hb_][:, :], [f'ps{hb_}'], ['vtm'])
                    for j in range(nf):
                        tb_ = j % 2
                        self._load_tab(tab[tb_], sname, j, nf, f'tab{tb_}')
                        ba, bb = 2 * (j % 2), 2 * (j % 2) + 1
                        for ab, bank in ((0, ba), (1, bb)):
                            for i in range(nf):
                                self.mm(self.ps[bank][:, :], tab[tb_][ab][:, i, :], vtm[:, i, :], i == 0, i == nf - 1,
                                        [f'tab{tb_}{ab}', 'vtm'], [f'ps{bank}'])
                        z1, z2, z3, z4 = zt4[j % 2]
                        kz = j % 2
                        self.tt('dve', z1[:], self.ps[ba][:, :], Kh[:, j, 0, :], ALU.mult, [f'ps{ba}', 'Kh'], [f'z1{kz}'])
                        self.tt('dve', z2[:], self.ps[bb][:, :], Kh[:, j, 1, :], ALU.mult, [f'ps{bb}', 'Kh'], [f'z2{kz}'])
                        self.tt('pool', Zst[:, j, 0, :], z1[:], z2[:], ALU.subtract, [f'z1{kz}', f'z2{kz}'], ['Zst'])
                        self.tt('dve', z3[:], self.ps[ba][:, :], Kh[:, j, 1, :], ALU.mult, [f'ps{ba}', 'Kh'], [f'z3{kz}'])
                        self.tt('dve', z4[:], self.ps[bb][:, :], Kh[:, j, 0, :], ALU.mult, [f'ps{bb}', 'Kh'], [f'z4{kz}'])
                        self.tt('pool', Zst[:, j, 1, :], z3[:], z4[:], ALU.add, [f'z3{kz}', f'z4{kz}'], ['Zst'])
                    for i in range(nf):
                        self.mm(self.ps[6][0:1, :], self.altcol[:, 0:1], vtm[:, i, :], i == 0, i == nf - 1, ['altcol', 'vtm'], ['ps6'])
                    self.tt('dve', znyq[:], self.ps[6][0:1, :], knyq[:], ALU.mult, ['ps6', 'knyq'], ['znyq'])
                    self.dump(f'd_vtm_{sname}{o}', vtm[:], [128, nf, 512], BF16, ['vtm'])
                    self.dump(f'd_identbf_{sname}{o}', self.identbf[:], [128, 128], BF16, ['identbf'])
                    self.P.flush()
                if self.halt_at(f'hyC_{l}'):
                    return
                with contextlib.ExitStack() as s4:
                    sb4 = lambda n, s, d: self.sb(s4, n, s, d)
                    TB = 256
                    xg = sb4("xg", [128, 4, L], BF16)
                    rtab = [[sb4("rtab", [128, nf, TB], BF16) for _ in range(2)] for _ in range(2)]
                    gt_ = [sb4("gtmp", [128, TB], F32) for _ in range(2)]
                    self.dma('sp', xg[:], self.hcd[:, 4 + 4 * o:8 + 4 * o, toff:toff + L], [], ['xg'])
                    n = 0
                    for tb in range(L // TB):
                        rb = tb % 2
                        for ab, nm in ((0, 'A'), (1, 'B')):
                            if sname == 'lat':
                                src = K['k_' + nm + 'r'].rearrange("(j p) t -> p j t", p=128)[:, :, tb * TB:(tb + 1) * TB]
                            else:
                                src = K['k_' + nm + 'c'].rearrange("(j p) t -> p j t", p=128)[:, :, tb * TB:(tb + 1) * TB]
                            self.dma('sp', rtab[rb][ab][:], src, [], [f'rtab{rb}{ab}'])
                        for c in range(4):
                            bank = n % 4
                            gb_ = n % 2
                            n += 1
                            for j in range(nf):
                                for ab in range(2):
                                    self.mm(self.ps[bank][:, 0:TB], Zst[:, j, ab, c * 128:(c + 1) * 128], rtab[rb][ab][:, j, :], j == 0 and ab == 0, False,
                                            ['Zst', f'rtab{rb}{ab}'], [f'ps{bank}'])
                            self.mm(self.ps[bank][:, 0:TB], znyq[0:1, c * 128:(c + 1) * 128], self.altrow[0:1, tb * TB:(tb + 1) * TB], False, True,
                                    ['znyq', 'altrow'], [f'ps{bank}'])
                            tsl = slice(tb * TB, (tb + 1) * TB)
                            self.stt(gt_[gb_][:], vT[:, c, tsl], hbias[:, o, c:c + 1], self.ps[bank][:, 0:TB], ALU.mult, ALU.add,
                                     ['vT', 'hbias', f'ps{bank}'], [f'gt{gb_}'])
                            self.tt('pool', yT[:, c, tsl], gt_[gb_][:], xg[:, c, tsl], ALU.mult, [f'gt{gb_}', 'xg'], ['yT' if o == 0 else 'yT2'])
                    self.dump(f'd_hbias_{sname}{o}', hbias[:], [128, 2, 4], F32, ['hbias'])
                    self.dump(f'd_knyq_{sname}{o}', knyq[:], [1, 512], F32, ['knyq'])
                    self.dump(f'd_vT_{sname}{o}', vT[:], [128, 4, L], BF16, ['vT'])
                    self.dump(f'd_xg_{sname}{o}', xg[:], [128, 4, L], BF16, ['xg'])
                    self.dump(f'd_yT_{sname}{o}', yT[:], [128, 4, L], BF16, ['yT', 'yT2'])
                    self.dump(f'd_Kh_{sname}{o}', Kh[:], [128, nf, 2, 512], BF16, ['Kh'])
                    self.dump(f'd_Zst_{sname}{o}', Zst[:], [128, nf, 2, 512], BF16, ['Zst'])
                    if o == 1:
                        self.dma('sp', self.ybr_c[:, :, toff:toff + L], yT[:], ['yT2'], ['ybr_c'])
                    self.P.flush()


def _load_tab(self, tabs, sname, j, nf, key):
    K = self.K
    for ab, nm in ((0, 'A'), (1, 'B')):
        if sname == 'lat':
            src = K['k_' + nm + 't'][j]
        else:
            src = K['k_' + nm + 'c'].rearrange("(i p) (j q) -> j p i q", p=128, q=128)[j]
        self.dma('sp', tabs[ab][:], src, [], [key + str(ab)])


KB.phase_hy = phase_hy
KB._load_tab = _load_tab

def phase_merge(self, l):
    W = self.W
    need_ctx = (l == 0)
    tiles = TT if need_ctx else TT[:4]
    hi = NT if need_ctx else NL
    with contextlib.ExitStack() as st:
        sb = lambda n, s, d: self.sb(st, n, s, d)
        xnT = sb("xnT", [128, 8, NT], BF16)
        ya = sb("ya", [128, 4, NT], BF16)
        yc = sb("yc", [128, 4, NT], BF16)
        yb = sb("yb", [64, 8, NT], BF16)
        mg = sb("mg", [128, 8, NT], BF16)
        wg = [sb("wg", [128, 8, 3, 128], BF16) for _ in range(2)]
        wba = [sb("wba", [128, 4, 128], BF16) for _ in range(2)]
        wbc = [sb("wbc", [128, 4, 128], BF16) for _ in range(2)]
        wbb = [sb("wbb", [64, 8, 128], BF16) for _ in range(2)]
        sgt = [sb("sgm", [128, 512], F32) for _ in range(2)]
        acc = [sb("acc", [128, 512], F32) for _ in range(2)]
        tmpm = [sb("tmpm", [128, 512], F32) for _ in range(2)]
        for it_, (t0_, sz_, _c) in enumerate(tiles):
            tsl_ = slice(t0_, t0_ + sz_)
            self.dma('sp', xnT[:, :, tsl_], self.xnd[:, :, tsl_], [], [f'xnT_{it_}'])
            self.dma('sp', ya[:, :, tsl_], self.ybr_a[:, :, tsl_], [], [f'ya{it_}'])
            self.dma('sp', yb[:, :, tsl_], self.ybr_b[:, :, tsl_], [], [f'yb{it_}'])
            self.dma('sp', yc[:, :, tsl_], self.ybr_c[:, :, tsl_], [], [f'yc{it_}'])
        win = W['w_in'][l].rearrange("(k p) n -> p k n", p=128)
        wbr = W['w_branch'][l]
        n_ = 0
        m_ = 0
        def load_w(dt):
            b = dt % 2
            for n in range(3):
                c0 = 2816 + n * 1024 + dt * 128
                self.dma('pool', wg[b][:, :, n, :], win[:, :, c0:c0 + 128], [], [f'wg{b}{n}'])
            self.dma('pool', wba[b][:], wbr[0].rearrange("(c p) n -> p c n", p=128)[:, :, dt * 128:(dt + 1) * 128], [], [f'wba{b}'])
            self.dma('pool', wbc[b][:], wbr[2].rearrange("(c p) n -> p c n", p=128)[:, :, dt * 128:(dt + 1) * 128], [], [f'wbc{b}'])
            self.dma('pool', wbb[b][:], wbr[1].rearrange("(h j) n -> j h n", j=64)[:, :, dt * 128:(dt + 1) * 128], [], [f'wbb{b}'])

        load_w(0)
        for dt in range(8):
            b = dt % 2
            if dt + 1 < 8:
                load_w(dt + 1)
            for it, (t0, sz, col) in enumerate(tiles):
                ab = m_ % 2
                m_ += 1
                for n in range(3):
                    gbank = n_ % 2
                    pbank = 2 + (n_ % 2)
                    sb_ = n_ % 2
                    n_ += 1
                    for k in range(8):
                        self.mm(self.ps[gbank][:, 0:sz], wg[b][:, k, n, :], xnT[:, k, t0:t0 + sz], k == 0, k == 7, [f'wg{b}{n}', f'xnT_{it}'], [f'ps{gbank}'])
                    self.act(sgt[sb_][:, 0:sz], self.ps[gbank][:, 0:sz], AF.Sigmoid, [f'ps{gbank}'], [f'sgm{sb_}'])
                    if n == 0:
                        for c in range(4):
                            self.mm(self.ps[pbank][:, 0:sz], wba[b][:, c, :], ya[:, c, t0:t0 + sz], c == 0, c == 3, [f'wba{b}', f'ya{it}'], [f'ps{pbank}'])
                        self.tt('dve', acc[ab][:, 0:sz], self.ps[pbank][:, 0:sz], sgt[sb_][:, 0:sz], ALU.mult, [f'ps{pbank}', f'sgm{sb_}'], [f'acc{ab}'])
                    elif n == 1:
                        for h in range(8):
                            self.mm(self.ps[pbank][:, 0:sz], wbb[b][:, h, :], yb[:, h, t0:t0 + sz], h == 0, h == 7, [f'wbb{b}', f'yb{it}'], [f'ps{pbank}'])
                        self.tt('dve', tmpm[0][:, 0:sz], self.ps[pbank][:, 0:sz], sgt[sb_][:, 0:sz], ALU.mult, [f'ps{pbank}', f'sgm{sb_}'], ['tmpm0'])
                        self.tt('pool', acc[ab][:, 0:sz], acc[ab][:, 0:sz], tmpm[0][:, 0:sz], ALU.add, [f'acc{ab}', 'tmpm0'], [f'acc{ab}'])
                    else:
                        for c in range(4):
                            self.mm(self.ps[pbank][:, 0:sz], wbc[b][:, c, :], yc[:, c, t0:t0 + sz], c == 0, c == 3, [f'wbc{b}', f'yc{it}'], [f'ps{pbank}'])
                        self.tt('dve', tmpm[1][:, 0:sz], self.ps[pbank][:, 0:sz], sgt[sb_][:, 0:sz], ALU.mult, [f'ps{pbank}', f'sgm{sb_}'], ['tmpm1'])
                        self.tt('pool', mg[:, dt, t0:t0 + sz], acc[ab][:, 0:sz], tmpm[1][:, 0:sz], ALU.add, [f'acc{ab}', 'tmpm1'], [f'mg{dt}'])
        mgkeys = [f'mg{i}' for i in range(8)]
        wo = [sb("wo", [128, 8, 128], BF16) for _ in range(2)]
        NXO = 4
        xo = [sb("xo", [128, 512], F32) for _ in range(NXO)]
        wout = W['w_out'][l].rearrange("(k p) n -> p k n", p=128)
        n_ = 0
        ulist = [(dt, it) for dt in range(8) for it in range(len(tiles))]

        def xload(ui):
            dt_, it_ = ulist[ui]
            t0_, sz_, _c = tiles[it_]
            self.dma('sp', xo[ui % NXO][:, 0:sz_], self.xres[:, dt_, t0_:t0_ + sz_], [], [f'xo{ui % NXO}'])
        for ui in range(NXO - 1):
            xload(ui)
        for dt in range(8):
            b = dt % 2
            self.dma('pool', wo[b][:], wout[:, :, dt * 128:(dt + 1) * 128], [], [f'wo{b}'])
            for it, (t0, sz, col) in enumerate(tiles):
                bank = 4 + (n_ % 2)
                xb = n_ % NXO
                for k in range(8):
                    self.mm(self.ps[bank][:, 0:sz], wo[b][:, k, :], mg[:, k, t0:t0 + sz], k == 0, k == 7, [f'wo{b}'] + mgkeys, [f'ps{bank}'])
                self.stt(xo[xb][:, 0:sz], self.ps[bank][:, 0:sz], self.modT[:, 16 + dt, col:col + 1], xo[xb][:, 0:sz], ALU.mult, ALU.add,
                         [f'ps{bank}', f'xo{xb}', 'modT'], [f'xo{xb}'])
                if n_ + NXO - 1 < len(ulist):
                    xload(n_ + NXO - 1)
                self.dma('sp', self.xres[:, dt, t0:t0 + sz], xo[xb][:, 0:sz], [f'xo{xb}'], [f'xres{dt}_{it}'])
                n_ += 1
        self.P.flush()


KB.phase_merge = phase_merge
def build_nc(debug=(), stop_after=None):
    nc = bass.Bass("TRN2", target_bir_lowering=False)
    kb = KB(nc, debug=debug, stop_after=stop_after)
    kb.declare()

    def stop(tag):
        return stop_after == tag

    with kb.gst:
        kb.consts()
        kb.phase_input()
        for l in range(2):
            with contextlib.ExitStack() as lst:
                kb.phase_mod(l, lst)
                kb.phase_norm(l, 1, TT)
                if stop(f'norm1_{l}'):
                    break
                kb.phase_s5(l)
                if stop(f's5_{l}'):
                    break
                if hasattr(kb, 'phase_att'):
                    kb.phase_att(l)
                if stop(f'att_{l}'):
                    break
                if hasattr(kb, 'phase_hy'):
                    kb.phase_hy(l)
                if getattr(kb, 'halted', False):
                    break
                if stop(f'hy_{l}'):
                    break
                if hasattr(kb, 'phase_merge'):
                    kb.phase_merge(l)
                if stop(f'merge_{l}'):
                    break
                if hasattr(kb, 'phase_ffn'):
                    kb.phase_norm(l, 2, TT if l == 0 else TT[:4], router=(l == 1))
                    kb.phase_ffn(l)
                if stop(f'ffn_{l}'):
                    break
        else:
            if hasattr(kb, 'phase_output'):
                kb.phase_output()
        kb.P.finish()
    return nc


def make_in_maps(inputs, cores):
    hc = host_constants()
    maps = []
    for b in cores:
        m = {}
        m['x'] = np.ascontiguousarray(inputs['x'][b])
        m['c'] = np.ascontiguousarray(inputs['c'][b])
        m['ctx'] = np.ascontiguousarray(inputs['ctx'][b])
        m['c_ctx'] = np.ascontiguousarray(inputs['c_ctx'])
        for k, v in inputs.items():
            if k not in ('x', 'c', 'ctx', 'c_ctx'):
                m[k] = np.ascontiguousarray(v)
        m.update(hc)
        maps.append(m)
    return maps


_NC_CACHE = {}


def kernel(**inputs):
    inputs = {k: np.asarray(v) for k, v in inputs.items()}
    if 'nc' not in _NC_CACHE:
        _NC_CACHE['nc'] = build_nc()
    nc = _NC_CACHE['nc']
    cores = list(range(8))
    maps = make_in_maps(inputs, cores)
    res = run_bass_kernel_spmd(nc, maps, core_ids=cores)
    out = np.stack([np.asarray(res.results[i]['out'], dtype=np.float32) for i in range(8)], axis=0)
    return out
```

```python
import math
import contextlib
import numpy as np
import ml_dtypes
import concourse.bass as bass
import concourse.mybir as mybir
from concourse.bass_utils import run_bass_kernel_spmd

F32 = mybir.dt.float32
BF16 = mybir.dt.bfloat16
I32 = mybir.dt.int32
AF = mybir.ActivationFunctionType
ALU = mybir.AluOpType
AX = mybir.AxisListType

D = 1024
NL = 2048
NCX = 256
NT = NL + NCX
TT = [(0, 512, 0), (512, 512, 0), (1024, 512, 0), (1536, 512, 0), (2048, 256, 1)]
IN_W = 5888
TWO_PI = 2.0 * math.pi
CH = 256


class Prog:
    ENG = ('pe', 'act', 'dve', 'pool', 'sp')
    DMAENG = ('pool', 'sp')
    NSLOT = 20
    ENGOBJ = {'pe': 'tensor', 'act': 'scalar', 'dve': 'vector', 'pool': 'gpsimd', 'sp': 'sync'}

    def __init__(self, nc, stack):
        self.nc = nc
        self.csem = {e: stack.enter_context(nc.semaphore('c_' + e)) for e in self.ENG}
        self.dsem = {e: [stack.enter_context(nc.semaphore(f'd_{e}{i}')) for i in range(self.NSLOT)] for e in self.DMAENG}
        self.ctot = {e: 0 for e in self.ENG}
        self.dtot = {e: [0] * self.NSLOT for e in self.DMAENG}
        self.dcount = {e: 0 for e in self.DMAENG}
        self.nflush = 0
        self.count = 0
        self._reset()

    def _reset(self):
        self.ins = {e: [] for e in self.ENG}
        self.lastw = {}
        self.readers = {}

    def op(self, eng, fn, reads=(), writes=(), dma=False):
        deps = set()
        for k in reads:
            w = self.lastw.get(k)
            if w is not None:
                deps.add(w)
        for k in writes:
            w = self.lastw.get(k)
            if w is not None:
                deps.add(w)
            for r in self.readers.get(k, ()):
                deps.add(r)
        idx = len(self.ins[eng])
        me = (eng, idx)
        if eng == 'pe':
            deps = {d for d in deps if d[0] != 'pe'}
        deps.discard(me)
        it = dict(fn=fn, deps=deps, dma=dma, sig=dma)
        if dma:
            assert eng in self.DMAENG
            slot = self.dcount[eng] % self.NSLOT
            self.dcount[eng] += 1
            it['slot'] = slot
            it['prev'] = self.dtot[eng][slot]
            self.dtot[eng][slot] += 16
            it['val'] = self.dtot[eng][slot]
        self.ins[eng].append(it)
        for k in reads:
            self.readers.setdefault(k, []).append(me)
        for k in writes:
            self.lastw[k] = me
            self.readers[k] = []
        self.count += 1
        return me

    def flush(self):
        nc = self.nc
        ins = self.ins
        for e in self.ENG:
            for it in ins[e]:
                for (e2, i2) in it['deps']:
                    ins[e2][i2]['sig'] = True
            for it in reversed(ins[e]):
                if not it['dma']:
                    it['sig'] = True
                    break
        start_c = dict(self.ctot)
        start_d = getattr(self, '_dprev', {e: [0] * self.NSLOT for e in self.DMAENG})
        for e in self.ENG:
            cc = self.ctot[e]
            for it in ins[e]:
                if it['sig'] and not it['dma']:
                    cc += 1
                    it['val'] = cc
            self.ctot[e] = cc
        self._dprev = {e: list(self.dtot[e]) for e in self.DMAENG}
        csem, dsem = self.csem, self.dsem
        first = self.nflush == 0
        self.nflush += 1
        with nc.Block() as block:
            def emit(e):
                def body(eng):
                    waited = {}
                    if not first:
                        for e2 in self.ENG:
                            if start_c[e2] > 0:
                                eng.wait_ge(csem[e2], start_c[e2])
                        for e2 in self.DMAENG:
                            for s in range(self.NSLOT):
                                if start_d[e2][s] > 0:
                                    eng.wait_ge(dsem[e2][s], start_d[e2][s])
                                    waited[(e2, s)] = start_d[e2][s]
                    for it in ins[e]:
                        need = {}
                        for (e2, i2) in it['deps']:
                            src = ins[e2][i2]
                            key = (e2, src['slot']) if src['dma'] else (e2, None)
                            need[key] = max(need.get(key, 0), src['val'])
                        if it['dma'] and it['prev'] > 0:
                            key = (e, it['slot'])
                            need[key] = max(need.get(key, 0), it['prev'])
                        for key, v in need.items():
                            if waited.get(key, 0) < v:
                                sem = csem[key[0]] if key[1] is None else dsem[key[0]][key[1]]
                                eng.wait_ge(sem, v)
                                waited[key] = v
                        r = it['fn'](eng)
                        if it['dma']:
                            r.then_inc(dsem[e][it['slot']], 16)
                        elif it['sig']:
                            r.then_inc(csem[e], 1)
                return body
            for e in self.ENG:
                getattr(block, self.ENGOBJ[e])(emit(e))
        self._reset()

    def finish(self):
        nc = self.nc
        with nc.Block() as block:
            def body(eng):
                for e2 in self.ENG:
                    if self.ctot[e2] > 0:
                        eng.wait_ge(self.csem[e2], self.ctot[e2])
                for e2 in self.DMAENG:
                    for s in range(self.NSLOT):
                        if self.dtot[e2][s] > 0:
                            eng.wait_ge(self.dsem[e2][s], self.dtot[e2][s])
            block.sync(body)


def _bf(a):
    return np.ascontiguousarray(a.astype(ml_dtypes.bfloat16))


_CONST_CACHE = {}


def host_constants():
    if _CONST_CACHE:
        return _CONST_CACHE
    c = {}
    c['k_ident32'] = np.eye(128, dtype=np.float32)
    c['k_identbf'] = _bf(np.eye(128, dtype=np.float32))
    pos = np.arange(NL)
    row = (pos // 64).astype(np.float64)
    col = (pos % 64).astype(np.float64)
    inv = 10000.0 ** (-2.0 * np.arange(16, dtype=np.float64) / 32.0)
    ang = np.concatenate([row[None, :] * inv[:, None], col[None, :] * inv[:, None]], axis=0)
    ang = ang.astype(np.float32).astype(np.float64)
    cos2 = np.concatenate([np.cos(ang), np.cos(ang)], axis=0)
    sin2 = np.concatenate([-np.sin(ang), np.sin(ang)], axis=0)
    c['k_cos2'] = cos2.astype(np.float32)
    c['k_sin2'] = sin2.astype(np.float32)
    sperm = np.zeros((64, 64), np.float32)
    for m in range(64):
        sperm[(m + 32) % 64, m] = 1.0
    c['k_sperm'] = _bf(sperm)
    j = np.arange(128)[:, None]
    i = np.arange(128)[None, :]
    mprev = (j >= i).astype(np.float32)
    mnext = (j <= i).astype(np.float32)
    c['k_mprev'] = _bf(np.tile(mprev, (1, 4)))
    c['k_mnext'] = _bf(np.tile(mnext, (1, 4)))
    t = np.arange(2048, dtype=np.float64)
    ph = 2.0 * np.pi * np.outer(t, t) / 4096.0
    A = np.cos(ph)
    B = np.sin(ph)
    c['k_Ar'] = _bf(A)
    c['k_Br'] = _bf(B)
    c['k_At'] = _bf(A.reshape(16, 128, 16, 128).transpose(2, 1, 0, 3))
    c['k_Bt'] = _bf(B.reshape(16, 128, 16, 128).transpose(2, 1, 0, 3))
    tc_ = np.arange(256, dtype=np.float64)
    phc = 2.0 * np.pi * np.outer(tc_, tc_) / 512.0
    c['k_Ac'] = _bf(np.cos(phc))
    c['k_Bc'] = _bf(np.sin(phc))
    alt = (1.0 - 2.0 * (np.arange(2048) % 2)).astype(np.float32)
    c['k_altrow'] = _bf(alt[None, :])
    c['k_altcol'] = _bf(alt[:128, None])
    fs = np.full((128, 16), 2.0 / 4096.0, np.float32)
    fs[0, 0] = 1.0 / 4096.0
    c['k_fscale'] = fs
    fsc = np.full((128, 2), 2.0 / 512.0, np.float32)
    fsc[0, 0] = 1.0 / 512.0
    c['k_fscale_c'] = fsc

    def feats(L):
        tt = np.arange(L, dtype=np.float32) / np.float32(L)
        bands = np.arange(1, 17, dtype=np.float32)
        arg = (2.0 * math.pi) * tt[:, None] * bands[None, :]
        f = np.concatenate([tt[:, None], np.sin(arg), np.cos(arg)], axis=-1).astype(np.float32)
        return np.ascontiguousarray(f.T), tt
    fl, tl = feats(2048)
    fc, tcx = feats(256)
    c['k_featsT'] = fl
    c['k_featsT_c'] = fc
    c['k_negt'] = np.ascontiguousarray((-tl).reshape(16, 128).T)
    c['k_negt_c'] = np.ascontiguousarray((-tcx).reshape(2, 128).T)
    p = np.arange(128)
    c['k_m01'] = np.stack([((p // 16) % 2 == 0), ((p // 16) % 2 == 1)], axis=1).astype(np.float32)
    c['k_tau'] = np.tile(np.arange(CH, dtype=np.float32)[None, :], (128, 1))
    _CONST_CACHE.update(c)
    return c


CONST_SPECS = None


class KB:
    def __init__(self, nc, debug=(), nlayers=2, stop_after=None):
        self.nc = nc
        self.debug = set(debug)
        self.nlayers = nlayers
        self.stop_after = stop_after
        self.gst = contextlib.ExitStack()
        self.P = Prog(nc, self.gst)
        self.uid = 0

    def halt_at(self, tag):
        if self.stop_after == tag:
            self.halted = True
        return getattr(self, 'halted', False)

    def dump(self, name, ap, shape, dt, keys):
        if name not in self.debug:
            return
        t = self.nc.dram_tensor(name, list(shape), dt, kind="ExternalOutput").ap()
        self.dma('sp', t, ap, keys, [name])

    def din(self, name, shape, dt=F32):
        return self.nc.dram_tensor(name, list(shape), dt, kind="ExternalInput").ap()

    def dscr(self, name, shape, dt):
        kind = "ExternalOutput" if name in self.debug else "Internal"
        return self.nc.dram_tensor(name, list(shape), dt, kind=kind).ap()

    def sb(self, st, name, shape, dt):
        self.uid += 1
        return st.enter_context(self.nc.sbuf_tensor(f"{name}_{self.uid}", list(shape), dt))

    def mm(self, out, lhsT, rhs, start, stop, r, w, tp=None):
        kw = {}
        if tp is not None:
            kw['tile_position'] = tp
            kw['skip_group_check'] = True
        self.P.op('pe', lambda e: e.matmul(out, lhsT=lhsT, rhs=rhs, start=start, stop=stop, **kw), r, w)

    def tr(self, out, in_, ident, r, w):
        self.P.op('pe', lambda e: e.transpose(out, in_, ident), r, w)

    def act(self, out, in_, func, r, w, scale=None, bias=None):
        kw = {}
        if scale is not None:
            kw['scale'] = scale
        if bias is not None:
            kw['bias'] = bias
        self.P.op('act', lambda e: e.activation(out=out, in_=in_, func=func, **kw), r, w)

    def tt(self, eng, out, in0, in1, op, r, w):
        self.P.op(eng, lambda e: e.tensor_tensor(out=out, in0=in0, in1=in1, op=op), r, w)

    def ts(self, eng, out, in0, s1, op0, r, w, s2=None, op1=None):
        if op1 is None:
            self.P.op(eng, lambda e: e.tensor_scalar(out=out, in0=in0, scalar1=s1, scalar2=None, op0=op0), r, w)
        else:
            self.P.op(eng, lambda e: e.tensor_scalar(out=out, in0=in0, scalar1=s1, scalar2=s2, op0=op0, op1=op1), r, w)

    def stt(self, out, in0, scalar, in1, op0, op1, r, w):
        self.P.op('dve', lambda e: e.scalar_tensor_tensor(out=out, in0=in0, scalar=scalar, in1=in1, op0=op0, op1=op1), r, w)

    def cp(self, eng, out, in_, r, w):
        if eng == 'act':
            self.P.op('act', lambda e: e.activation(out=out, in_=in_, func=AF.Copy), r, w)
        else:
            self.P.op(eng, lambda e: e.tensor_copy(out=out, in_=in_), r, w)

    def recip(self, out, in_, r, w):
        self.P.op('dve', lambda e: e.reciprocal(out=out, in_=in_), r, w)

    def memset(self, eng, ap, val, w):
        self.P.op(eng, lambda e: e.memset(ap, val), (), w)

    def dma(self, eng, out, in_, r, w, slow=False):
        if slow:
            self.P.op(eng, lambda e: e.dma_start(out=out, in_=in_, allow_slow_non_contiguous=True), r, w, dma=True)
        else:
            self.P.op(eng, lambda e: e.dma_start(out=out, in_=in_), r, w, dma=True)

    def load_xnT(self, xnT, ntok):
        for it_, (t0_, sz_, _c) in enumerate(TT):
            if t0_ >= ntok:
                break
            self.dma('sp', xnT[:, :, t0_:t0_ + sz_], self.xnd[:, :, t0_:t0_ + sz_], [], [f'xnT_{it_}'])

    def scan(self, out, d0, d1, r, w):
        self.P.op('dve', lambda e: e.tensor_tensor_scan(out=out, data0=d0, data1=d1, initial=0.0, op0=ALU.mult, op1=ALU.add), r, w)

    def sin_rr(self, out, x, add, ti, tf, key, r, w, eng='dve'):
        P_ = self.P
        self.ts(eng, ti, x, float(add), ALU.add, r, [key + 'i'], s2=1.0 / TWO_PI, op1=ALU.mult)
        self.cp(eng, tf, ti, [key + 'i'], [key + 'f'])
        if eng == 'dve':
            self.stt(tf, tf, -TWO_PI, x, ALU.mult, ALU.add, list(r) + [key + 'f'], [key + 'f'])
        else:
            self.ts(eng, tf, tf, -TWO_PI, ALU.mult, [key + 'f'], [key + 'f'])
            self.tt(eng, tf, tf, x, ALU.add, list(r) + [key + 'f'], [key + 'f'])
        if add != 0.0:
            self.ts(eng, tf, tf, float(add), ALU.add, [key + 'f'], [key + 'f'])
        self.ts(eng, tf, tf, -3.1415925, ALU.max, [key + 'f'], [key + 'f'], s2=3.1415925, op1=ALU.min)
        self.act(out, tf, AF.Sin, [key + 'f'], w)

    def declare(self):
        nc = self.nc
        self.x = self.din("x", [NL, D])
        self.c = self.din("c", [D])
        self.ctx = self.din("ctx", [NCX, D])
        self.c_ctx = self.din("c_ctx", [D])
        shapes = dict(
            mod_w=(2, 1024, 6144), mod_b=(2, 6144), norm1_g=(2, 1024), norm2_g=(2, 1024), w_in=(2, 1024, 5888),
            s5_lam_re=(2, 2, 32, 64), s5_lam_im=(2, 2, 32, 64), s5_log_dt=(2, 2, 32), s5_b_re=(2, 2, 32, 64, 16),
            s5_b_im=(2, 2, 32, 64, 16), s5_c_re=(2, 2, 32, 16, 64), s5_c_im=(2, 2, 32, 16, 64), s5_d=(2, 512),
            s5_glu_w=(2, 512, 512), s5_glu_b=(2, 512), q_norm_g=(2, 64), k_norm_g=(2, 64), attn_sink=(2, 8),
            hy_conv_w=(2, 3, 1536), hy_conv_b=(2, 1536), hy_filt_w1=(2, 33, 64), hy_filt_b1=(2, 64),
            hy_filt_w2=(2, 64, 64), hy_filt_b2=(2, 64), hy_filt_w3=(2, 64, 2048), hy_filt_b3=(2, 2048),
            hy_filt_freq=(2, 64), hy_filt_decay=(2, 2048), hy_bias=(2, 2, 512), w_branch=(2, 3, 512, 1024),
            w_out=(2, 1024, 1024), ffn_w1=(1, 1024, 2816), ffn_w3=(1, 1024, 2816), ffn_w2=(1, 2816, 1024),
            router_w=(1, 1024, 8), moe_w1=(1, 8, 1024, 3584), moe_w3=(1, 8, 1024, 3584), moe_w2=(1, 8, 3584, 1024))
        self.W = {k: self.din(k, v) for k, v in shapes.items()}
        hc = host_constants()
        self.K = {}
        for k, v in hc.items():
            dt = BF16 if v.dtype == ml_dtypes.bfloat16 else F32
            self.K[k] = self.din(k, v.shape, dt)
        self.out = nc.dram_tensor("out", [NL, D], F32, kind="ExternalOutput").ap()
        self.xres = self.dscr("xres", [128, 8, NT], F32)
        self.ybr_a = self.dscr("ybr_a", [128, 4, NT], BF16)
        self.ybr_b = self.dscr("ybr_b", [64, 8, NT], BF16)
        self.ybr_c = self.dscr("ybr_c", [128, 4, NT], BF16)
        self.hcd = self.dscr("hcd", [128, 12, NT], BF16)
        self.combd = self.dscr("combd", [8, NL], F32)
        self.xnd = self.dscr("xnd", [128, 8, NT], BF16)

    def consts(self):
        g = self.gst
        K = self.K
        self.ident32 = self.sb(g, "ident32", [128, 128], F32)
        self.identbf = self.sb(g, "identbf", [128, 128], BF16)
        self.ones_bf = self.sb(g, "ones_bf", [128, 128], BF16)
        self.ones32 = self.sb(g, "ones32", [128, 128], F32)
        self.eps = self.sb(g, "eps", [128, 1], F32)
        self.m01 = self.sb(g, "m01", [128, 2], F32)
        self.altcol = self.sb(g, "altcol", [128, 1], BF16)
        self.altrow = self.sb(g, "altrow", [1, 2048], BF16)
        self.dma('sp', self.ident32[:], K['k_ident32'][:, :], [], ['ident32'])
        self.dma('sp', self.identbf[:], K['k_identbf'][:, :], [], ['identbf'])
        self.dma('sp', self.m01[:], K['k_m01'][:, :], [], ['m01'])
        self.dma('sp', self.altcol[:], K['k_altcol'][:, :], [], ['altcol'])
        self.dma('sp', self.altrow[:], K['k_altrow'][:, :], [], ['altrow'])
        self.memset('dve', self.ones_bf[:], 1.0, ['ones_bf'])
        self.memset('dve', self.ones32[:], 1.0, ['ones32'])
        self.memset('dve', self.eps[:], 1e-6, ['eps'])
        self.ps = [g.enter_context(self.nc.psum_tensor(f"ps{i}", [128, 512], F32)) for i in range(7)]
        self.psb = g.enter_context(self.nc.psum_tensor("psb", [128, 1024], BF16))
        self.P.flush()

    def phase_input(self):
        with contextlib.ExitStack() as st:
            xin = [self.sb(st, "xin", [128, 1024], F32) for _ in range(3)]
            xo = [self.sb(st, "xo", [128, 8, 512], F32) for _ in range(2)]
            for ti in range(18):
                b = ti % 3
                g4, j4 = ti // 4, ti % 4
                ob = g4 % 2
                src = self.x[ti * 128:(ti + 1) * 128, :] if ti < 16 else self.ctx[(ti - 16) * 128:(ti - 15) * 128, :]
                self.dma('sp', xin[b][:], src, [], [f'xin{b}'])
                for k in range(8):
                    bank = 2 * (ti % 2) + k // 4
                    self.tr(self.ps[bank][:, (k % 4) * 128:(k % 4 + 1) * 128], xin[b][:, k * 128:(k + 1) * 128], self.ident32[:],
                            [f'xin{b}', 'ident32'], [f'ps{bank}'])
                for h_ in range(2):
                    bank = 2 * (ti % 2) + h_
                    self.cp('act' if h_ == 0 else 'dve', xo[ob][:, h_ * 4:(h_ + 1) * 4, j4 * 128:(j4 + 1) * 128],
                            self.ps[bank][:, :].rearrange("p (a b) -> p a b", a=4), [f'ps{bank}'], [f'xo{ob}{h_}'])
                if j4 == 3 or ti == 17:
                    t0 = g4 * 512
                    w_ = (j4 + 1) * 128
                    self.dma('sp', self.xres[:, :, t0:t0 + w_], xo[ob][:, :, 0:w_], [f'xo{ob}0', f'xo{ob}1'], [f'xres{g4}'])
            self.P.flush()

    def phase_mod(self, l, lst):
        W = self.W
        self.modT = self.sb(lst, "modT", [128, 48, 2], F32)
        self.gm1 = self.sb(lst, "gm1", [128, 8, 2], F32)
        self.gm2 = self.sb(lst, "gm2", [128, 8, 2], F32)
        with contextlib.ExitStack() as st:
            cT = self.sb(st, "cT", [128, 2, 8], F32)
            scb = self.sb(st, "scb", [128, 8, 2], BF16)
            mb = self.sb(st, "mb", [128, 48], F32)
            ng = self.sb(st, "ng", [128, 2, 8], F32)
            tmp = self.sb(st, "modtmp", [128, 8, 2], F32)
            wblk = [self.sb(st, "mw", [128, 8, 512], BF16) for _ in range(2)]
            self.dma('sp', cT[:, 0, :], self.c.rearrange("(k p) -> p k", p=128), [], ['cT'], slow=True)
            self.dma('sp', cT[:, 1, :], self.c_ctx.rearrange("(k p) -> p k", p=128), [], ['cT'], slow=True)
            self.dma('sp', mb[:], W['mod_b'][l].rearrange("(j p) -> p j", p=128), [], ['mb'], slow=True)
            self.dma('sp', ng[:, 0, :], W['norm1_g'][l].rearrange("(k p) -> p k", p=128), [], ['ng'], slow=True)
            self.dma('sp', ng[:, 1, :], W['norm2_g'][l].rearrange("(k p) -> p k", p=128), [], ['ng'], slow=True)
            self.act(scb[:].rearrange("p k w -> p w k"), cT[:], AF.Silu, ['cT'], ['scb'])
            mw = W['mod_w'][l].rearrange("(k p) n -> p k n", p=128)
            for blk in range(12):
                b = blk % 2
                self.dma('pool', wblk[b][:], mw[:, :, blk * 512:(blk + 1) * 512], [], [f'mw{b}'])
                for jj in range(4):
                    j = blk * 4 + jj
                    for k in range(8):
                        self.mm(self.ps[0][:, j * 2:(j + 1) * 2], wblk[b][:, k, jj * 128:(jj + 1) * 128], scb[:, k, :],
                                k == 0, k == 7, [f'mw{b}', 'scb'], ['ps0'])
            self.tt('dve', self.modT[:], self.ps[0][:, 0:96].rearrange("p (j w) -> p j w", w=2),
                    mb[:].unsqueeze(2).to_broadcast([128, 48, 2]), ALU.add, ['ps0', 'mb'], ['modT'])
            for (gm, sidx, gi) in ((self.gm1, 1, 0), (self.gm2, 4, 1)):
                self.ts('dve', tmp[:], self.modT[:, sidx * 8:(sidx + 1) * 8, :], 1.0, ALU.add, ['modT'], ['modtmp'])
                self.tt('dve', gm[:], tmp[:], ng[:, gi, :].unsqueeze(2).to_broadcast([128, 8, 2]), ALU.mult,
                        ['modtmp', 'ng'], ['gm'])
            self.P.flush()

    def phase_norm(self, l, which, tiles, router=False):
        W = self.W
        sh_idx = 0 if which == 1 else 3
        gm = self.gm1 if which == 1 else self.gm2
        with contextlib.ExitStack() as st:
            xnT = self.sb(st, "xnT", [128, 8, NT], BF16)
            NB = 4
            pbank = [0, 1, 4, 5]
            xt = [self.sb(st, "xt", [128, 8, 512], F32) for _ in range(NB)]
            sq = [self.sb(st, "sq", [128, 8, 512], BF16) for _ in range(NB)]
            rs = [self.sb(st, "rs", [128, 512], F32) for _ in range(NB)]
            if router:
                rw = self.sb(st, "rw", [128, 8, 8], F32)
                lg = self.sb(st, "lg", [128, 4, 8], F32)
                mx = self.sb(st, "mx", [128, 4, 8], F32)
                ee = self.sb(st, "ee", [128, 4, 8], F32)
                mk = self.sb(st, "mk", [128, 4, 8], F32)
                ssum = self.sb(st, "ssum", [128, 4], F32)
                combT = self.sb(st, "combT", [8, NL], F32)
                self.dma('sp', rw[:], W['router_w'][0].rearrange("(k p) e -> p k e", p=128), [], ['rw'])
            def load(it_):
                t0_, sz_, _c = tiles[it_]
                b_ = it_ % NB
                self.dma('sp', xt[b_][:, :, 0:sz_], self.xres[:, :, t0_:t0_ + sz_], ['xres'], [f'xt{b_}'])
            for it_ in range(min(NB, len(tiles))):
                load(it_)
            pending = None

            def sumsq(it_):
                t0_, sz_, _c = tiles[it_]
                b_ = it_ % NB
                pb_ = pbank[b_]
                self.act(sq[b_][:, :, 0:sz_], xt[b_][:, :, 0:sz_], AF.Square, [f'xt{b_}'], [f'sq{b_}'])
                for k in range(8):
                    self.mm(self.ps[pb_][:, 0:sz_], self.ones_bf[:], sq[b_][:, k, 0:sz_], k == 0, k == 7, [f'sq{b_}', 'ones_bf'], [f'ps{pb_}'])
            sumsq(0)
            for it, (t0, sz, col) in enumerate(tiles):
                b = it % NB
                pb = pbank[b]
                if it + 1 < len(tiles):
                    sumsq(it + 1)
                self.act(rs[b][:, 0:sz], self.ps[pb][:, 0:sz], AF.Sqrt, [f'ps{pb}', 'eps'], [f'rs{b}'], scale=1.0 / D, bias=self.eps[:, 0:1])
                self.recip(rs[b][:, 0:sz], rs[b][:, 0:sz], [f'rs{b}'], [f'rs{b}'])
                self.tt('dve', xt[b][:, :, 0:sz], xt[b][:, :, 0:sz], rs[b][:, 0:sz].unsqueeze(1).to_broadcast([128, 8, sz]), ALU.mult,
                        [f'xt{b}', f'rs{b}'], [f'xt{b}'])
                for k in range(8):
                    if router:
                        self.act(xt[b][:, k, 0:sz], xt[b][:, k, 0:sz], AF.Identity, [f'xt{b}', 'gm', 'modT'], [f'xt{b}'],
                                 scale=gm[:, k, col:col + 1], bias=self.modT[:, sh_idx * 8 + k, col:col + 1])
                        self.cp('pool', xnT[:, k, t0:t0 + sz], xt[b][:, k, 0:sz], [f'xt{b}'], [f'xnT{it}'])
                    else:
                        self.act(xnT[:, k, t0:t0 + sz], xt[b][:, k, 0:sz], AF.Identity, [f'xt{b}', 'gm', 'modT'], [f'xnT{it}'],
                                 scale=gm[:, k, col:col + 1], bias=self.modT[:, sh_idx * 8 + k, col:col + 1])
                if it + NB < len(tiles):
                    load(it + NB)
                self.dma('sp', self.xnd[:, :, t0:t0 + sz], xnT[:, :, t0:t0 + sz], [f'xnT{it}'], [f'xnd{it}'])
                if router:
                    def router_part(b=b, t0=t0, sz=sz):
                        for s in range(sz // 128):
                            for k in range(8):
                                self.mm(self.ps[2][:, s * 8:(s + 1) * 8], xt[b][:, k, s * 128:(s + 1) * 128], rw[:, k, :], k == 0, k == 7,
                                        [f'xt{b}', 'rw'], ['ps2'])
                        ns = sz // 128
                        self.cp('dve', lg[:, 0:ns, :], self.ps[2][:, 0:ns * 8].rearrange("p (s e) -> p s e", e=8), ['ps2'], ['lg'])
                        for s in range(ns):
                            self.P.op('dve', (lambda o, i_: (lambda e: e.max(out=o, in_=i_)))(mx[:, s, :], lg[:, s, :]), ['lg'], ['mx'])
                        self.tt('dve', ee[:, 0:ns, :], lg[:, 0:ns, :], mx[:, 0:ns, 0:1].to_broadcast([128, ns, 8]), ALU.subtract, ['lg', 'mx'], ['ee'])
                        self.act(ee[:, 0:ns, :], ee[:, 0:ns, :], AF.Exp, ['ee'], ['ee'])
                        self.tt('dve', mk[:, 0:ns, :], lg[:, 0:ns, :], mx[:, 0:ns, 1:2].to_broadcast([128, ns, 8]), ALU.is_ge, ['lg', 'mx'], ['mk'])
                        self.tt('dve', ee[:, 0:ns, :], ee[:, 0:ns, :], mk[:, 0:ns, :], ALU.mult, ['ee', 'mk'], ['ee'])
                        self.P.op('dve', (lambda o, i_: (lambda e: e.tensor_reduce(out=o, in_=i_, axis=AX.X, op=ALU.add)))(ssum[:, 0:ns], ee[:, 0:ns, :]),
                                  ['ee'], ['ssum'])
                        self.recip(ssum[:, 0:ns], ssum[:, 0:ns], ['ssum'], ['ssum'])
                        self.tt('dve', ee[:, 0:ns, :], ee[:, 0:ns, :], ssum[:, 0:ns].unsqueeze(2).to_broadcast([128, ns, 8]), ALU.mult,
                                ['ee', 'ssum'], ['ee'])
                        for s in range(ns):
                            self.tr(self.ps[3][0:8, s * 128:(s + 1) * 128], ee[:, s, :], self.ident32[:], ['ee', 'ident32'], ['ps3'])
                        self.cp('dve', combT[:, t0:t0 + sz], self.ps[3][0:8, 0:sz], ['ps3'], ['combT'])

                    if pending is not None:
                        pending()
                    pending = router_part
            if router:
                pending()
                self.dma('sp', self.combd[:, :], combT[:], ['combT'], ['combd'])
            self.P.flush()


def phase_s5(self, l):
    W = self.W
    K = self.K
    PI = math.pi
    with contextlib.ExitStack() as st:
        sb = lambda n, s, d: self.sb(st, n, s, d)
        uT = sb("uT", [128, 4, NT], BF16)
        N2 = sb("N2", [128, 5, 32], F32)
        cosT = sb("cosT", [128, 32, CH], BF16)
        sinT = sb("sinT", [128, 32, CH], BF16)
        BT = sb("BT", [128, 2, 2, 4, 128], BF16)
        CT = sb("CT", [128, 3, 2, 16, 32], BF16)
        dvec = sb("dvec", [128, 4], F32)
        gb = sb("gb", [128, 4], F32)
        with contextlib.ExitStack() as s2:
            sb2 = lambda n, s, d: self.sb(s2, n, s, d)
            xnT = sb2("xnT", [128, 8, NT], BF16)
            wblk = sb2("ws5", [128, 8, 512], BF16)
            self.load_xnT(xnT, NT)
            self.dma('pool', wblk[:], W['w_in'][l].rearrange("(k p) n -> p k n", p=128)[:, :, 0:512], [], ['ws5'])
            for it, (t0, sz, col) in enumerate(TT):
                for c4 in range(4):
                    bank = c4
                    for k in range(8):
                        self.mm(self.ps[bank][:, 0:sz], wblk[:, k, c4 * 128:(c4 + 1) * 128], xnT[:, k, t0:t0 + sz], k == 0, k == 7,
                                ['ws5', f'xnT_{it}'], [f'ps{bank}'])
                    self.cp('act' if c4 % 2 == 0 else 'dve', uT[:, c4, t0:t0 + sz], self.ps[bank][:, 0:sz], [f'ps{bank}'], [f'uT{c4}'])
            self.P.flush()
        with contextlib.ExitStack() as s2:
            sb2 = lambda n, s, d: self.sb(s2, n, s, d)
            self.dma('sp', dvec[:], W['s5_d'][l].rearrange("(c p) -> p c", p=128), [], ['dvec'], slow=True)
            self.dma('sp', gb[:], W['s5_glu_b'][l].rearrange("(c p) -> p c", p=128), [], ['gb'], slow=True)
            lre = sb2("lre", [64, 64], F32)
            lim = sb2("lim", [64, 64], F32)
            ldt = sb2("ldt", [64, 1], F32)
            self.dma('sp', lre[:], W['s5_lam_re'][l].rearrange("d g p -> (d g) p"), [], ['lre'])
            self.dma('sp', lim[:], W['s5_lam_im'][l].rearrange("d g p -> (d g) p"), [], ['lim'])
            self.dma('sp', ldt[:], W['s5_log_dt'][l].rearrange("d (g o) -> (d g) o", o=1), [], ['ldt'])
            dt = sb2("dt", [64, 1], F32)
            xr = sb2("xr", [64, 64], F32)
            xi = sb2("xi", [64, 64], F32)
            mag = sb2("mag", [64, 64], F32)
            sn = sb2("sn", [64, 64], F32)
            cs = sb2("cs", [64, 64], F32)
            ti = sb2("ti", [64, 64], I32)
            tf = sb2("tf", [64, 64], F32)
            t1 = sb2("t1", [64, 64], F32)
            t2 = sb2("t2", [64, 64], F32)
            abre = sb2("abre", [64, 64], F32)
            abim = sb2("abim", [64, 64], F32)
            rden = sb2("rden", [64, 64], F32)
            Q = sb2("Q", [64, 7, 128], F32)
            self.act(dt[:], ldt[:], AF.Exp, ['ldt'], ['dt'])
            self.ts('dve', xr[:], lre[:], dt[:, 0:1], ALU.mult, ['lre', 'dt'], ['xr'])
            self.ts('dve', xi[:], lim[:], dt[:, 0:1], ALU.mult, ['lim', 'dt'], ['xi'])
            self.act(mag[:], xr[:], AF.Exp, ['xr'], ['mag'])
            self.sin_rr(sn[:], xi[:], 0.0, ti[:], tf[:], 's5a', ['xi'], ['sn'])
            self.sin_rr(cs[:], xi[:], PI / 2, ti[:], tf[:], 's5a', ['xi'], ['cs'])
            self.tt('dve', abre[:], mag[:], cs[:], ALU.mult, ['mag', 'cs'], ['abre'])
            self.tt('dve', abim[:], mag[:], sn[:], ALU.mult, ['mag', 'sn'], ['abim'])
            self.tt('dve', t1[:], lre[:], lre[:], ALU.mult, ['lre'], ['t1'])
            self.tt('dve', t2[:], lim[:], lim[:], ALU.mult, ['lim'], ['t2'])
            self.tt('dve', t1[:], t1[:], t2[:], ALU.add, ['t1', 't2'], ['t1'])
            self.recip(rden[:], t1[:], ['t1'], ['rden'])
            self.ts('dve', t1[:], abre[:], -1.0, ALU.add, ['abre', 'rden'], ['t1'])
            self.tt('dve', t2[:], t1[:], lre[:], ALU.mult, ['t1', 'lre'], ['t2'])
            self.tt('dve', tf[:], abim[:], lim[:], ALU.mult, ['abim', 'lim'], ['s5af'])
            self.tt('dve', t2[:], t2[:], tf[:], ALU.add, ['t2', 's5af'], ['t2'])
            self.tt('dve', Q[:, 5, 0:64], t2[:], rden[:], ALU.mult, ['t2', 'rden'], ['Q5'])
            self.tt('dve', t2[:], abim[:], lre[:], ALU.mult, ['abim', 'lre', 'Q5'], ['t2'])
            self.tt('dve', tf[:], t1[:], lim[:], ALU.mult, ['t1', 'lim'], ['s5af'])
            self.tt('dve', t2[:], t2[:], tf[:], ALU.subtract, ['t2', 's5af'], ['t2'])
            self.tt('dve', Q[:, 6, 0:64], t2[:], rden[:], ALU.mult, ['t2', 'rden'], ['Q6'])
            self.ts('dve', t1[:], xi[:], float(CH), ALU.mult, ['xi', 'Q6'], ['t1'])
            self.sin_rr(sn[:], t1[:], 0.0, ti[:], tf[:], 's5a', ['t1', 'abim'], ['sn'])
            self.sin_rr(cs[:], t1[:], PI / 2, ti[:], tf[:], 's5a', ['t1', 'abre'], ['cs'])
            self.cp('dve', Q[:, 0, 0:64], xi[:], ['xi'], ['Q0'])
            self.cp('dve', Q[:, 1, 0:64], mag[:], ['mag'], ['Q1'])
            self.tt('dve', Q[:, 2, 0:64], mag[:], cs[:], ALU.mult, ['mag', 'cs'], ['Q2'])
            self.tt('dve', Q[:, 3, 0:64], mag[:], sn[:], ALU.mult, ['mag', 'sn'], ['Q3'])
            self.ts('dve', Q[:, 4, 0:64], Q[:, 3, 0:64], -1.0, ALU.mult, ['Q3'], ['Q4'])
            self.cp('dve', Q[:, :, 64:128], Q[:, :, 0:64], [f'Q{i}' for i in range(7)], ['Qd'])
            for qi in range(5):
                self.tr(self.ps[4][:, qi * 64:(qi + 1) * 64], Q[:, qi, :], self.ident32[0:64, 0:64], ['Qd', f'Q{qi}', 'ident32'], ['ps4'])
            self.cp('dve', N2[0:64, :, :], self.ps[4][0:64, 0:320].rearrange("p (q c two) -> p q c two", q=5, two=2)[:, :, :, 0], ['ps4'], ['N2a'])
            self.cp('dve', N2[64:128, :, :], self.ps[4][64:128, 0:320].rearrange("p (q c two) -> p q c two", q=5, two=2)[:, :, :, 1], ['ps4'], ['N2b'])
            fN1 = sb2("fN1", [64, 2, 64], F32)
            for qi in range(2):
                self.tr(self.ps[5][0:64, qi * 64:(qi + 1) * 64], Q[:, 5 + qi, 0:64], self.ident32[0:64, 0:64], [f'Q{5 + qi}', 'ident32'], ['ps5'])
            self.cp('dve', fN1[:], self.ps[5][0:64, 0:128].rearrange("p (q c) -> p q c", q=2), ['ps5'], ['fN1'])
            bre = sb2("bre", [64, 64, 16], F32)
            bim = sb2("bim", [64, 64, 16], F32)
            Bb = sb2("Bb", [64, 2, 64, 16], F32)
            tb1 = sb2("tb1", [64, 64, 16], F32)
            tb2 = sb2("tb2", [64, 64, 16], F32)
            self.dma('sp', bre[:], W['s5_b_re'][l].rearrange("d g p c -> p (d g) c"), [], ['bre'])
            self.dma('sp', bim[:], W['s5_b_im'][l].rearrange("d g p c -> p (d g) c"), [], ['bim'])
            frb = fN1[:, 0, :].unsqueeze(2).to_broadcast([64, 64, 16])
            fib = fN1[:, 1, :].unsqueeze(2).to_broadcast([64, 64, 16])
            self.tt('dve', tb1[:], bre[:], frb, ALU.mult, ['bre', 'fN1'], ['tb1'])
            self.tt('pool', tb2[:], bim[:], fib, ALU.mult, ['bim', 'fN1'], ['tb2'])
            self.tt('dve', Bb[:, 0, :, :], tb1[:], tb2[:], ALU.subtract, ['tb1', 'tb2'], ['Bb0'])
            self.tt('dve', tb1[:], bim[:], frb, ALU.mult, ['bim', 'fN1', 'Bb0'], ['tb1'])
            self.tt('pool', tb2[:], bre[:], fib, ALU.mult, ['bre', 'fN1', 'Bb0'], ['tb2'])
            self.tt('dve', Bb[:, 1, :, :], tb1[:], tb2[:], ALU.add, ['tb1', 'tb2'], ['Bb1'])
            n = 0
            for ri in range(2):
                for d in range(2):
                    for c4 in range(4):
                        bank = 5 + (n % 2)
                        n += 1
                        g0 = d * 32 + c4 * 8
                        self.tr(self.ps[bank][:, 0:64], Bb[:, ri, g0:g0 + 8, :].rearrange("p g c -> p (g c)"), self.ident32[0:64, 0:64],
                                [f'Bb{ri}', 'ident32'], [f'ps{bank}'])
                        self.ts('dve', BT[:, ri, d, c4, 0:64], self.ps[bank][:, 0:64], self.m01[:, 0:1], ALU.mult, [f'ps{bank}', 'm01'], ['BT'])
                        self.ts('dve', BT[:, ri, d, c4, 64:128], self.ps[bank][:, 0:64], self.m01[:, 1:2], ALU.mult, [f'ps{bank}', 'm01'], ['BT'])
            Cl = sb2("Cl", [128, 2, 8, 128], F32)
            for ri, nm in enumerate(('s5_c_re', 's5_c_im')):
                src = W[nm][l].rearrange("d (c4 g8) c p -> (g8 c) (d c4) p", g8=8)
                for dup in range(2):
                    self.dma('sp', Cl[:, ri, :, dup * 64:(dup + 1) * 64], src, [], ['Cl'])
            self.memset('pool', CT[:], 0.0, ['CT'])
            n = 0
            for ri in range(2):
                for d in range(2):
                    for c4 in range(4):
                        bank = 5 + (n % 2)
                        n += 1
                        self.tr(self.ps[bank][:, 0:128], Cl[:, ri, d * 4 + c4, :], self.ident32[:], ['Cl', 'ident32'], [f'ps{bank}'])
                        sc = 1.0 if ri == 0 else -1.0
                        v0 = self.ps[bank][0:64, 0:128].rearrange("p (q g c) -> p q g c", q=4, g=2)[:, :, 0, :]
                        v1 = self.ps[bank][64:128, 0:128].rearrange("p (q g c) -> p q g c", q=4, g=2)[:, :, 1, :]
                        self.act(CT[0:64, ri, d, c4 * 4:(c4 + 1) * 4, 0:16], v0, AF.Identity, [f'ps{bank}'], ['CT'], scale=sc)
                        self.act(CT[64:128, ri, d, c4 * 4:(c4 + 1) * 4, 16:32], v1, AF.Identity, [f'ps{bank}'], ['CT'], scale=sc)
                        if ri == 0:
                            self.act(CT[0:64, 2, d, c4 * 4:(c4 + 1) * 4, 0:16], v0, AF.Identity, [f'ps{bank}'], ['CT'], scale=-1.0)
                            self.act(CT[64:128, 2, d, c4 * 4:(c4 + 1) * 4, 16:32], v1, AF.Identity, [f'ps{bank}'], ['CT'], scale=-1.0)
            tau = sb2("tau", [128, CH], F32)
            ph = sb2("ph", [128, 32, CH], F32)
            tib = sb2("tib", [128, 32 * CH], I32)
            tfb = sb2("tfb", [128, 32 * CH], F32)
            self.dma('sp', tau[:], K['k_tau'][:, :], [], ['tau'])
            self.tt('dve', ph[:], N2[:, 0, :].unsqueeze(2).to_broadcast([128, 32, CH]), tau[:].unsqueeze(1).to_broadcast([128, 32, CH]), ALU.mult,
                    ['N2a', 'N2b', 'tau'], ['ph'])
            phf = ph[:].rearrange("p a b -> p (a b)")
            self.sin_rr(sinT[:].rearrange("p a b -> p (a b)"), phf, 0.0, tib[:], tfb[:], 's5b', ['ph'], ['sinT'])
            self.sin_rr(cosT[:].rearrange("p a b -> p (a b)"), phf, PI / 2, tib[:], tfb[:], 's5b', ['ph'], ['cosT'])
            self.P.flush()
        with contextlib.ExitStack() as s3:
            sb3 = lambda n, s, d: self.sb(s3, n, s, d)
            NCH = NT // CH
            Rt = [sb3("Rt", [128, 2, CH], F32) for _ in range(2)]
            onesc = sb3("onesc", [128, 2, CH], F32)
            self.memset('dve', onesc[:], 1.0, ['onesc'])
            self.memset('dve', onesc[:, 1, 0:1], 0.0, ['onesc'])
            coefX = sb3("coefX", [128, 32, 2], F32)
            ctmp = sb3("ctmp", [128, 2, 1], F32)
            self.cp('dve', coefX[:, :, 0], N2[:, 4, :], ['N2a', 'N2b'], ['coefX'])
            self.cp('dve', coefX[:, :, 1], N2[:, 3, :], ['N2a', 'N2b'], ['coefX'])
            gin = sb3("gin", [128, NCH, 2, CH], BF16)
            gout = [sb3("gout", [128, NCH, 2, CH], F32) for _ in range(2)]
            gob = [sb3("gob", [128, 2, NT], BF16)] * 2
            bfull = [sb3("bfull", [128, 2, NT], BF16) for _ in range(2)]
            dmt = [sb3("dmt", [128, NT], BF16) for _ in range(4)]
            hq = dmt
            gflat = gout[0][:].rearrange("p c r t -> p (c r t)")
            ypre = gflat[:, 0:NT]
            gx = gflat[:, NT:2 * NT]
            yg = sb3("yg", [128, 4, NT], BF16)
            lat_ch = NL // CH
            chunks_f = list(range(lat_ch, NCH)) + list(range(lat_ch))
            chunks_b = list(range(NCH - 1, lat_ch - 1, -1)) + list(range(lat_ch - 1, -1, -1))
            vch = lambda ap: ap.rearrange("p (a b) -> p a b", b=CH)
            units = [(c4, q, d) for c4 in range(4) for q in range(4) for d in range(2)]

            def uinfo(ui):
                c4, q, d = units[ui]
                gp = c4 * 4 + q
                u = d * 16 + gp
                ub = ui % 2
                cosu = cosT[:, u, :] if d == 0 else cosT[:, u, :][:, ::-1]
                sinu = sinT[:, u, :] if d == 0 else sinT[:, u, :][:, ::-1]
                cbN = cosu.unsqueeze(1).to_broadcast([128, NCH, CH])
                sbN = sinu.unsqueeze(1).to_broadcast([128, NCH, CH])
                return c4, q, d, gp, u, ub, cbN, sbN

            def evac(ui):
                c4, q, d, gp, u, ub, cbN, sbN = uinfo(ui)
                bf_ = bfull[ub]
                for it, (t0, sz, col) in enumerate(TT):
                    pb5 = 5 + (it % 2)
                    self.mm(self.ps[pb5][:, 0:sz], BT[32 * q:32 * q + 32, 0, d, c4, :], uT[32 * q:32 * q + 32, c4, t0:t0 + sz], True, True,
                            [f'uT{c4}'], [f'ps{pb5}'], tp=(32 * q, 0))
                    self.cp('act', bf_[:, 0, t0:t0 + sz], self.ps[pb5][:, 0:sz], [f'ps{pb5}'], [f'bfr{ub}'])
                for it, (t0, sz, col) in enumerate(TT):
                    pb5 = 5 + ((it + 1) % 2)
                    self.mm(self.ps[pb5][:, 0:sz], BT[32 * q:32 * q + 32, 1, d, c4, :], uT[32 * q:32 * q + 32, c4, t0:t0 + sz], True, True,
                            [f'uT{c4}'], [f'ps{pb5}'], tp=(32 * q, 0))
                    self.cp('act', bf_[:, 1, t0:t0 + sz], self.ps[pb5][:, 0:sz], [f'ps{pb5}'], [f'bfi{ub}'])

            def mod(ui):
                c4, q, d, gp, u, ub, cbN, sbN = uinfo(ui)
                bf_ = bfull[ub]
                self.ts('dve', Rt[ub][:], onesc[:], N2[:, 1, u:u + 1], ALU.mult, ['onesc'], [f'R{ub}'])
                self.tt('dve', vch(dmt[0][:]), vch(bf_[:, 0, :]), cbN, ALU.mult, [f'bfr{ub}'], ['dm0'])
                self.tt('dve', vch(dmt[1][:]), vch(bf_[:, 1, :]), sbN, ALU.mult, [f'bfi{ub}'], ['dm1'])
                self.tt('dve', gin[:, :, 0, :], vch(dmt[0][:]), vch(dmt[1][:]), ALU.add, ['dm0', 'dm1'], ['ginr'])
                self.tt('dve', vch(dmt[2][:]), vch(bf_[:, 1, :]), cbN, ALU.mult, [f'bfi{ub}'], ['dm2'])
                self.tt('dve', vch(dmt[3][:]), vch(bf_[:, 0, :]), sbN, ALU.mult, [f'bfr{ub}'], ['dm3'])
                self.tt('dve', gin[:, :, 1, :], vch(dmt[2][:]), vch(dmt[3][:]), ALU.subtract, ['dm2', 'dm3'], ['gini'])

            def scans(ui):
                c4, q, d, gp, u, ub, cbN, sbN = uinfo(ui)
                R = Rt[ub]
                go = gout[ub]
                order = chunks_f if d == 0 else chunks_b
                prev = None
                tf_ = 0 if d == 0 else CH - 1
                tl_ = CH - 1 if d == 0 else 0
                gkeys = ['ginr', 'gini']
                gokeys = [f'gor{ub}', f'goi{ub}']
                for ci in order:
                    if prev is not None:
                        gv = gin[:, ci, :, tf_:tf_ + 1]
                        self.stt(gv, go[:, prev, :, tl_:tl_ + 1], N2[:, 2, u:u + 1], gv, ALU.mult, ALU.add, gkeys + gokeys, gkeys)
                        self.tt('dve', ctmp[:], go[:, prev, ::-1, tl_:tl_ + 1], coefX[:, u, :].unsqueeze(2), ALU.mult, gokeys + ['coefX'], ['ctmp'])
                        self.tt('dve', gv, gv, ctmp[:], ALU.add, gkeys + ['ctmp'], gkeys)
                    src = gin[:, ci, :, :].rearrange("p r t -> p (r t)")
                    dst = go[:, ci, :, :].rearrange("p r t -> p (r t)")
                    if d == 1:
                        src = src[:, ::-1]
                        dst = dst[:, ::-1]
                    self.scan(dst, R[:].rearrange("p r t -> p (r t)"), src, gkeys + [f'R{ub}'], gokeys)
                    prev = ci
                gb_ = gob[ub]
                self.cp('act', vch(gb_[:, 0, :]), go[:, :, 0, :], [f'gor{ub}'], ['gbr'])
                self.cp('act', vch(gb_[:, 1, :]), go[:, :, 1, :], [f'goi{ub}'], ['gbi'])

            def demod(ui):
                c4, q, d, gp, u, ub, cbN, sbN = uinfo(ui)
                gb_ = gob[ub]
                self.tt('dve', vch(hq[0][:]), vch(gb_[:, 0, :]), cbN, ALU.mult, ['gbr'], ['dm0'])
                self.tt('dve', vch(hq[1][:]), vch(gb_[:, 1, :]), sbN, ALU.mult, ['gbi'], ['dm1'])
                self.tt('dve', vch(hq[2][:]), vch(gb_[:, 0, :]), sbN, ALU.mult, ['gbr'], ['dm2'])
                self.tt('dve', vch(hq[3][:]), vch(gb_[:, 1, :]), cbN, ALU.mult, ['gbi'], ['dm3'])
                csel = (0, 2, 1, 1)
                for it, (t0, sz, col) in enumerate(TT):
                    for m in range(4):
                        self.mm(self.ps[it][32 * q:32 * q + 32, 0:sz], CT[:, csel[m], d, gp, :], hq[m][:, t0:t0 + sz],
                                d == 0 and m == 0, d == 1 and m == 3, [f'dm{m}'], [f'ps{it}'], tp=(0, 32 * q))

            def gelu(c4):
                for it, (t0, sz, col) in enumerate(TT):
                    self.stt(ypre[:, t0:t0 + sz], uT[:, c4, t0:t0 + sz], dvec[:, c4:c4 + 1], self.ps[it][:, 0:sz], ALU.mult, ALU.add,
                             [f'ps{it}', 'dvec'], ['gor0'])
                self.tt('pool', gx, ypre, ypre, ALU.mult, ['gor0'], ['goi0'])
                self.ts('pool', gx, gx, 0.044715, ALU.mult, ['goi0'], ['goi0'], s2=1.0, op1=ALU.add)
                self.tt('pool', gx, gx, ypre, ALU.mult, ['goi0', 'gor0'], ['goi0'])
                self.act(gx, gx, AF.Sigmoid, ['goi0'], ['goi0'], scale=1.5957691216057308)
                self.tt('pool', yg[:, c4, :], ypre, gx, ALU.mult, ['goi0', 'gor0'], [f'yg{c4}'])

            nu = len(units)
            evac(0)
            mod(0)
            for ui in range(nu):
                if ui + 1 < nu:
                    evac(ui + 1)
                scans(ui)
                if ui + 1 < nu:
                    mod(ui + 1)
                demod(ui)
                if ui % 8 == 7:
                    gelu(units[ui][0])
            gw = sb3("gw", [128, 4, 512], BF16)
            self.dma('pool', gw[:], W['s5_glu_w'][l].rearrange("(c p) n -> p c n", p=128), [], ['gw'])
            yat = [sb3("yat", [128, 4, 512], BF16) for _ in range(2)]
            sgt = [sb3("sgt", [128, 512], F32) for _ in range(2)]
            n = 0
            for it, (t0, sz, col) in enumerate(TT):
                b = it % 2
                for co in range(4):
                    bank = 5 + (n % 2)
                    sg = sgt[n % 2]
                    n += 1
                    for ci in range(4):
                        self.mm(self.ps[bank][:, 0:sz], gw[:, ci, co * 128:(co + 1) * 128], yg[:, ci, t0:t0 + sz], ci == 0, ci == 3,
                                ['gw'] + [f'yg{i}' for i in range(4)], [f'ps{bank}'])
                    self.act(sg[:, 0:sz], self.ps[bank][:, 0:sz], AF.Sigmoid, [f'ps{bank}', 'gb'], [f'sg{n % 2}'], bias=gb[:, co:co + 1])
                    self.tt('dve', yat[b][:, co, 0:sz], yg[:, co, t0:t0 + sz], sg[:, 0:sz], ALU.mult, [f'sg{n % 2}', f'yg{co}'], [f'yat{b}'])
                self.dma('sp', self.ybr_a[:, :, t0:t0 + sz], yat[b][:, :, 0:sz], [f'yat{b}'], ['ybr_a'])
            self.P.flush()


KB.phase_s5 = phase_s5

def phase_att(self, l):
    W = self.W
    K = self.K
    need_ctx = (l == 0)
    with contextlib.ExitStack() as st:
        sb = lambda n, s, d: self.sb(st, n, s, d)
        qT = sb("qT", [128, 4, NT], BF16)
        kT = sb("kT", [128, NT], BF16)
        vtm = sb("vtm", [128, 18, 128], BF16)
        yb = sb("yb", [64, 8, NT], BF16)
        with contextlib.ExitStack() as s2:
            sb2 = lambda n, s, d: self.sb(s2, n, s, d)
            xnT = sb2("xnT", [128, 8, NT], BF16)
            wraw = sb2("wraw", [128, 8, 768], BF16)
            wqk = sb2("wqk", [128, 8, 640], BF16)
            gq = sb2("gq", [128, 1], F32)
            gk = sb2("gk", [128, 1], F32)
            cos2 = sb2("cos2", [128, NL], F32)
            sin2 = sb2("sin2", [128, NL], F32)
            sperm = sb2("sperm", [128, 128], BF16)
            ones2 = sb2("ones2", [128, 128], BF16)
            self.load_xnT(xnT, NT)
            self.dma('pool', wraw[:], W['w_in'][l].rearrange("(k p) n -> p k n", p=128)[:, :, 512:1280], [], ['wraw'])
            self.memset('dve', sperm[:], 0.0, ['sperm'])
            self.memset('dve', ones2[:], 0.0, ['ones2'])
            for hf in range(2):
                ps_ = slice(hf * 64, (hf + 1) * 64)
                self.dma('sp', cos2[ps_, :], K['k_cos2'][:, :], [], ['cos2'])
                self.dma('sp', sin2[ps_, :], K['k_sin2'][:, :], [], ['sin2'])
                self.dma('sp', sperm[ps_, hf * 64:(hf + 1) * 64], K['k_sperm'][:, :], ['sperm'], ['sperm'])
                self.memset('dve', ones2[ps_, hf * 64:(hf + 1) * 64], 1.0, ['ones2'])
                for g_t, nm in ((gq, 'q_norm_g'), (gk, 'k_norm_g')):
                    src = W[nm][l].rearrange("(i t) -> i t", t=2)
                    self.dma('sp', g_t[hf * 64:hf * 64 + 32, :], src[:, 0:1], [], ['gqk'], slow=True)
                    self.dma('sp', g_t[hf * 64 + 32:hf * 64 + 64, :], src[:, 1:2], [], ['gqk'], slow=True)
            for k in range(8):
                for g0 in range(2):
                    self.cp('pool' if (k + g0) % 2 == 0 else 'dve',
                            wqk[:, k, 0:512].rearrange("p (q g t i) -> p q g t i", q=4, g=2, t=2, i=32)[:, :, g0, :, :],
                            wraw[:, k, g0 * 256:(g0 + 1) * 256].rearrange("p (q i t) -> p q t i", q=4, i=32, t=2), ['wraw'], [f'wqk{k}_{g0}'])
                self.cp('dve' if k % 2 == 0 else 'pool',
                        wqk[:, k, 512:640].rearrange("p (h t i) -> p h t i", h=2, t=2, i=32),
                        wraw[:, k, 512:640].rearrange("p (h i t) -> p h t i", h=2, i=32, t=2), ['wraw'], [f'wqk{k}_2'])
            wqk_keys = [f'wqk{k}_{g}' for k in range(8) for g in range(3)]
            sqt = [sb2("sqt", [128, 512], BF16) for _ in range(2)]
            rst = [sb2("rst", [128, 512], F32) for _ in range(2)]
            qnt = [sb2("qnt", [128, 512], BF16) for _ in range(2)]
            r1t = [sb2("r1t", [128, 512], F32) for _ in range(2)]
            r2t = [sb2("r2t", [128, 512], F32) for _ in range(2)]
            cnt = [0]

            ulist = []
            for it, (t0, sz, col) in enumerate(TT):
                for pr in range(5):
                    if pr < 4 and col == 1 and not need_ctx:
                        continue
                    ulist.append((it, t0, sz, col, pr))
            pbanks = [0, 1, 6]

            def uinfo(n):
                it, t0, sz, col, pr = ulist[n]
                bank = pbanks[n % 3]
                b = n % 2
                g_t = gq if pr < 4 else gk
                dst = qT[:, pr, t0:t0 + sz] if pr < 4 else kT[:, t0:t0 + sz]
                dstkey = 'qT' if pr < 4 else 'kT'
                rope_t0 = t0 if col == 0 else None
                return it, t0, sz, pr, bank, b, g_t, dst, dstkey, rope_t0

            def proj(n):
                it, t0, sz, pr, bank, b, g_t, dst, dstkey, rope_t0 = uinfo(n)
                for k in range(8):
                    self.mm(self.ps[bank][:, 0:sz], wqk[:, k, pr * 128:(pr + 1) * 128], xnT[:, k, t0:t0 + sz], k == 0, k == 7,
                            wqk_keys + [f'xnT_{it}'], [f'ps{bank}'])

            def st1(n):
                it, t0, sz, pr, bank, b, g_t, dst, dstkey, rope_t0 = uinfo(n)
                self.act(sqt[b][:, 0:sz], self.ps[bank][:, 0:sz], AF.Square, [f'ps{bank}'], [f'sqt{b}'])
                self.mm(self.ps[2 + b][:, 0:sz], ones2[:], sqt[b][:, 0:sz], True, True, [f'sqt{b}', 'ones2'], [f'ps{2 + b}'])

            def st2(n):
                it, t0, sz, pr, bank, b, g_t, dst, dstkey, rope_t0 = uinfo(n)
                raw = self.ps[bank][:, 0:sz]
                self.act(rst[b][:, 0:sz], self.ps[2 + b][:, 0:sz], AF.Sqrt, [f'ps{2 + b}', 'eps'], [f'rst{b}'], scale=1.0 / 64.0, bias=self.eps[:, 0:1])
                self.recip(rst[b][:, 0:sz], rst[b][:, 0:sz], [f'rst{b}'], [f'rst{b}'])
                if rope_t0 is None:
                    self.stt(dst, raw, g_t[:, 0:1], rst[b][:, 0:sz], ALU.mult, ALU.mult, [f'ps{bank}', f'rst{b}', 'gqk'], [dstkey])
                    return
                self.stt(qnt[b][:, 0:sz], raw, g_t[:, 0:1], rst[b][:, 0:sz], ALU.mult, ALU.mult, [f'ps{bank}', f'rst{b}', 'gqk'], [f'qnt{b}'])
                self.mm(self.ps[4 + b][:, 0:sz], sperm[:], qnt[b][:, 0:sz], True, True, [f'qnt{b}', 'sperm'], [f'ps{4 + b}'])

            def st3(n):
                it, t0, sz, pr, bank, b, g_t, dst, dstkey, rope_t0 = uinfo(n)
                if rope_t0 is None:
                    return
                self.tt('pool', r1t[b][:, 0:sz], qnt[b][:, 0:sz], cos2[:, rope_t0:rope_t0 + sz], ALU.mult, [f'qnt{b}', 'cos2'], [f'r1t{b}'])
                self.tt('dve', r2t[b][:, 0:sz], self.ps[4 + b][:, 0:sz], sin2[:, rope_t0:rope_t0 + sz], ALU.mult, [f'ps{4 + b}', 'sin2'], [f'r2t{b}'])
                self.tt('pool', dst, r1t[b][:, 0:sz], r2t[b][:, 0:sz], ALU.add, [f'r1t{b}', f'r2t{b}'], [dstkey])

            NU = len(ulist)
            proj(0)
            for n in range(NU + 2):
                if n + 1 < NU:
                    proj(n + 1)
                if n < NU:
                    st1(n)
                if 0 <= n - 1 < NU:
                    st2(n - 1)
                if 0 <= n - 2 < NU:
                    st3(n - 2)
            vbanks = [6, 0, 1]
            for i in range(18):
                vb = vbanks[i % 3]
                for k in range(8):
                    self.mm(self.ps[vb][:, 0:128], xnT[:, k, i * 128:(i + 1) * 128], wraw[:, k, 640:768], k == 0, k == 7,
                            [f'xnT_{min(i // 4, 4)}', 'wraw'], [f'ps{vb}'])
                self.cp('act', vtm[:, i, :], self.ps[vb][:, 0:128], [f'ps{vb}'], ['vtm'])
            self.P.flush()
        with contextlib.ExitStack() as s3:
            sb3 = lambda n, s, d: self.sb(s3, n, s, d)
            mprev = sb3("mprev", [128, 512], BF16)
            mnext = sb3("mnext", [128, 512], BF16)
            sk = sb3("sk", [64, 8], F32)
            eskb = sb3("eskb", [64, 8, 128], F32)
            pt = [sb3("pt", [128, 512], BF16) for _ in range(4)]
            sbanks = [0, 1, 6]
            dsum = [sb3("dsum", [64, 512], F32) for _ in range(2)]
            self.dma('sp', mprev[:], K['k_mprev'][:, :], [], ['mprev'])
            self.dma('sp', mnext[:], K['k_mnext'][:, :], [], ['mnext'])
            self.dma('sp', sk[:], W['attn_sink'][l].partition_broadcast(64), [], ['sk'])
            self.act(sk[:], sk[:], AF.Exp, ['sk'], ['sk'])
            self.cp('dve', eskb[:], sk[:].unsqueeze(2).to_broadcast([64, 8, 128]), ['sk'], ['eskb'])
            qbs = list(range(16)) + ([16, 17] if need_ctx else [])
            n = 0
            m = 0
            for qb in qbs:
                for kvh in range(2):
                    if qb < 16:
                        kbs = ([(qb - 1, 'prev')] if qb > 0 else []) + [(qb, 'own')] + ([(qb + 1, 'next')] if qb < 15 else []) + [(16, 'ctx'), (17, 'ctx')]
                    else:
                        kbs = [(16, 'ctx'), (17, 'ctx')]
                    ob = 2 + 2 * (m % 2)
                    db = ob + 1
                    mb_ = m % 2
                    m += 1
                    def score(j):
                        kb, kind = kbs[j]
                        sbk = sbanks[(n + j) % 3]
                        pb = (n + j) % 4
                        self.mm(self.ps[sbk][:, :], kT[kvh * 64:(kvh + 1) * 64, kb * 128:(kb + 1) * 128], qT[kvh * 64:(kvh + 1) * 64, :, qb * 128:(qb + 1) * 128],
                                True, True, ['kT', 'qT'], [f'ps{sbk}'])
                        self.act(pt[pb][:], self.ps[sbk][:, :], AF.Exp, [f'ps{sbk}'], [f'pt{pb}'], scale=0.125)
                        if kind == 'prev':
                            self.tt('pool', pt[pb][:], pt[pb][:], mprev[:], ALU.mult, [f'pt{pb}', 'mprev'], [f'pt{pb}'])
                        elif kind == 'next':
                            self.tt('pool', pt[pb][:], pt[pb][:], mnext[:], ALU.mult, [f'pt{pb}', 'mnext'], [f'pt{pb}'])

                    def pv(j):
                        kb, kind = kbs[j]
                        pb = (n + j) % 4
                        self.mm(self.ps[ob][0:64, :], vtm[:, kb, kvh * 64:(kvh + 1) * 64], pt[pb][:], j == 0, j == len(kbs) - 1,
                                ['vtm', f'pt{pb}'], [f'ps{ob}'])
                        self.mm(self.ps[db][0:64, :], self.ones_bf[:, 0:64], pt[pb][:], j == 0, j == len(kbs) - 1,
                                ['ones_bf', f'pt{pb}'], [f'ps{db}'])

                    score(0)
                    if len(kbs) > 1:
                        score(1)
                    for j in range(len(kbs)):
                        if j + 2 < len(kbs):
                            score(j + 2)
                        pv(j)
                    n += len(kbs)
                    ds = dsum[mb_]
                    self.tt('dve', ds[:].rearrange("p (a b) -> p a b", a=4), self.ps[db][0:64, :].rearrange("p (a b) -> p a b", a=4),
                            eskb[:, kvh * 4:(kvh + 1) * 4, :], ALU.add, [f'ps{db}', 'eskb'], [f'dsum{mb_}'])
                    self.recip(ds[:], ds[:], [f'dsum{mb_}'], [f'dsum{mb_}'])
                    self.tt('dve', yb[:, kvh * 4:(kvh + 1) * 4, qb * 128:(qb + 1) * 128], self.ps[ob][0:64, :].rearrange("p (a b) -> p a b", a=4),
                            ds[:].rearrange("p (a b) -> p a b", a=4), ALU.mult, [f'ps{ob}', f'dsum{mb_}'], ['yb'])
            hi = NT if need_ctx else NL
            self.dma('sp', self.ybr_b[:, :, 0:hi], yb[:, :, 0:hi], ['yb'], ['ybr_b'])
            self.P.flush()


KB.phase_att = phase_att

def phase_ffn(self, l):
    W = self.W
    dense = (l == 0)
    tiles = TT if dense else TT[:4]
    ntok = NT if dense else NL
    H = 2816 if dense else 3584
    nh = H // 128
    nexp = 1 if dense else 8
    with contextlib.ExitStack() as st:
        sb = lambda n, s, d: self.sb(st, n, s, d)
        xnT = sb("xnT", [128, 8, ntok], BF16)
        hT = sb("hT", [128, nh, ntok], BF16)
        w1b = [sb("w1b", [128, 8, 256], BF16) for _ in range(2)]
        w3b = [sb("w3b", [128, 8, 256], BF16) for _ in range(2)]
        w2b = [sb("w2b", [128, nh, 128], BF16) for _ in range(2)]
        sl = [sb("silu", [128, 512], F32) for _ in range(2)]
        xo = [sb("xo", [128, 512], F32) for _ in range(3)]
        tmpc = [sb("tmpc", [128, 512], F32) for _ in range(2)]
        if not dense:
            comb = [sb("comb", [128, NL], F32)] * 2
        self.load_xnT(xnT, ntok)
        na = 0
        nb = 0
        nw2 = 0
        for e in range(nexp):
            if dense:
                w1, w3, w2 = W['ffn_w1'][0], W['ffn_w3'][0], W['ffn_w2'][0]
            else:
                w1, w3, w2 = W['moe_w1'][0][e], W['moe_w3'][0][e], W['moe_w2'][0][e]
                cbt = comb[e % 2]
                self.dma('sp', cbt[:], self.combd[e:e + 1, :].partition_broadcast(128), [], ['comb0'])
            w1v = w1.rearrange("(k p) n -> p k n", p=128)
            w3v = w3.rearrange("(k p) n -> p k n", p=128)
            w2v = w2.rearrange("(j p) n -> p j n", p=128)
            for hb in range(H // 256):
                wbuf = (e * (H // 256) + hb) % 2
                self.dma('pool', w1b[wbuf][:], w1v[:, :, hb * 256:(hb + 1) * 256], [], [f'w1b{wbuf}'])
                self.dma('pool', w3b[wbuf][:], w3v[:, :, hb * 256:(hb + 1) * 256], [], [f'w3b{wbuf}'])
                for js in range(2):
                    j = hb * 2 + js
                    for it, (t0, sz, col) in enumerate(tiles):
                        pa = 2 * (na % 2)
                        pb = pa + 1
                        sk = na % 2
                        na += 1
                        for k in range(8):
                            self.mm(self.ps[pa][:, 0:sz], w1b[wbuf][:, k, js * 128:(js + 1) * 128], xnT[:, k, t0:t0 + sz], k == 0, k == 7,
                                    [f'w1b{wbuf}', f'xnT_{it}'], [f'ps{pa}'])
                        for k in range(8):
                            self.mm(self.ps[pb][:, 0:sz], w3b[wbuf][:, k, js * 128:(js + 1) * 128], xnT[:, k, t0:t0 + sz], k == 0, k == 7,
                                    [f'w3b{wbuf}', f'xnT_{it}'], [f'ps{pb}'])
                        self.act(sl[sk][:, 0:sz], self.ps[pa][:, 0:sz], AF.Silu, [f'ps{pa}'], [f'silu{sk}'])
                        self.tt('dve', hT[:, j, t0:t0 + sz], self.ps[pb][:, 0:sz], sl[sk][:, 0:sz], ALU.mult, [f'ps{pb}', f'silu{sk}'], [f'hT{j}'])
            hkeys = [f'hT{j}' for j in range(nh)]
            for dt in range(8):
                wb = nw2 % 2
                nw2 += 1
                self.dma('pool', w2b[wb][:], w2v[:, :, dt * 128:(dt + 1) * 128], [], [f'w2b{wb}'])
                for it, (t0, sz, col) in enumerate(tiles):
                    bank = 4 + (nb % 2)
                    xb = nb % 3
                    tk = nb % 2
                    nb += 1
                    self.dma('sp', xo[xb][:, 0:sz], self.xres[:, dt, t0:t0 + sz], [f'xres{dt}_{it}'], [f'xo{xb}'])
                    for j in range(nh):
                        self.mm(self.ps[bank][:, 0:sz], w2b[wb][:, j, :], hT[:, j, t0:t0 + sz], j == 0, j == nh - 1, [f'w2b{wb}'] + hkeys, [f'ps{bank}'])
                    g2 = self.modT[:, 40 + dt, col:col + 1]
                    if dense:
                        self.stt(xo[xb][:, 0:sz], self.ps[bank][:, 0:sz], g2, xo[xb][:, 0:sz], ALU.mult, ALU.add, [f'ps{bank}', f'xo{xb}', 'modT'], [f'xo{xb}'])
                    else:
                        self.tt('dve', tmpc[tk][:, 0:sz], self.ps[bank][:, 0:sz], cbt[:, t0:t0 + sz], ALU.mult, [f'ps{bank}', f'comb{e % 2}'], [f'tmpc{tk}'])
                        self.stt(xo[xb][:, 0:sz], tmpc[tk][:, 0:sz], g2, xo[xb][:, 0:sz], ALU.mult, ALU.add, [f'tmpc{tk}', f'xo{xb}', 'modT'], [f'xo{xb}'])
                    self.dma('sp', self.xres[:, dt, t0:t0 + sz], xo[xb][:, 0:sz], [f'xo{xb}'], [f'xres{dt}_{it}'])
        self.P.flush()


def phase_output(self):
    with contextlib.ExitStack() as st:
        xt = [self.sb(st, "xto", [128, 8, 512], F32) for _ in range(2)]
        ot = [self.sb(st, "oto", [128, 1024], F32) for _ in range(2)]
        self.dma('sp', xt[0][:], self.xres[:, :, 0:512], [], ['xto0'])
        for g4 in range(4):
            xb = g4 % 2
            if g4 + 1 < 4:
                self.dma('sp', xt[1 - xb][:], self.xres[:, :, (g4 + 1) * 512:(g4 + 2) * 512], [], [f'xto{1 - xb}'])
            for j4 in range(4):
                ti = g4 * 4 + j4
                b = ti % 2
                for k in range(8):
                    bank = 2 * b + k // 4
                    self.tr(self.ps[bank][:, (k % 4) * 128:(k % 4 + 1) * 128], xt[xb][:, k, j4 * 128:(j4 + 1) * 128], self.ident32[:],
                            [f'xto{xb}', 'ident32'], [f'ps{bank}'])
                self.cp('act', ot[b][:, 0:512], self.ps[2 * b][:, :], [f'ps{2 * b}'], [f'oto{b}a'])
                self.cp('dve', ot[b][:, 512:1024], self.ps[2 * b + 1][:, :], [f'ps{2 * b + 1}'], [f'oto{b}b'])
                self.dma('sp', self.out[ti * 128:(ti + 1) * 128, :], ot[b][:], [f'oto{b}a', f'oto{b}b'], [f'out{ti}'])
        self.P.flush()


KB.phase_ffn = phase_ffn
KB.phase_output = phase_output

def phase_hy(self, l):
    W = self.W
    K = self.K
    need_ctx = (l == 0)
    hi_tok = NT if need_ctx else NL
    with contextlib.ExitStack() as st:
        sb = lambda n, s, d: self.sb(st, n, s, d)
        xnT = sb("xnT", [128, 8, NT], BF16)
        wblk = [sb("why", [128, 8, 512], BF16) for _ in range(2)]
        zc = [sb("zc", [128, NT], BF16) for _ in range(2)]
        ho = [sb("ho", [128, NT], BF16) for _ in range(2)]
        cw = sb("cw", [128, 3, 12], F32)
        cb = sb("cb", [128, 12], F32)
        self.load_xnT(xnT, NT)
        for tap in range(3):
            self.dma('sp', cw[:, tap, :], W['hy_conv_w'][l][tap].rearrange("(k p) -> p k", p=128), [], ['cw'], slow=True)
        self.dma('sp', cb[:], W['hy_conv_b'][l].rearrange("(k p) -> p k", p=128), [], ['cb'], slow=True)
        wsrc = W['w_in'][l].rearrange("(k p) n -> p k n", p=128)
        n = 0
        for j in range(12):
            blk, jj = j // 4, j % 4
            wb = blk % 2
            zb = j % 2
            if jj == 0:
                self.dma('pool', wblk[wb][:], wsrc[:, :, 1280 + blk * 512:1280 + (blk + 1) * 512], [], [f'why{wb}'])
            for it, (t0, sz, col) in enumerate(TT):
                if col == 1 and not need_ctx:
                    continue
                bank = n % 4
                n += 1
                for k in range(8):
                    self.mm(self.ps[bank][:, 0:sz], wblk[wb][:, k, jj * 128:(jj + 1) * 128], xnT[:, k, t0:t0 + sz], k == 0, k == 7,
                            [f'why{wb}', f'xnT_{it}'], [f'ps{bank}'])
                self.cp('act', zc[zb][:, t0:t0 + sz], self.ps[bank][:, 0:sz], [f'ps{bank}'], [f'zc{zb}'])
            for (a, b) in ([(0, NL), (NL, NT)] if need_ctx else [(0, NL)]):
                self.ts('dve', ho[zb][:, a:b], zc[zb][:, a:b], cw[:, 1, j:j + 1], ALU.mult, [f'zc{zb}', 'cw', 'cb'], [f'ho{zb}'], s2=cb[:, j:j + 1], op1=ALU.add)
                self.stt(ho[zb][:, a + 1:b], zc[zb][:, a:b - 1], cw[:, 0, j:j + 1], ho[zb][:, a + 1:b], ALU.mult, ALU.add, [f'zc{zb}', f'ho{zb}', 'cw'], [f'ho{zb}'])
                self.stt(ho[zb][:, a:b - 1], zc[zb][:, a + 1:b], cw[:, 2, j:j + 1], ho[zb][:, a:b - 1], ALU.mult, ALU.add, [f'zc{zb}', f'ho{zb}', 'cw'], [f'ho{zb}'])
            self.dma('sp', self.hcd[:, j, 0:hi_tok], ho[zb][:, 0:hi_tok], [f'ho{zb}'], ['hcd'])
        self.P.flush()
    if self.halt_at(f'hyA_{l}'):
        return
    seqs = [('lat', NL, 0)] + ([('ctx', NCX, NL)] if need_ctx else [])
    for (sname, L, toff) in seqs:
        nf = L // 128
        N = 2 * L
        with contextlib.ExitStack() as sq_:
            sbq = lambda n, s, d: self.sb(sq_, n, s, d)
            Kh = sbq("Kh", [128, nf, 2, 512], BF16)
            Zst = sbq("Zst", [128, nf, 2, 512], BF16)
            vbuf = [sbq("vbuf", [128, 4, L], BF16) for _ in range(2)]
            h2T = sbq("h2T", [64, L], F32)
            h2Tb = sbq("h2Tb", [64, L], BF16)
            knyq = sbq("knyq", [1, 512], F32)
            znyq = sbq("znyq", [1, 512], BF16)
            hbias = sbq("hbias", [128, 2, 4], F32)
            fscale = sbq("fscale", [128, nf], F32)
            negt = sbq("negt", [128, nf], F32)
            with contextlib.ExitStack() as s1:
                sb1 = lambda n, s, d: self.sb(s1, n, s, d)
                featsT = sb1("featsT", [33, L], F32)
                w1f = sb1("w1f", [33, 64], F32)
                w2f = sb1("w2f", [64, 64], F32)
                bfr = sb1("bfr", [64, 3], F32)
                h1T = sb1("h1T", [64, L], F32)
                zt = sb1("zt", [64, 512], F32)
                tiq = sb1("tiq", [64, 512], I32)
                tfq = sb1("tfq", [64, 512], F32)
                self.dma('sp', featsT[:], K['k_featsT' if sname == 'lat' else 'k_featsT_c'][:, :], [], ['featsT'])
                self.dma('sp', w1f[:], W['hy_filt_w1'][l], [], ['w1f'])
                self.dma('sp', w2f[:], W['hy_filt_w2'][l], [], ['w2f'])
                self.dma('sp', bfr[:, 0:1], W['hy_filt_b1'][l].rearrange("(p o) -> p o", o=1), [], ['bfr'])
                self.dma('sp', bfr[:, 1:2], W['hy_filt_b2'][l].rearrange("(p o) -> p o", o=1), [], ['bfr'])
                self.dma('sp', bfr[:, 2:3], W['hy_filt_freq'][l].rearrange("(p o) -> p o", o=1), [], ['bfr'])
                for oo in range(2):
                    self.dma('sp', hbias[:, oo, :], W['hy_bias'][l][oo].rearrange("(c p) -> p c", p=128), [], ['hbias'], slow=True)
                self.dma('sp', fscale[:], K['k_fscale' if sname == 'lat' else 'k_fscale_c'][:, :], [], ['fscale'])
                self.dma('sp', negt[:], K['k_negt' if sname == 'lat' else 'k_negt_c'][:, :], [], ['negt'])
                bw = min(512, L)
                for (wf, kin, src, dst, bcol, srck, dstk) in ((w1f, 33, featsT, h1T, 0, 'featsT', 'h1T'), (w2f, 64, h1T, h2T, 1, 'h1T', 'h2T')):
                    for tb in range(L // bw):
                        self.mm(self.ps[0][0:64, 0:bw], wf[0:kin, 0:64], src[0:kin, tb * bw:(tb + 1) * bw], True, True, [srck, 'w1f', 'w2f'], ['ps0'])
                        self.ts('dve', zt[:, 0:bw], self.ps[0][0:64, 0:bw], bfr[:, bcol:bcol + 1], ALU.add, ['ps0', 'bfr'], ['zt'], s2=bfr[:, 2:3], op1=ALU.mult)
                        self.sin_rr(dst[:, tb * bw:(tb + 1) * bw], zt[:, 0:bw], 0.0, tiq[:, 0:bw], tfq[:, 0:bw], 'hyq', ['zt'], [dstk])
                self.cp('dve', h2Tb[:], h2T[:], ['h2T'], ['h2Tb'])
                self.P.flush()
            if self.halt_at(f'hyM_{l}'):
                return
            for o in range(2):
                vT = vbuf[o % 2]
                yT = vbuf[(o + 1) % 2]
                with contextlib.ExitStack() as s2:
                    sb2 = lambda n, s, d: self.sb(s2, n, s, d)
                    w3f = sb2("w3f", [64, 1024], BF16)
                    b3r = sb2("b3r", [1, 1024], BF16)
                    dcy = sb2("dcy", [128, 1024], F32)
                    hst = sb2("hst", [128, nf, 1024], BF16)
                    dect = [sb2("dect", [128, 512], F32) for _ in range(2)]
                    sqt = [sb2("sqh", [128, 512], BF16) for _ in range(2)]
                    sc = sb2("sc", [128, 512], F32)
                    tsum = [sb2("tsum", [128, 512], BF16) for _ in range(2)]
                    tab = [[sb2("tab", [128, nf, 128], BF16) for _ in range(2)] for _ in range(2)]
                    self.dma('pool', w3f[:], W['hy_filt_w3'][l][:, o * 1024:(o + 1) * 1024], [], ['w3f'])
                    self.dma('pool', b3r[:], W['hy_filt_b3'][l].rearrange("(o n) -> o n", o=2)[o:o + 1, :], [], ['b3r'])
                    self.dma('sp', dcy[:], W['hy_filt_decay'][l].rearrange("(o n) -> o n", o=2)[o:o + 1, :].partition_broadcast(128), [], ['dcy'])
                    n = 0
                    for i in range(nf):
                        for dr in range(2):
                            b = n % 2
                            n += 1
                            cs_ = slice(dr * 512, (dr + 1) * 512)
                            self.mm(self.ps[b][:, :], h2Tb[0:64, i * 128:(i + 1) * 128], w3f[0:64, cs_], True, False, ['h2Tb', 'w3f'], [f'ps{b}'])
                            self.mm(self.ps[b][:, :], self.ones_bf[0:1, 0:128], b3r[0:1, cs_], False, True, ['ones_bf', 'b3r'], [f'ps{b}'])
                            self.act(dect[b][:], dcy[:, cs_], AF.Exp, ['dcy', 'negt'], [f'dect{b}'], scale=negt[:, i:i + 1])
                            self.tt('dve', hst[:, i, cs_], self.ps[b][:, :], dect[b][:], ALU.mult, [f'ps{b}', f'dect{b}'], [f'hst{i}'])
                            self.act(sqt[b][:], hst[:, i, cs_], AF.Square, [f'hst{i}'], [f'sqh{b}'])
                            self.mm(self.ps[4 + dr][:, :], self.ones_bf[:], sqt[b][:], i == 0, i == nf - 1, [f'sqh{b}', 'ones_bf'], [f'ps{4 + dr}'])
                        if i == 0:
                            self.memset('pool', hst[0:1, 0, 512:1024], 0.0, ['hst0'])
                        tb2 = i % 2
                        self.tt('pool', tsum[tb2][:], hst[:, i, 0:512], hst[:, i, 512:1024], ALU.add, [f'hst{i}'], [f'tsum{tb2}'])
                        self.tt('pool', hst[:, i, 512:1024], hst[:, i, 0:512], hst[:, i, 512:1024], ALU.subtract, [f'hst{i}'], [f'hst{i}'])
                        self.cp('pool', hst[:, i, 0:512], tsum[tb2][:], [f'tsum{tb2}', f'hst{i}'], [f'hst{i}'])
                    self.cp('dve', sc[:], self.ps[5][:, :], ['ps5'], ['sc'])
                    self.tt('dve', sc[:], sc[:], self.ps[4][:, :], ALU.add, ['sc', 'ps4'], ['sc'])
                    self.act(sc[:], sc[:], AF.Sqrt, ['sc', 'eps'], ['sc'], bias=self.eps[:, 0:1])
                    self.recip(sc[:], sc[:], ['sc'], ['sc'])
                    hkeys = [f'hst{i}' for i in range(nf)]
                    for j in range(nf):
                        tb_ = j % 2
                        self._load_tab(tab[tb_], sname, j, nf, f'tab{tb_}')
                        for ab in range(2):
                            bank = 2 * (j % 2) + ab
                            for i in range(nf):
                                self.mm(self.ps[bank][:, :], tab[tb_][ab][:, i, :], hst[:, i, ab * 512:(ab + 1) * 512], i == 0, i == nf - 1,
                                        [f'tab{tb_}{ab}'] + hkeys, [f'ps{bank}'])
                            self.stt(Kh[:, j, ab, :], self.ps[bank][:, :], fscale[:, j:j + 1], sc[:], ALU.mult, ALU.mult, [f'ps{bank}', 'fscale', 'sc'], ['Kh'])
                    for i in range(nf):
                        self.mm(self.ps[6][0:1, :], self.altcol[:, 0:1], hst[:, i, 0:512], i == 0, i == nf - 1, ['altcol'] + hkeys, ['ps6'])
                    self.stt(knyq[:], self.ps[6][0:1, :], 1.0 / N, sc[0:1, :], ALU.mult, ALU.mult, ['ps6', 'sc'], ['knyq'])
                    self.P.flush()
                if self.halt_at(f'hyB_{l}'):
                    return
                with contextlib.ExitStack() as s3:
                    sb3 = lambda n, s, d: self.sb(s3, n, s, d)
                    vtm = sb3("vtmh", [128, nf, 512], BF16)
                    tab = [[sb3("tab", [128, nf, 128], BF16) for _ in range(2)] for _ in range(2)]
                    zt4 = [[sb3("zt4", [128, 512], F32) for _ in range(4)] for _ in range(2)]
                    if o == 0:
                        self.dma('sp', vT[:], self.hcd[:, 0:4, toff:toff + L], [], ['vT'])
                    for i in range(nf):
                        hb_ = 4 + (i % 2)
                        for c in range(4):
                            self.mm(self.ps[hb_][:, c * 128:(c + 1) * 128], vT[:, c, i * 128:(i + 1) * 128], self.identbf[:], True, True,
                                    ['vT', 'identbf'], [f'ps{hb_}'])
                        self.cp('act' if i % 2 == 0 else 'dve', vtm[:, i, :], self.ps[hb_][:, :], [f'ps{hb_}'], ['vtm'])
                    for j in range(nf):
                        tb_ = j % 2
                        self._load_tab(tab[tb_], sname, j, nf, f'tab{tb_}')
                        ba, bb = 2 * (j % 2), 2 * (j % 2) + 1
                        for ab, bank in ((0, ba), (1, bb)):
                            for i in range(nf):
                                self.mm(self.ps[bank][:, :], tab[tb_][ab][:, i, :], vtm[:, i, :], i == 0, i == nf - 1,
                                        [f'tab{tb_}{ab}', 'vtm'], [f'ps{bank}'])
                        z1, z2, z3, z4 = zt4[j % 2]
                        kz = j % 2
                        self.tt('dve', z1[:], self.ps[ba][:, :], Kh[:, j, 0, :], ALU.mult, [f'ps{ba}', 'Kh'], [f'z1{kz}'])
                        self.tt('dve', z2[:], self.ps[bb][:, :], Kh[:, j, 1, :], ALU.mult, [f'ps{bb}', 'Kh'], [f'z2{kz}'])
                        self.tt('pool', Zst[:, j, 0, :], z1[:], z2[:], ALU.subtract, [f'z1{kz}', f'z2{kz}'], ['Zst'])
                        self.tt('dve', z3[:], self.ps[ba][:, :], Kh[:, j, 1, :], ALU.mult, [f'ps{ba}', 'Kh'], [f'z3{kz}'])
                        self.tt('dve', z4[:], self.ps[bb][:, :], Kh[:, j, 0, :], ALU.mult, [f'ps{bb}', 'Kh'], [f'z4{kz}'])
                        self.tt('pool', Zst[:, j, 1, :], z3[:], z4[:], ALU.add, [f'z3{kz}', f'z4{kz}'], ['Zst'])
                    for i in range(nf):
                        self.mm(self.ps[6][0:1, :], self.altcol[:, 0:1], vtm[:, i, :], i == 0, i == nf - 1, ['altcol', 'vtm'], ['ps6'])
                    self.tt('dve', znyq[:], self.ps[6][0:1, :], knyq[:], ALU.mult, ['ps6', 'knyq'], ['znyq'])
                    self.dump(f'd_vtm_{sname}{o}', vtm[:], [128, nf, 512], BF16, ['vtm'])
                    self.dump(f'd_identbf_{sname}{o}', self.identbf[:], [128, 128], BF16, ['identbf'])
                    self.P.flush()
                if self.halt_at(f'hyC_{l}'):
                    return
                with contextlib.ExitStack() as s4:
                    sb4 = lambda n, s, d: self.sb(s4, n, s, d)
                    TB = 256
                    xg = sb4("xg", [128, 4, L], BF16)
                    rtab = [[sb4("rtab", [128, nf, TB], BF16) for _ in range(2)] for _ in range(2)]
                    gt_ = [sb4("gtmp", [128, TB], F32) for _ in range(2)]
                    self.dma('sp', xg[:], self.hcd[:, 4 + 4 * o:8 + 4 * o, toff:toff + L], [], ['xg'])
                    n = 0
                    for tb in range(L // TB):
                        rb = tb % 2
                        for ab, nm in ((0, 'A'), (1, 'B')):
                            if sname == 'lat':
                                src = K['k_' + nm + 'r'].rearrange("(j p) t -> p j t", p=128)[:, :, tb * TB:(tb + 1) * TB]
                            else:
                                src = K['k_' + nm + 'c'].rearrange("(j p) t -> p j t", p=128)[:, :, tb * TB:(tb + 1) * TB]
                            self.dma('sp', rtab[rb][ab][:], src, [], [f'rtab{rb}{ab}'])
                        for c in range(4):
                            bank = n % 4
                            gb_ = n % 2
                            n += 1
                            for j in range(nf):
                                for ab in range(2):
                                    self.mm(self.ps[bank][:, 0:TB], Zst[:, j, ab, c * 128:(c + 1) * 128], rtab[rb][ab][:, j, :], j == 0 and ab == 0, False,
                                            ['Zst', f'rtab{rb}{ab}'], [f'ps{bank}'])
                            self.mm(self.ps[bank][:, 0:TB], znyq[0:1, c * 128:(c + 1) * 128], self.altrow[0:1, tb * TB:(tb + 1) * TB], False, True,
                                    ['znyq', 'altrow'], [f'ps{bank}'])
                            tsl = slice(tb * TB, (tb + 1) * TB)
                            self.stt(gt_[gb_][:], vT[:, c, tsl], hbias[:, o, c:c + 1], self.ps[bank][:, 0:TB], ALU.mult, ALU.add,
                                     ['vT', 'hbias', f'ps{bank}'], [f'gt{gb_}'])
                            self.tt('pool', yT[:, c, tsl], gt_[gb_][:], xg[:, c, tsl], ALU.mult, [f'gt{gb_}', 'xg'], ['yT' if o == 0 else 'yT2'])
                    self.dump(f'd_hbias_{sname}{o}', hbias[:], [128, 2, 4], F32, ['hbias'])
                    self.dump(f'd_knyq_{sname}{o}', knyq[:], [1, 512], F32, ['knyq'])
                    self.dump(f'd_vT_{sname}{o}', vT[:], [128, 4, L], BF16, ['vT'])
                    self.dump(f'd_xg_{sname}{o}', xg[:], [128, 4, L], BF16, ['xg'])
                    self.dump(f'd_yT_{sname}{o}', yT[:], [128, 4, L], BF16, ['yT', 'yT2'])
                    self.dump(f'd_Kh_{sname}{o}', Kh[:], [128, nf, 2, 512], BF16, ['Kh'])
                    self.dump(f'd_Zst_{sname}{o}', Zst[:], [128, nf, 2, 512], BF16, ['Zst'])
                    if o == 1:
                        self.dma('sp', self.ybr_c[:, :, toff:toff + L], yT[:], ['yT2'], ['ybr_c'])
                    self.P.flush()


def _load_tab(self, tabs, sname, j, nf, key):
    K = self.K
    for ab, nm in ((0, 'A'), (1, 'B')):
        if sname == 'lat':
            src = K['k_' + nm + 't'][j]
        else:
            src = K['k_' + nm + 'c'].rearrange("(i p) (j q) -> j p i q", p=128, q=128)[j]
        self.dma('sp', tabs[ab][:], src, [], [key + str(ab)])


KB.phase_hy = phase_hy
KB._load_tab = _load_tab

def phase_merge(self, l):
    W = self.W
    need_ctx = (l == 0)
    tiles = TT if need_ctx else TT[:4]
    hi = NT if need_ctx else NL
    with contextlib.ExitStack() as st:
        sb = lambda n, s, d: self.sb(st, n, s, d)
        xnT = sb("xnT", [128, 8, NT], BF16)
        ya = sb("ya", [128, 4, NT], BF16)
        yc = sb("yc", [128, 4, NT], BF16)
        yb = sb("yb", [64, 8, NT], BF16)
        mg = sb("mg", [128, 8, NT], BF16)
        wg = [sb("wg", [128, 8, 3, 128], BF16) for _ in range(2)]
        wba = [sb("wba", [128, 4, 128], BF16) for _ in range(2)]
        wbc = [sb("wbc", [128, 4, 128], BF16) for _ in range(2)]
        wbb = [sb("wbb", [64, 8, 128], BF16) for _ in range(2)]
        sgt = [sb("sgm", [128, 512], F32) for _ in range(2)]
        acc = [sb("acc", [128, 512], F32) for _ in range(2)]
        tmpm = [sb("tmpm", [128, 512], F32) for _ in range(2)]
        for it_, (t0_, sz_, _c) in enumerate(tiles):
            tsl_ = slice(t0_, t0_ + sz_)
            self.dma('sp', xnT[:, :, tsl_], self.xnd[:, :, tsl_], [], [f'xnT_{it_}'])
            self.dma('sp', ya[:, :, tsl_], self.ybr_a[:, :, tsl_], [], [f'ya{it_}'])
            self.dma('sp', yb[:, :, tsl_], self.ybr_b[:, :, tsl_], [], [f'yb{it_}'])
            self.dma('sp', yc[:, :, tsl_], self.ybr_c[:, :, tsl_], [], [f'yc{it_}'])
        win = W['w_in'][l].rearrange("(k p) n -> p k n", p=128)
        wbr = W['w_branch'][l]
        n_ = 0
        m_ = 0
        def load_w(dt):
            b = dt % 2
            for n in range(3):
                c0 = 2816 + n * 1024 + dt * 128
                self.dma('pool', wg[b][:, :, n, :], win[:, :, c0:c0 + 128], [], [f'wg{b}{n}'])
            self.dma('pool', wba[b][:], wbr[0].rearrange("(c p) n -> p c n", p=128)[:, :, dt * 128:(dt + 1) * 128], [], [f'wba{b}'])
            self.dma('pool', wbc[b][:], wbr[2].rearrange("(c p) n -> p c n", p=128)[:, :, dt * 128:(dt + 1) * 128], [], [f'wbc{b}'])
            self.dma('pool', wbb[b][:], wbr[1].rearrange("(h j) n -> j h n", j=64)[:, :, dt * 128:(dt + 1) * 128], [], [f'wbb{b}'])

        load_w(0)
        for dt in range(8):
            b = dt % 2
            if dt + 1 < 8:
                load_w(dt + 1)
            for it, (t0, sz, col) in enumerate(tiles):
                ab = m_ % 2
                m_ += 1
                for n in range(3):
                    gbank = n_ % 2
                    pbank = 2 + (n_ % 2)
                    sb_ = n_ % 2
                    n_ += 1
                    for k in range(8):
                        self.mm(self.ps[gbank][:, 0:sz], wg[b][:, k, n, :], xnT[:, k, t0:t0 + sz], k == 0, k == 7, [f'wg{b}{n}', f'xnT_{it}'], [f'ps{gbank}'])
                    self.act(sgt[sb_][:, 0:sz], self.ps[gbank][:, 0:sz], AF.Sigmoid, [f'ps{gbank}'], [f'sgm{sb_}'])
                    if n == 0:
                        for c in range(4):
                            self.mm(self.ps[pbank][:, 0:sz], wba[b][:, c, :], ya[:, c, t0:t0 + sz], c == 0, c == 3, [f'wba{b}', f'ya{it}'], [f'ps{pbank}'])
                        self.tt('dve', acc[ab][:, 0:sz], self.ps[pbank][:, 0:sz], sgt[sb_][:, 0:sz], ALU.mult, [f'ps{pbank}', f'sgm{sb_}'], [f'acc{ab}'])
                    elif n == 1:
                        for h in range(8):
                            self.mm(self.ps[pbank][:, 0:sz], wbb[b][:, h, :], yb[:, h, t0:t0 + sz], h == 0, h == 7, [f'wbb{b}', f'yb{it}'], [f'ps{pbank}'])
                        self.tt('dve', tmpm[0][:, 0:sz], self.ps[pbank][:, 0:sz], sgt[sb_][:, 0:sz], ALU.mult, [f'ps{pbank}', f'sgm{sb_}'], ['tmpm0'])
                        self.tt('pool', acc[ab][:, 0:sz], acc[ab][:, 0:sz], tmpm[0][:, 0:sz], ALU.add, [f'acc{ab}', 'tmpm0'], [f'acc{ab}'])
                    else:
                        for c in range(4):
                            self.mm(self.ps[pbank][:, 0:sz], wbc[b][:, c, :], yc[:, c, t0:t0 + sz], c == 0, c == 3, [f'wbc{b}', f'yc{it}'], [f'ps{pbank}'])
                        self.tt('dve', tmpm[1][:, 0:sz], self.ps[pbank][:, 0:sz], sgt[sb_][:, 0:sz], ALU.mult, [f'ps{pbank}', f'sgm{sb_}'], ['tmpm1'])
                        self.tt('pool', mg[:, dt, t0:t0 + sz], acc[ab][:, 0:sz], tmpm[1][:, 0:sz], ALU.add, [f'acc{ab}', 'tmpm1'], [f'mg{dt}'])
        mgkeys = [f'mg{i}' for i in range(8)]
        wo = [sb("wo", [128, 8, 128], BF16) for _ in range(2)]
        NXO = 4
        xo = [sb("xo", [128, 512], F32) for _ in range(NXO)]
        wout = W['w_out'][l].rearrange("(k p) n -> p k n", p=128)
        n_ = 0
        ulist = [(dt, it) for dt in range(8) for it in range(len(tiles))]

        def xload(ui):
            dt_, it_ = ulist[ui]
            t0_, sz_, _c = tiles[it_]
            self.dma('sp', xo[ui % NXO][:, 0:sz_], self.xres[:, dt_, t0_:t0_ + sz_], [], [f'xo{ui % NXO}'])
        for ui in range(NXO - 1):
            xload(ui)
        for dt in range(8):
            b = dt % 2
            self.dma('pool', wo[b][:], wout[:, :, dt * 128:(dt + 1) * 128], [], [f'wo{b}'])
            for it, (t0, sz, col) in enumerate(tiles):
                bank = 4 + (n_ % 2)
                xb = n_ % NXO
                for k in range(8):
                    self.mm(self.ps[bank][:, 0:sz], wo[b][:, k, :], mg[:, k, t0:t0 + sz], k == 0, k == 7, [f'wo{b}'] + mgkeys, [f'ps{bank}'])
                self.stt(xo[xb][:, 0:sz], self.ps[bank][:, 0:sz], self.modT[:, 16 + dt, col:col + 1], xo[xb][:, 0:sz], ALU.mult, ALU.add,
                         [f'ps{bank}', f'xo{xb}', 'modT'], [f'xo{xb}'])
                if n_ + NXO - 1 < len(ulist):
                    xload(n_ + NXO - 1)
                self.dma('sp', self.xres[:, dt, t0:t0 + sz], xo[xb][:, 0:sz], [f'xo{xb}'], [f'xres{dt}_{it}'])
                n_ += 1
        self.P.flush()


KB.phase_merge = phase_merge
def build_nc(debug=(), stop_after=None):
    nc = bass.Bass("TRN2", target_bir_lowering=False)
    kb = KB(nc, debug=debug, stop_after=stop_after)
    kb.declare()

    def stop(tag):
        return stop_after == tag

    with kb.gst:
        kb.consts()
        kb.phase_input()
        for l in range(2):
            with contextlib.ExitStack() as lst:
                kb.phase_mod(l, lst)
                kb.phase_norm(l, 1, TT)
                if stop(f'norm1_{l}'):
                    break
                kb.phase_s5(l)
                if stop(f's5_{l}'):
                    break
                if hasattr(kb, 'phase_att'):
                    kb.phase_att(l)
                if stop(f'att_{l}'):
                    break
                if hasattr(kb, 'phase_hy'):
                    kb.phase_hy(l)
                if getattr(kb, 'halted', False):
                    break
                if stop(f'hy_{l}'):
                    break
                if hasattr(kb, 'phase_merge'):
                    kb.phase_merge(l)
                if stop(f'merge_{l}'):
                    break
                if hasattr(kb, 'phase_ffn'):
                    kb.phase_norm(l, 2, TT if l == 0 else TT[:4], router=(l == 1))
                    kb.phase_ffn(l)
                if stop(f'ffn_{l}'):
                    break
        else:
            if hasattr(kb, 'phase_output'):
                kb.phase_output()
        kb.P.finish()
    return nc


def make_in_maps(inputs, cores):
    hc = host_constants()
    maps = []
    for b in cores:
        m = {}
        m['x'] = np.ascontiguousarray(inputs['x'][b])
        m['c'] = np.ascontiguousarray(inputs['c'][b])
        m['ctx'] = np.ascontiguousarray(inputs['ctx'][b])
        m['c_ctx'] = np.ascontiguousarray(inputs['c_ctx'])
        for k, v in inputs.items():
            if k not in ('x', 'c', 'ctx', 'c_ctx'):
                m[k] = np.ascontiguousarray(v)
        m.update(hc)
        maps.append(m)
    return maps


_NC_CACHE = {}


def kernel(**inputs):
    inputs = {k: np.asarray(v) for k, v in inputs.items()}
    if 'nc' not in _NC_CACHE:
        _NC_CACHE['nc'] = build_nc()
    nc = _NC_CACHE['nc']
    cores = list(range(8))
    maps = make_in_maps(inputs, cores)
    res = run_bass_kernel_spmd(nc, maps, core_ids=cores)
    out = np.stack([np.asarray(res.results[i]['out'], dtype=np.float32) for i in range(8)], axis=0)
    return out
```
